# Optimizing a Trainium2 kernel written in Bass

```python
import math
import jax, jax.numpy as jnp
from jax import lax
import numpy as np

D_MODEL = 1024
BATCH = 8
SEQ = 4096
DEPTH = 1
DEC_BATCH = 8
DEC_SEQ = 8192
PAST_LEN = 128

RET_HEADS = 4
RET_DK = 128
RET_DV = 256
RET_CHUNK = 128
MLA_HEADS = 8
MLA_NOPE = 128
MLA_ROPE = 64
MLA_V = 128
MLA_QK = MLA_NOPE + MLA_ROPE
Q_LORA = 384
KV_LORA = 256
Q_BLOCK = 128
D_FF = 2816
CONV_W = 3
ROPE_BASE = 10000.0
EPS = 1e-6

RET_Q_W = RET_HEADS * RET_DK
RET_V_W = RET_HEADS * RET_DV
MLA_O_W = MLA_HEADS * MLA_V
IN_SPLITS = (RET_Q_W, RET_Q_W, RET_V_W, RET_V_W, Q_LORA, KV_LORA, MLA_ROPE, D_MODEL, D_MODEL)
IN_W = sum(IN_SPLITS)
IN_SPLIT_POINTS = tuple(int(v) for v in np.cumsum(IN_SPLITS)[:-1])

kernel_name = "hybrid_retention_mla_encoder"


def rms_norm(x, g):
    xf = x.astype(jnp.float32)
    y = xf * lax.rsqrt(jnp.mean(xf * xf, axis=-1, keepdims=True) + EPS)
    return (y * g.astype(jnp.float32)).astype(x.dtype)


def head_group_norm(x, g):
    b, s, h, dv = x.shape
    xf = x.astype(jnp.float32)
    mu = jnp.mean(xf, axis=-1, keepdims=True)
    xc = xf - mu
    y = xc * lax.rsqrt(jnp.mean(xc * xc, axis=-1, keepdims=True) + EPS)
    return (y.reshape(b, s, h * dv) * g.astype(jnp.float32)).astype(x.dtype)


def rope(x, pos):
    d = x.shape[-1]
    inv = ROPE_BASE ** (-jnp.arange(0, d, 2, dtype=jnp.float32) / d)
    ang = pos.astype(jnp.float32)[:, None] * inv[None, :]
    cos = jnp.cos(ang)[None, :, None, :].astype(x.dtype)
    sin = jnp.sin(ang)[None, :, None, :].astype(x.dtype)
    x1, x2 = x[..., : d // 2], x[..., d // 2:]
    return jnp.concatenate([x1 * cos - x2 * sin, x1 * sin + x2 * cos], axis=-1)


def retention_one_direction(q, k, v, log_gamma, strict):
    b, s, h, dk = q.shape
    dv = v.shape[-1]
    c = RET_CHUNK
    n = s // c
    idx = jnp.arange(c, dtype=jnp.float32)
    diff = idx[:, None] - idx[None, :]
    mask = (diff > 0) if strict else (diff >= 0)
    dmat = jnp.where(mask[None], jnp.exp(log_gamma[:, None, None] * jnp.maximum(diff, 0.0)[None]), 0.0)
    dmat = dmat.astype(q.dtype)
    q_dec = jnp.exp(log_gamma[None, :] * (idx[:, None] + 1.0)).astype(q.dtype)[None, :, :, None]
    k_dec = jnp.exp(log_gamma[None, :] * (c - 1.0 - idx[:, None])).astype(q.dtype)[None, :, :, None]
    chunk_dec = jnp.exp(log_gamma * c)[None, :, None, None]

    def to_chunks(t):
        return t.reshape(b, n, c, h, t.shape[-1]).transpose(1, 0, 2, 3, 4)

    def step(state, inp):
        qn, kn, vn = inp
        scores = jnp.einsum('bihd,bjhd->bhij', qn, kn) * dmat[None]
        intra = jnp.einsum('bhij,bjhv->bihv', scores, vn)
        cross = jnp.einsum('bihd,bhdv->bihv', qn * q_dec, state.astype(qn.dtype))
        new_state = state * chunk_dec + jnp.einsum('bjhd,bjhv->bhdv', kn * k_dec, vn).astype(jnp.float32)
        return new_state, intra + cross

    state0 = jnp.zeros((b, h, dk, dv), jnp.float32)
    _, out = lax.scan(step, state0, (to_chunks(q), to_chunks(k), to_chunks(v)))
    return out.transpose(1, 0, 2, 3, 4).reshape(b, s, h, dv)


def bidirectional_retention(q, k, v, decay_fwd, decay_bwd):
    lg_f = jax.nn.log_sigmoid(decay_fwd.astype(jnp.float32))
    lg_b = jax.nn.log_sigmoid(decay_bwd.astype(jnp.float32))
    fwd = retention_one_direction(q, k, v, lg_f, False)
    bwd = retention_one_direction(q[:, ::-1], k[:, ::-1], v[:, ::-1], lg_b, True)[:, ::-1]
    return fwd + bwd


def mla_attention(c_q, c_kv, k_rope, pos, g_cq, w_uq, g_ckv, w_ukv, g_qn, g_kn):
    b, s, _ = c_q.shape
    q = (rms_norm(c_q, g_cq) @ w_uq).reshape(b, s, MLA_HEADS, MLA_QK)
    kv = (rms_norm(c_kv, g_ckv) @ w_ukv).reshape(b, s, MLA_HEADS, MLA_NOPE + MLA_V)
    k_nope, v = kv[..., :MLA_NOPE], kv[..., MLA_NOPE:]
    k = jnp.concatenate([k_nope, jnp.broadcast_to(k_rope[:, :, None, :], (b, s, MLA_HEADS, MLA_ROPE))], axis=-1)
    q = rms_norm(q, g_qn)
    k = rms_norm(k, g_kn)
    q = jnp.concatenate([q[..., :MLA_NOPE], rope(q[..., MLA_NOPE:], pos)], axis=-1)
    k = jnp.concatenate([k[..., :MLA_NOPE], rope(k[..., MLA_NOPE:], pos)], axis=-1)
    nb = s // Q_BLOCK
    qb = (q * (MLA_QK ** -0.5)).reshape(b, nb, Q_BLOCK, MLA_HEADS, MLA_QK).transpose(1, 0, 2, 3, 4)

    def attend(q_blk):
        sc = jnp.einsum('bqhd,bkhd->bhqk', q_blk, k).astype(jnp.float32)
        p = jax.nn.softmax(sc, axis=-1).astype(v.dtype)
        return jnp.einsum('bhqk,bkhv->bqhv', p, v)

    o = lax.map(attend, qb)
    return o.transpose(1, 0, 2, 3, 4).reshape(b, s, MLA_O_W)


def depthwise_conv3(u, w, bias):
    up = jnp.pad(u, ((0, 0), (1, 1), (0, 0)))
    return up[:, :-2] * w[0] + up[:, 1:-1] * w[1] + up[:, 2:] * w[2] + bias


def encoder_trunk(x, g_mix, w_in, ret_decay_fwd, ret_decay_bwd, ret_gn_g, w_ret_o,
                  g_cq, w_uq, g_ckv, w_ukv, g_qn, g_kn, w_mla_o, w_out,
                  g_ffn, w_up, conv_w, conv_b, w_down):
    b, s, _ = x.shape
    pos = jnp.arange(s)
    for l in range(DEPTH):
        h = rms_norm(x, g_mix[l])
        proj = h @ w_in[l]
        rq, rk, rv, rg, cq, ckv, kr, gate_r, gate_a = jnp.split(proj, IN_SPLIT_POINTS, axis=-1)
        rq = rope(rq.reshape(b, s, RET_HEADS, RET_DK), pos)
        rk = rope(rk.reshape(b, s, RET_HEADS, RET_DK), pos) * (RET_DK ** -0.5)
        rv = rv.reshape(b, s, RET_HEADS, RET_DV)
        ret = bidirectional_retention(rq, rk, rv, ret_decay_fwd[l], ret_decay_bwd[l])
        ret_branch = (jax.nn.silu(rg) * head_group_norm(ret, ret_gn_g[l])) @ w_ret_o[l]
        kr = rope(kr[:, :, None, :], pos)[:, :, 0, :]
        mla = mla_attention(cq, ckv, kr, pos, g_cq[l], w_uq[l], g_ckv[l], w_ukv[l], g_qn[l], g_kn[l])
        mla_branch = mla @ w_mla_o[l]
        merged = jax.nn.sigmoid(gate_r) * ret_branch + jax.nn.sigmoid(gate_a) * mla_branch
        x = x + merged @ w_out[l]
        h = rms_norm(x, g_ffn[l])
        u = depthwise_conv3(h @ w_up[l], conv_w[l], conv_b[l])
        x = x + (jax.nn.silu(u[..., :D_FF]) * u[..., D_FF:]) @ w_down[l]
    return x


def setup_inputs(seed: int = 0) -> dict:
    key = jax.random.key(seed)
    ks = jax.random.split(key, 24)
    f32 = jnp.float32

    def nrm(k, shape, scale):
        return jax.random.normal(k, shape, f32) * scale

    def gain(k, shape):
        return 1.0 + 0.02 * jax.random.normal(k, shape, f32)

    decay_base = jnp.log(2.0 ** (5.0 + jnp.arange(RET_HEADS, dtype=f32)) - 1.0)
    return {
        "x_prompt": jax.random.normal(ks[0], (BATCH, SEQ, D_MODEL), f32),
        "x_sample": jax.random.normal(ks[1], (DEC_BATCH, DEC_SEQ, D_MODEL), f32),
        "g_mix": gain(ks[2], (DEPTH, D_MODEL)),
        "w_in": nrm(ks[3], (DEPTH, D_MODEL, IN_W), D_MODEL ** -0.5),
        "ret_decay_fwd": decay_base[None] + 0.1 * jax.random.normal(ks[4], (DEPTH, RET_HEADS), f32),
        "ret_decay_bwd": decay_base[None] + 0.1 * jax.random.normal(ks[5], (DEPTH, RET_HEADS), f32),
        "ret_gn_g": gain(ks[6], (DEPTH, RET_V_W)),
        "w_ret_o": nrm(ks[7], (DEPTH, RET_V_W, D_MODEL), RET_V_W ** -0.5),
        "g_cq": gain(ks[8], (DEPTH, Q_LORA)),
        "w_uq": nrm(ks[9], (DEPTH, Q_LORA, MLA_HEADS * MLA_QK), Q_LORA ** -0.5),
        "g_ckv": gain(ks[10], (DEPTH, KV_LORA)),
        "w_ukv": nrm(ks[11], (DEPTH, KV_LORA, MLA_HEADS * (MLA_NOPE + MLA_V)), KV_LORA ** -0.5),
        "g_qn": gain(ks[12], (DEPTH, MLA_QK)),
        "g_kn": gain(ks[13], (DEPTH, MLA_QK)),
        "w_mla_o": nrm(ks[14], (DEPTH, MLA_O_W, D_MODEL), MLA_O_W ** -0.5),
        "w_out": nrm(ks[15], (DEPTH, D_MODEL, D_MODEL), D_MODEL ** -0.5),
        "g_ffn": gain(ks[16], (DEPTH, D_MODEL)),
        "w_up": nrm(ks[17], (DEPTH, D_MODEL, 2 * D_FF), D_MODEL ** -0.5),
        "conv_w": nrm(ks[18], (DEPTH, CONV_W, 2 * D_FF), CONV_W ** -0.5),
        "conv_b": nrm(ks[19], (DEPTH, 2 * D_FF), 0.02),
        "w_down": nrm(ks[20], (DEPTH, D_FF, D_MODEL), D_FF ** -0.5),
    }


def reference(x_prompt, x_sample, g_mix, w_in, ret_decay_fwd, ret_decay_bwd, ret_gn_g, w_ret_o,
              g_cq, w_uq, g_ckv, w_ukv, g_qn, g_kn, w_mla_o, w_out,
              g_ffn, w_up, conv_w, conv_b, w_down):
    weights = (g_mix, w_in, ret_decay_fwd, ret_decay_bwd, ret_gn_g, w_ret_o,
               g_cq, w_uq, g_ckv, w_ukv, g_qn, g_kn, w_mla_o, w_out,
               g_ffn, w_up, conv_w, conv_b, w_down)
    y_prompt = encoder_trunk(x_prompt, *weights)
    y_sample = encoder_trunk(x_sample, *weights)
    return (y_prompt, y_sample)
```

```python
import contextlib
import math
import numpy as np
import ml_dtypes
import concourse.bass as bass
import concourse.mybir as mybir
from concourse.bass_utils import run_bass_kernel_spmd

F32 = mybir.dt.float32
BF16 = mybir.dt.bfloat16
AF = mybir.ActivationFunctionType
ALU = mybir.AluOpType

D = 1024
IN_W = 5824
EPS = 1e-6
SAME_ENGINE_SYNC = True
ENGS = ("sync", "scalar", "vector", "gpsimd", "tensor")

C_A, C_B, C_MF, C_MB, C_C1, C_C2, C_ID, C_ONE, C_SM = 0, 128, 256, 384, 512, 640, 768, 896, 1024
NCF = 1032


class Op:
    __slots__ = ("eng", "fn", "chan", "inc", "waits", "signal", "val", "idx")

    def __init__(self, eng, fn, chan, inc):
        self.eng = eng; self.fn = fn; self.chan = chan; self.inc = inc
        self.waits = {}; self.signal = False; self.val = None; self.idx = None


class Sched:
    def __init__(self, nc, tag=""):
        self.nc = nc
        self.tag = tag
        self.ops = {e: [] for e in ENGS}
        self.chan_ops = {}
        self.last_w = {}
        self.readers = {}
        self.n = 0

    def op(self, eng, fn, reads=(), writes=(), chan=None):
        is_dma = chan is not None
        if chan is None:
            chan = "E_" + eng
        o = Op(eng, fn, chan, 16 if is_dma else 1)
        o.idx = self.n; self.n += 1
        deps = []
        for b in reads:
            w = self.last_w.get(b)
            if w is not None:
                deps.append(w)
        for b in writes:
            w = self.last_w.get(b)
            if w is not None:
                deps.append(w)
            deps.extend(self.readers.get(b, ()))
        for d in deps:
            if d is o:
                continue
            if d.eng == eng and d.chan == chan:
                if eng == "tensor" or not SAME_ENGINE_SYNC:
                    continue
            cur = o.waits.get(d.chan)
            if cur is None or cur.idx < d.idx:
                o.waits[d.chan] = d
        for b in writes:
            self.last_w[b] = o
            self.readers[b] = []
        for b in reads:
            self.readers.setdefault(b, []).append(o)
        self.ops[eng].append(o)
        self.chan_ops.setdefault(chan, []).append(o)
        return o

    def emit(self, es):
        nc = self.nc
        for e in ENGS:
            for o in self.ops[e]:
                for d in o.waits.values():
                    d.signal = True
        for c, lst in self.chan_ops.items():
            lst[-1].signal = True
            if lst[0].inc == 16:
                for o in lst:
                    o.signal = True
        sems = {}
        finals = {}
        for c, lst in self.chan_ops.items():
            sems[c] = nc.alloc_semaphore(name="s_" + self.tag + "_" + c)
            v = 0
            for o in lst:
                if o.signal:
                    v += o.inc
                    o.val = v
            finals[c] = v
        block = es.enter_context(nc.Block())

        def run(engname):
            def body(eng):
                waited = {}
                for o in self.ops[engname]:
                    for c, d in o.waits.items():
                        if waited.get(c, 0) >= d.val:
                            continue
                        eng.wait_ge(sems[c], d.val)
                        waited[c] = d.val
                    ins = o.fn(eng)
                    if o.signal:
                        ins.then_inc(sems[o.chan], o.inc)
                for c, v in finals.items():
                    if v > 0 and waited.get(c, 0) < v:
                        eng.wait_ge(sems[c], v)
            return body
        block.sync(run("sync"))
        block.scalar(run("scalar"))
        block.vector(run("vector"))
        block.gpsimd(run("gpsimd"))
        block.tensor(run("tensor"))


class Ring:
    def __init__(self, name, tiles):
        self.name = name; self.tiles = tiles; self.i = -1

    def next(self):
        self.i = (self.i + 1) % len(self.tiles)
        return self.tiles[self.i], "%s%d" % (self.name, self.i)


class Phase:
    def __init__(self, nc, name):
        self.nc = nc; self.name = name
        self.es = contextlib.ExitStack()
        self.es.enter_context(nc.cleanup_on_exit())
        self.es.callback(nc.all_engine_barrier)
        self.S = Sched(nc, name)
        self.wtog = 0

    def sb(self, name, shape, dt):
        return self.es.enter_context(self.nc.sbuf_tensor(self.name + name, shape, dt))

    def ring(self, name, n, shape, dt):
        return Ring(name, [self.sb("%s_%d" % (name, i), shape, dt) for i in range(n)])

    def pring(self, name, n, shape, dt):
        return Ring(name, [self.es.enter_context(self.nc.psum_tensor("%s%s_%d" % (self.name, name, i), shape, dt))
                           for i in range(n)])

    def dma(self, out, in_, reads, writes, chan, q="sync", slow=False):
        if slow:
            f = lambda e: e.dma_start(out=out, in_=in_, allow_slow_non_contiguous=True)
        else:
            f = lambda e: e.dma_start(out=out, in_=in_)
        return self.S.op(q, f, reads, writes, chan=chan)

    def load(self, out, in_, key, slow=False):
        return self.dma(out, in_, (), [key], chan="L" + key, q="sync", slow=slow)

    def store(self, out, in_, key, dkey=None):
        return self.dma(out, in_, [key], [dkey] if dkey else (), chan="T" + key, q="gpsimd")

    def act(self, out, in_, func, reads, writes, scale=None, bias=None, accum=None):
        kw = {}
        if scale is not None:
            kw["scale"] = scale
        if bias is not None:
            kw["bias"] = bias
        if accum is not None:
            kw["accum_out"] = accum
        return self.S.op("scalar", lambda e: e.activation(out=out, in_=in_, func=func, **kw), reads, writes)

    def ts(self, out, in0, s1, s2, op0, op1, reads, writes, eng="vector"):
        return self.S.op(eng, lambda e: e.tensor_scalar(out=out, in0=in0, scalar1=s1, scalar2=s2, op0=op0, op1=op1),
                         reads, writes)

    def tt(self, out, in0, in1, op, reads, writes, eng="vector"):
        return self.S.op(eng, lambda e: e.tensor_tensor(out=out, in0=in0, in1=in1, op=op), reads, writes)

    def stt(self, out, in0, scalar, in1, op0, op1, reads, writes):
        return self.S.op("vector", lambda e: e.scalar_tensor_tensor(out=out, in0=in0, scalar=scalar, in1=in1,
                                                                    op0=op0, op1=op1), reads, writes)

    def cp(self, out, in_, reads, writes, eng="vector"):
        return self.S.op(eng, lambda e: e.tensor_copy(out=out, in_=in_), reads, writes)

    def recip(self, out, in_, reads, writes):
        return self.S.op("vector", lambda e: e.reciprocal(out=out, in_=in_), reads, writes)

    def memset(self, ap, val, writes, eng="vector"):
        return self.S.op(eng, lambda e: e.memset(ap, val), (), writes)

    def mm(self, out, pairs, reads, writes):
        n = len(pairs)
        for i, (l, r) in enumerate(pairs):
            self.S.op("tensor", lambda e, l=l, r=r, i=i: e.matmul(out, l, r, start=(i == 0), stop=(i == n - 1)),
                      reads, writes)

    def tr(self, out, in_, ident, reads, writes):
        return self.S.op("tensor", lambda e: e.transpose(out, in_, ident), reads, writes)

    def finish(self):
        self.S.emit(self.es)
        self.es.close()

    def wcast(self, out, in_, scol, const, reads, writes):
        if scol is None:
            return self.ts(out, in_, float(const), None, ALU.mult, ALU.bypass, reads, writes)
        return self.ts(out, in_, scol, float(const), ALU.mult, ALU.mult, reads, writes)


def r3(ap, pat, **kw):
    return ap.rearrange(pat, **kw)


def ffn_groups(S):
    ng = -(-S // 510)
    base, rem = divmod(S, ng)
    out = []
    t = 0
    for i in range(ng):
        n = base + (1 if i < rem else 0)
        out.append((t, n))
        t += n
    return out


def build(SEQS):
    nc = bass.Bass("TRN2", target_bir_lowering=False)
    NS = len(SEQS)
    SMAX = max(SEQS)

    def din(name, shape, dt=F32):
        return nc.dram_tensor(name, list(shape), dt, kind="ExternalInput").ap()

    import os
    DBG = os.environ.get("KDBG", "") != ""

    def dscr(name, shape, dt=BF16):
        return nc.dram_tensor(name, list(shape), dt, kind="ExternalOutput" if DBG else "Internal").ap()

    X = [din("x%d" % i, [S, D]) for i, S in enumerate(SEQS)]
    Y = [nc.dram_tensor("y%d" % i, [S, D], F32, kind="ExternalOutput").ap() for i, S in enumerate(SEQS)]
    g_mix = din("g_mix", [D]); w_in = din("w_in", [D, IN_W])
    dec_f = din("ret_decay_fwd", [4]); dec_b = din("ret_decay_bwd", [4])
    ret_gn_g = din("ret_gn_g", [1024]); w_ret_o = din("w_ret_o", [1024, D])
    g_cq = din("g_cq", [384]); w_uq = din("w_uq", [384, 1536])
    g_ckv = din("g_ckv", [256]); w_ukv = din("w_ukv", [256, 2048])
    g_qn = din("g_qn", [192]); g_kn = din("g_kn", [192])
    w_mla_o = din("w_mla_o", [1024, D]); w_out = din("w_out", [D, D])
    g_ffn = din("g_ffn", [D]); w_up = din("w_up", [D, 5632])
    conv_w = din("conv_w", [3, 5632]); conv_b = din("conv_b", [5632])
    w_down = din("w_down", [2816, D])
    cf_d = din("cf", [128, NCF]); cb_d = din("cb", [128, 256], BF16)
    cosr_d = din("cosr", [128, SMAX]); sinr_d = din("sinr", [128, SMAX])
    cosm_d = din("cosm", [64, SMAX]); sinm_d = din("sinm", [64, SMAX])

    SC = []
    for i, S in enumerate(SEQS):
        p = "s%d_" % i
        SC.append(dict(
            hT=dscr(p + "hT", [D, S]), qrT=dscr(p + "qrT", [512, S]), krT=dscr(p + "krT", [512, S]),
            ktok=dscr(p + "ktok", [S, 512]), vtok=dscr(p + "vtok", [S, 1024]), rgsT=dscr(p + "rgsT", [1024, S]),
            grT=dscr(p + "grT", [1024, S]), gaT=dscr(p + "gaT", [1024, S]),
            qmnT=dscr(p + "qmnT", [8, 128, S]), qmrT=dscr(p + "qmrT", [8, 64, S]),
            kmnT=dscr(p + "kmnT", [8, 128, S]), kmrT=dscr(p + "kmrT", [64, S]),
            vmtok=dscr(p + "vmtok", [S, 1024]), rstdk=dscr(p + "rstdk", [S, 8], F32),
            sb=dscr(p + "sb", [S // 128, 128, 1024]), m1T=dscr(p + "m1T", [1024, S]),
            attnT=dscr(p + "attnT", [1024, S]), h2T=dscr(p + "h2T", [D, S]),
        ))

    def consts(P, need_cf=True):
        cb = P.sb("cb", [128, 256], BF16)
        P.load(cb[:], cb_d[:, :], "cb")
        cf = None
        if need_cf:
            cf = P.sb("cf", [128, NCF], F32)
            P.load(cf[:], cf_d[:, :], "cf")
        return cf, cb

    def make_cols(P, cf, rows, psk, name="cols"):
        R = sum(max(s.shape[0] for _, s in segs) for segs in rows)
        vst = P.sb(name + "_st", [R, 128], F32)
        P.memset(vst[:], 0.0, [name + "_st"])
        r = 0
        k = 0
        for segs in rows:
            nr = max(s.shape[0] for _, s in segs)
            for c0, src in segs:
                P.dma(vst[r:r + src.shape[0], c0:c0 + src.shape[1]], src, [], [name + "_st"],
                      chan="L%s%d" % (name, k), q="sync")
                k += 1
            r += nr
        ps, pkey = psk
        P.mm(ps[:, 0:R], [(vst[0:R, :], cf[0:R, C_ID:C_ID + R])], [name + "_st", "cf"], [pkey])
        cols = P.sb(name, [128, R], F32)
        P.cp(cols[:], ps[:, 0:R], [pkey], [name])
        return cols

    def v2(ap, p=128):
        return ap.rearrange("(k p) -> k p", p=p)

    def v1(ap):
        return ap.rearrange("(o n) -> o n", o=1)

    def phase1a():
        P = Phase(nc, "a")
        cf, cb = consts(P)
        ident = cb[:, 0:128]
        ps_r = P.pring("ps", 6, [128, 512], F32)
        cols = make_cols(P, cf, [[(0, v2(g_mix))]], ps_r.next())
        W = P.sb("W", [128, 8, 4096], BF16)
        stg = P.ring("stg", 2, [128, 3072], F32)
        for k in range(8):
            st, sk = stg.next()
            P.load(st[:], w_in[k * 128:(k + 1) * 128, 0:3072], sk)
            g = cols[:, k:k + 1]
            wk = "W"
            s4 = st[:, 0:512].rearrange("p (h d) -> p h d", d=128)
            P.wcast(W[:, k, 0:512], st[:, 0:512], g, 1.0, [sk, "cols"], [wk])
            o4 = W[:, k, 512:1024].rearrange("p (h d) -> p h d", d=128)
            P.wcast(o4[:, :, 0:64], s4[:, :, 64:128], g, -1.0, [sk, "cols"], [wk])
            P.wcast(o4[:, :, 64:128], s4[:, :, 0:64], g, 1.0, [sk, "cols"], [wk])
            sc = 128.0 ** -0.5
            s4 = st[:, 512:1024].rearrange("p (h d) -> p h d", d=128)
            P.wcast(W[:, k, 1024:1536], st[:, 512:1024], g, sc, [sk, "cols"], [wk])
            o4 = W[:, k, 1536:2048].rearrange("p (h d) -> p h d", d=128)
            P.wcast(o4[:, :, 0:64], s4[:, :, 64:128], g, -sc, [sk, "cols"], [wk])
            P.wcast(o4[:, :, 64:128], s4[:, :, 0:64], g, sc, [sk, "cols"], [wk])
            P.wcast(W[:, k, 2048:4096], st[:, 1024:3072], g, 1.0, [sk, "cols"], [wk])

        xt_r = P.ring("xt", 4, [128, D], F32)
        junk = P.sb("junk", [128, D], BF16)
        st_r = P.ring("stat", 2, [128, 4], F32)
        h_r = P.ring("h", 2, [128, D], BF16)
        pT_r = P.pring("pT", 1, [128, 8, 128], BF16)
        hT_r = P.ring("hT", 2, [128, 8, 512], BF16)
        cs_r = P.ring("cs", 2, [128, 2, 512], F32)
        ktp_r = P.pring("ktp", 1, [128, 4, 128], BF16)
        t1_r = P.ring("t1", 2, [128, 512], F32)
        t2_r = P.ring("t2", 2, [128, 512], F32)
        qo_r = P.ring("qo", 2, [128, 4, 512], BF16)
        ko_r = P.ring("ko", 2, [128, 4, 512], BF16)
        kt_r = P.ring("kt", 2, [128, 4, 512], BF16)
        rg_r = P.ring("rg", 2, [128, 8, 512], BF16)
        vo_r = P.ring("vo", 2, [128, 4, 1024], BF16)

        glist = [(si, g) for si, S in enumerate(SEQS) for g in range(S // 512)]

        def norm_group(si, g):
            sc_ = SC[si]
            t0 = g * 512
            xs = []
            for j in range(4):
                xt, xk = xt_r.next()
                P.load(xt[:], X[si][t0 + j * 128:t0 + (j + 1) * 128, :], xk)
                xs.append((xt, xk))
            hT, hk = hT_r.next()
            for j in range(4):
                xt, xk = xs[j]
                st, stk = st_r.next()
                P.act(junk[:], xt[:], AF.Square, [xk], ["junk", stk + "a"], accum=st[:, 0:1])
                P.act(st[:, 1:2], st[:, 0:1], AF.Sqrt, [stk + "a"], [stk + "b"], scale=1.0 / D, bias=EPS)
                P.recip(st[:, 2:3], st[:, 1:2], [stk + "b"], [stk + "c"])
                h, hhk = h_r.next()
                P.act(h[:], xt[:], AF.Copy, [xk, stk + "c"], [hhk], scale=st[:, 2:3])
                pT, pk = pT_r.next()
                for k in range(8):
                    P.tr(pT[:, k, :], h[:, k * 128:(k + 1) * 128], ident, [hhk, "cb"], [pk])
                P.cp(hT[:, :, j * 128:(j + 1) * 128], pT[:], [pk], [hk])
            P.store(sc_["hT"].rearrange("(k p) s -> p k s", p=128)[:, :, t0:t0 + 512], hT[:], hk)
            return hT, hk

        nxt = norm_group(*glist[0])
        for gi, (si, g) in enumerate(glist):
            if True:
                sc_ = SC[si]
                t0 = g * 512
                hT, hk = nxt
                cs, ck = cs_r.next()
                P.load(cs[:, 0, :], cosr_d[:, t0:t0 + 512], ck + "c")
                P.load(cs[:, 1, :], sinr_d[:, t0:t0 + 512], ck + "s")
                def proj(col0):
                    ps, pk = ps_r.next()
                    P.mm(ps[:], [(W[:, k, col0:col0 + 128], hT[:, k, :]) for k in range(8)], [hk, "W"], [pk])
                    return ps, pk

                qo, qk = qo_r.next()
                ko, kk = ko_r.next()
                kt, ktk = kt_r.next()
                for (base, dst, dk) in ((0, qo, qk), (1024, ko, kk)):
                    for hh in range(4):
                        pa, pak = proj(base + hh * 128)
                        pb, pbk = proj(base + 512 + hh * 128)
                        t1, t1k = t1_r.next()
                        t2, t2k = t2_r.next()
                        P.tt(t1[:], pa[:], cs[:, 0, :], ALU.mult, [pak, ck + "c"], [t1k])
                        P.tt(t2[:], pb[:], cs[:, 1, :], ALU.mult, [pbk, ck + "s"], [t2k])
                        P.tt(dst[:, hh, :], t1[:], t2[:], ALU.add, [t1k, t2k], [dk + "h%d" % hh], eng="gpsimd")
                        if base == 1024:
                            ktp, ktpk = ktp_r.next()
                            for j in range(4):
                                P.tr(ktp[:, j, :], ko[:, hh, j * 128:(j + 1) * 128], ident, [dk + "h%d" % hh, "cb"], [ktpk])
                            P.cp(kt[:, :, hh * 128:(hh + 1) * 128], ktp[:], [ktpk], [ktk + "h%d" % hh])
                hkeys = lambda kk_: [kk_ + "h%d" % i for i in range(4)]
                P.dma(sc_["qrT"].rearrange("(h p) s -> p h s", p=128)[:, :, t0:t0 + 512], qo[:], hkeys(qk), [],
                      chan="T" + qk, q="gpsimd")
                P.dma(sc_["krT"].rearrange("(h p) s -> p h s", p=128)[:, :, t0:t0 + 512], ko[:], hkeys(kk), [],
                      chan="T" + kk, q="gpsimd")
                P.dma(sc_["ktok"][t0:t0 + 512, :].rearrange("(j p) c -> p j c", p=128), kt[:], hkeys(ktk), [],
                      chan="T" + ktk, q="gpsimd")
                if gi + 1 < len(glist):
                    nxt = norm_group(*glist[gi + 1])
                rg, rgk = rg_r.next()
                for c in range(8):
                    ps, pk = proj(3072 + c * 128)
                    P.act(rg[:, c, :], ps[:], AF.Silu, [pk], [rgk + "c%d" % c])
                P.dma(sc_["rgsT"].rearrange("(c p) s -> p c s", p=128)[:, :, t0:t0 + 512], rg[:],
                      [rgk + "c%d" % c for c in range(8)], [], chan="T" + rgk, q="gpsimd")
                vo, vk = vo_r.next()
                for j in range(4):
                    for n in range(2):
                        ps, pk = ps_r.next()
                        P.mm(ps[:], [(hT[:, k, j * 128:(j + 1) * 128], W[:, k, 2048 + n * 512:2048 + (n + 1) * 512])
                                     for k in range(8)], [hk, "W"], [pk])
                        P.act(vo[:, j, n * 512:(n + 1) * 512], ps[:], AF.Copy, [pk], [vk + "p%d" % (j * 2 + n)])
                P.dma(sc_["vtok"][t0:t0 + 512, :].rearrange("(j p) c -> p j c", p=128), vo[:],
                      [vk + "p%d" % i for i in range(8)], [], chan="T" + vk, q="gpsimd")
        P.finish()

    def phase1b():
        P = Phase(nc, "b")
        cf, cb = consts(P)
        ones = cb[:, 128:256]
        rows = [[(0, v2(g_mix))], [(0, v2(g_cq))], [(0, v2(g_ckv))],
                [(0, v1(g_qn[0:128]))], [(0, v1(g_kn[0:128]))],
                [(0, v1(g_qn[128:192]))], [(0, v1(g_qn[160:192])), (32, v1(g_qn[128:160]))],
                [(0, v1(g_kn[128:192]))], [(0, v1(g_kn[160:192])), (32, v1(g_kn[128:160]))]]
        ps_r = P.pring("ps", 6, [128, 512], F32)
        cols = make_cols(P, cf, rows, ps_r.next())
        GCQ, GCKV, GQN, GKN, GQR, GQRS, GKR, GKRS = 8, 11, 13, 14, 15, 16, 17, 18
        gx = P.sb("gx", [128, 4], F32)
        qs = 192.0 ** -0.5
        P.stt(gx[:, 0:1], cols[:, GQN:GQN + 1], qs, cols[:, GKN:GKN + 1], ALU.mult, ALU.mult, ["cols"], ["gx"])
        P.ts(gx[:, 1:3], cols[:, GQR:GQR + 2], qs, None, ALU.mult, ALU.bypass, ["cols"], ["gx"])
        W = P.sb("W", [128, 8, 2816], BF16)
        stg = P.ring("stg", 2, [128, 3072], F32)
        for k in range(8):
            st, sk = stg.next()
            P.load(st[:, 0:2752], w_in[k * 128:(k + 1) * 128, 3072:5824], sk)
            g = cols[:, k:k + 1]
            P.wcast(W[:, k, 0:704], st[:, 0:704], g, 1.0, [sk, "cols"], ["W"])
            P.wcast(W[:, k, 704:736], st[:, 672:704], g, -1.0, [sk, "cols"], ["W"])
            P.wcast(W[:, k, 736:768], st[:, 640:672], g, 1.0, [sk, "cols"], ["W"])
            P.wcast(W[:, k, 768:2816], st[:, 704:2752], g, 1.0, [sk, "cols"], ["W"])
        Wuq = P.sb("Wuq", [128, 3, 2112], BF16)
        P.memset(Wuq[:, :, 2048:2112], 0.0, ["Wuqz"])
        for k in range(3):
            st, sk = stg.next()
            P.load(st[:, 0:1536], w_uq[k * 128:(k + 1) * 128, :], sk)
            s3 = st[:, 0:1536].rearrange("p (h d) -> p h d", d=192)
            P.wcast(Wuq[:, k, 0:1024].rearrange("p (h d) -> p h d", d=128), s3[:, :, 0:128], None, 1.0, [sk], ["Wuq"])
            P.wcast(Wuq[:, k, 1024:1536].rearrange("p (h d) -> p h d", d=64), s3[:, :, 128:192], None, 1.0, [sk], ["Wuq"])
            o3 = Wuq[:, k, 1536:2048].rearrange("p (h d) -> p h d", d=64)
            P.wcast(o3[:, :, 0:32], s3[:, :, 160:192], None, -1.0, [sk], ["Wuq"])
            P.wcast(o3[:, :, 32:64], s3[:, :, 128:160], None, 1.0, [sk], ["Wuq"])
        Wuk = P.sb("Wuk", [128, 2, 1024], BF16)
        Wuv = P.sb("Wuv", [128, 2, 1024], BF16)
        for k in range(2):
            st, sk = stg.next()
            P.load(st[:, 0:2048], w_ukv[k * 128:(k + 1) * 128, :], sk)
            s3 = st[:, 0:2048].rearrange("p (h d) -> p h d", d=256)
            P.wcast(Wuk[:, k, :].rearrange("p (h d) -> p h d", d=128), s3[:, :, 0:128], None, 1.0, [sk], ["Wuk"])
            P.wcast(Wuv[:, k, :].rearrange("p (h d) -> p h d", d=128), s3[:, :, 128:256], None, 1.0, [sk], ["Wuv"])

        hT_r = P.ring("hT", 2, [128, 8, 512], BF16)
        cs_r = P.ring("cs", 2, [64, 2, 512], F32)
        pss_r = P.pring("pss", 1, [128, 512], F32)
        pst_r = P.pring("pst", 1, [128, 4, 8], F32)
        gr_r = P.ring("gr", 2, [128, 8, 512], BF16)
        ga_r = gr_r
        sq_r = P.ring("sq", 3, [128, 512], BF16)
        sqr_r = P.ring("sqr", 2, [128, 512], BF16)
        sqkr_r = P.ring("sqkr", 2, [128, 512], BF16)
        for r_ in (sqr_r, sqkr_r):
            for i_, t_ in enumerate(r_.tiles):
                P.memset(t_[64:128, :], 0.0, ["%sz%d" % (r_.name, i_)])
        ZK = ["sqrz0", "sqrz1", "sqkrz0", "sqkrz1"]
        sd_r = P.ring("sd", 2, [128, 512], F32)
        rs_r = P.ring("rs", 2, [128, 512], F32)
        cqn_r = P.ring("cqn", 2, [128, 3, 512], BF16)
        ckvn_r = P.ring("ckvn", 2, [128, 2, 512], BF16)
        t1_r = P.ring("t1", 2, [64, 512], F32)
        t2_r = P.ring("t2", 2, [64, 512], F32)
        t3_r = P.ring("t3", 1, [64, 512], F32)
        kro_r = P.ring("kro", 2, [64, 512], BF16)
        ka1 = P.sb("ka1", [64, 512], F32)
        ka2 = P.sb("ka2", [64, 512], F32)
        qno_r = P.ring("qno", 1, [128, 8, 512], BF16)
        qro_r = P.ring("qro", 1, [64, 8, 512], BF16)
        kno_r = P.ring("kno", 1, [128, 8, 512], BF16)
        sdk_r = P.ring("sdk", 2, [128, 32], F32)
        rk_r = P.ring("rk", 2, [128, 4, 8], F32)
        vmo_r = P.ring("vmo", 1, [128, 4, 1024], BF16)

        for si, S in enumerate(SEQS):
            sc_ = SC[si]
            for g in range(S // 512):
                t0 = g * 512
                hT, hk = hT_r.next()
                P.load(hT[:], sc_["hT"].rearrange("(k p) s -> p k s", p=128)[:, :, t0:t0 + 512], hk)
                cs, ck = cs_r.next()
                P.load(cs[:, 0, :], cosm_d[:, t0:t0 + 512], ck + "c")
                P.load(cs[:, 1, :], sinm_d[:, t0:t0 + 512], ck + "s")

                def proj(col0, m=128):
                    ps, pk = ps_r.next()
                    P.mm(ps[0:m, :], [(W[:, k, col0:col0 + m], hT[:, k, :]) for k in range(8)], [hk, "W"], [pk])
                    return ps, pk
                for (base, ring, dst) in ((768, gr_r, "grT"), (1792, ga_r, "gaT")):
                    gt, gk = ring.next()
                    for c in range(8):
                        ps, pk = proj(base + c * 128)
                        P.act(gt[:, c, :], ps[:], AF.Sigmoid, [pk], [gk + "c%d" % c])
                    P.dma(sc_[dst].rearrange("(c p) s -> p c s", p=128)[:, :, t0:t0 + 512], gt[:],
                          [gk + "c%d" % c for c in range(8)], [], chan="T" + gk, q="gpsimd")

                def latent(col0, nch, gcol0, ring, inv_n):
                    pcs = []
                    pss, pssk = pss_r.next()
                    sqs = []
                    for c in range(nch):
                        ps, pk = proj(col0 + c * 128)
                        sq, sqk = sq_r.next()
                        P.act(sq[:], ps[:], AF.Square, [pk], [sqk])
                        pcs.append((ps, pk)); sqs.append((sq, sqk))
                    P.mm(pss[:], [(ones, sq[:]) for sq, _ in sqs], [k_ for _, k_ in sqs] + ["cb"], [pssk])
                    sd, sdk_ = sd_r.next()
                    P.act(sd[:], pss[:], AF.Sqrt, [pssk], [sdk_], scale=inv_n, bias=EPS)
                    rs, rsk = rs_r.next()
                    P.recip(rs[:], sd[:], [sdk_], [rsk])
                    o, ok = ring.next()
                    for c in range(nch):
                        ps, pk = pcs[c]
                        P.stt(o[:, c, :], ps[:], cols[:, gcol0 + c:gcol0 + c + 1], rs[:], ALU.mult, ALU.mult,
                              [pk, rsk, "cols"], [ok + "c%d" % c])
                    return o, [ok + "c%d" % c for c in range(nch)]
                cqn, cqk = latent(0, 3, GCQ, cqn_r, 1.0 / 384)
                ckvn, ckvk = latent(384, 2, GCKV, ckvn_r, 1.0 / 256)

                pkr, pkrk = proj(640, 128)
                pkrr, pkrrk = proj(704, 128)
                sqkr, sqkrk = sqkr_r.next()
                P.act(sqkr[0:64, :], pkr[0:64, :], AF.Square, [pkrk], [sqkrk])
                u1, u1k = t1_r.next(); u2, u2k = t2_r.next()
                P.tt(u1[:], pkr[0:64, :], cs[:, 0, :], ALU.mult, [pkrk, ck + "c"], [u1k])
                P.tt(u2[:], pkrr[0:64, :], cs[:, 1, :], ALU.mult, [pkrrk, ck + "s"], [u2k])
                P.tt(ka1[:], u1[:], u2[:], ALU.add, [u1k, u2k], ["ka1"], eng="gpsimd")
                u3, u3k = t1_r.next(); u4, u4k = t2_r.next()
                P.tt(u3[:], pkrr[0:64, :], cs[:, 0, :], ALU.mult, [pkrrk, ck + "c"], [u3k])
                P.tt(u4[:], pkr[0:64, :], cs[:, 1, :], ALU.mult, [pkrk, ck + "s"], [u4k])
                P.tt(ka2[:], u3[:], u4[:], ALU.subtract, [u3k, u4k], ["ka2"], eng="gpsimd")
                t1, t1k = t1_r.next(); t2, t2k = t2_r.next()
                P.stt(t1[:], ka1[:], cols[0:64, GKR:GKR + 1], cs[:, 0, :], ALU.mult, ALU.mult,
                      ["ka1", ck + "c", "cols"], [t1k])
                P.stt(t2[:], ka2[:], cols[0:64, GKRS:GKRS + 1], cs[:, 1, :], ALU.mult, ALU.mult,
                      ["ka2", ck + "s", "cols"], [t2k])
                kro, krok = kro_r.next()
                P.tt(kro[:], t1[:], t2[:], ALU.add, [t1k, t2k], [krok], eng="gpsimd")
                P.store(sc_["kmrT"][:, t0:t0 + 512], kro[:], krok)

                qno, qnk = qno_r.next()
                qro, qrk = qro_r.next()
                for hh in range(8):
                    psn, psnk = ps_r.next()
                    P.mm(psn[:], [(Wuq[:, k, hh * 128:(hh + 1) * 128], cqn[:, k, :]) for k in range(3)], cqk + ["Wuq"], [psnk])
                    psr, psrk = ps_r.next()
                    P.mm(psr[:, :], [(Wuq[:, k, 1024 + hh * 64:1024 + hh * 64 + 128], cqn[:, k, :]) for k in range(3)],
                         cqk + ["Wuq"], [psrk])
                    psrr, psrrk = ps_r.next()
                    P.mm(psrr[:, :], [(Wuq[:, k, 1536 + hh * 64:1536 + hh * 64 + 128], cqn[:, k, :]) for k in range(3)],
                         cqk + ["Wuq", "Wuqz"], [psrrk])
                    sq, sqk = sq_r.next()
                    P.act(sq[:], psn[:], AF.Square, [psnk], [sqk])
                    sqr, sqrk = sqr_r.next()
                    P.act(sqr[0:64, :], psr[0:64, :], AF.Square, [psrk], [sqrk])
                    pss, pssk = pss_r.next()
                    P.mm(pss[:], [(ones, sq[:]), (ones, sqr[:])], [sqk, sqrk, "cb"] + ZK, [pssk])
                    sd, sdk_ = sd_r.next()
                    P.act(sd[:], pss[:], AF.Sqrt, [pssk], [sdk_], scale=1.0 / 192, bias=EPS)
                    rs, rsk = rs_r.next()
                    P.recip(rs[:], sd[:], [sdk_], [rsk])
                    P.stt(qno[:, hh, :], psn[:], gx[:, 0:1], rs[:], ALU.mult, ALU.mult, [psnk, rsk, "gx"], [qnk + "h%d" % hh])
                    t1, t1k = t1_r.next(); t2, t2k = t2_r.next(); t3, t3k = t3_r.next()
                    P.stt(t1[:], psr[0:64, :], gx[0:64, 1:2], cs[:, 0, :], ALU.mult, ALU.mult, [psrk, ck + "c", "gx"], [t1k])
                    P.stt(t2[:], psrr[0:64, :], gx[0:64, 2:3], cs[:, 1, :], ALU.mult, ALU.mult, [psrrk, ck + "s", "gx"], [t2k])
                    P.tt(t3[:], t1[:], t2[:], ALU.add, [t1k, t2k], [t3k], eng="gpsimd")
                    P.tt(qro[:, hh, :], t3[:], rs[0:64, :], ALU.mult, [t3k, rsk], [qrk + "h%d" % hh], eng="gpsimd")
                P.dma(sc_["qmnT"].rearrange("h p s -> p h s")[:, :, t0:t0 + 512], qno[:],
                      [qnk + "h%d" % i for i in range(8)], [], chan="T" + qnk, q="gpsimd")
                P.dma(sc_["qmrT"].rearrange("h p s -> p h s")[:, :, t0:t0 + 512], qro[:],
                      [qrk + "h%d" % i for i in range(8)], [], chan="T" + qrk, q="gpsimd")

                kno, knk = kno_r.next()
                pst, pstk = pst_r.next()
                for hh in range(8):
                    ps, pk = ps_r.next()
                    P.mm(ps[:], [(Wuk[:, k, hh * 128:(hh + 1) * 128], ckvn[:, k, :]) for k in range(2)], ckvk + ["Wuk"], [pk])
                    P.act(kno[:, hh, :], ps[:], AF.Copy, [pk], [knk + "h%d" % hh])
                    sq, sqk = sq_r.next()
                    P.act(sq[:], ps[:], AF.Square, [pk], [sqk])
                    for j in range(4):
                        P.mm(pst[:, j, hh:hh + 1], [(sq[:, j * 128:(j + 1) * 128], cb[:, 128:129]),
                                                    (sqkr[:, j * 128:(j + 1) * 128], cb[:, 128:129])],
                             [sqk, sqkrk, "cb"] + ZK, [pstk])
                P.dma(sc_["kmnT"].rearrange("h p s -> p h s")[:, :, t0:t0 + 512], kno[:],
                      [knk + "h%d" % i for i in range(8)], [], chan="T" + knk, q="gpsimd")
                sdk, sdkk = sdk_r.next()
                P.act(sdk[:], pst[:].rearrange("p j h -> p (j h)"), AF.Sqrt, [pstk], [sdkk], scale=1.0 / 192, bias=EPS)
                rk, rkk = rk_r.next()
                P.recip(rk[:].rearrange("p j h -> p (j h)"), sdk[:], [sdkk], [rkk])
                P.store(sc_["rstdk"][t0:t0 + 512, :].rearrange("(j p) h -> p j h", p=128), rk[:], rkk)

                vmo, vmk = vmo_r.next()
                for j in range(4):
                    for n in range(2):
                        ps, pk = ps_r.next()
                        P.mm(ps[:], [(ckvn[:, k, j * 128:(j + 1) * 128], Wuv[:, k, n * 512:(n + 1) * 512]) for k in range(2)],
                             ckvk + ["Wuv"], [pk])
                        P.act(vmo[:, j, n * 512:(n + 1) * 512], ps[:], AF.Copy, [pk], [vmk + "p%d" % (j * 2 + n)])
                P.dma(sc_["vmtok"][t0:t0 + 512, :].rearrange("(j p) c -> p j c", p=128), vmo[:],
                      [vmk + "p%d" % i for i in range(8)], [], chan="T" + vmk, q="gpsimd")
        P.finish()

    def phase2():
        P = Phase(nc, "r")
        cf, cb = consts(P)
        ident = cb[:, 0:128]
        psP_r = P.pring("psP", 1, [128, 512], F32)
        cols = make_cols(P, cf, [[(0, v2(ret_gn_g))]], psP_r.next())
        dst = P.sb("dst", [1, 8], F32)
        P.dma(dst[0:1, 0:4], v1(dec_f), [], ["dst"], chan="Ldst0")
        P.dma(dst[0:1, 4:8], v1(dec_b), [], ["dst"], chan="Ldst1")
        pdc, pdck = psP_r.next()
        P.mm(pdc[:, 0:8], [(cf[0:1, C_ONE:C_ONE + 128], dst[0:1, 0:8])], ["dst", "cf"], [pdck])
        lg = P.sb("lg", [128, 8], F32)
        P.act(lg[:], pdc[:, 0:8], AF.Exp, [pdck], ["lg"], scale=-1.0)
        P.ts(lg[:], lg[:], 1.0, None, ALU.add, ALU.bypass, ["lg"], ["lg"])
        P.act(lg[:], lg[:], AF.Ln, ["lg"], ["lg"])
        P.ts(lg[:], lg[:], -1.0, None, ALU.mult, ALU.bypass, ["lg"], ["lg"])
        DT = P.sb("DT", [128, 4, 128], F32)
        decf = P.sb("decf", [128, 4, 128], F32)
        decb = P.sb("decb", [128, 4, 128], F32)
        kcol = P.sb("kcol", [128, 16], F32)
        e1 = P.sb("e1", [128, 128], F32)
        e2 = P.sb("e2", [128, 128], F32)
        for hh in range(4):
            lf = lg[:, hh:hh + 1]; lb = lg[:, 4 + hh:5 + hh]
            P.act(e1[:], cf[:, C_A:C_A + 128], AF.Exp, ["cf", "lg"], ["e1"], scale=lf)
            P.tt(e1[:], e1[:], cf[:, C_MF:C_MF + 128], ALU.mult, ["e1", "cf"], ["e1"])
            P.act(e2[:], cf[:, C_B:C_B + 128], AF.Exp, ["cf", "lg"], ["e2"], scale=lb)
            P.tt(e2[:], e2[:], cf[:, C_MB:C_MB + 128], ALU.mult, ["e2", "cf"], ["e2"])
            P.tt(DT[:, hh, :], e1[:], e2[:], ALU.add, ["e1", "e2"], ["DT"])
            P.act(decf[:, hh, :], cf[:, C_C1:C_C1 + 128], AF.Exp, ["cf", "lg"], ["decf"], scale=lf)
            P.act(decb[:, hh, :], cf[:, C_C2:C_C2 + 128], AF.Exp, ["cf", "lg"], ["decb"], scale=lb)
            P.act(kcol[:, hh:hh + 1], cf[:, C_SM:C_SM + 1], AF.Exp, ["cf", "lg"], ["kcol"], scale=lf)
            P.act(kcol[:, 4 + hh:5 + hh], cf[:, C_SM + 1:C_SM + 2], AF.Exp, ["cf", "lg"], ["kcol"], scale=lb)
            P.act(kcol[:, 8 + hh:9 + hh], cf[:, C_SM + 2:C_SM + 3], AF.Exp, ["cf", "lg"], ["kcol"], scale=lf)
            P.act(kcol[:, 12 + hh:13 + hh], cf[:, C_SM + 2:C_SM + 3], AF.Exp, ["cf", "lg"], ["kcol"], scale=lb)
        Wro = P.sb("Wro", [128, 8, 1024], BF16)
        stg = P.ring("stg", 2, [128, 1024], F32)
        for k in range(8):
            st, sk = stg.next()
            P.load(st[:], w_ret_o[k * 128:(k + 1) * 128, :], sk)
            P.wcast(Wro[:, k, :], st[:], cols[:, k:k + 1], 1.0, [sk, "cols"], ["Wro"])

        kt_r = P.ring("kt", 2, [128, 4, 512], BF16)
        v_r = P.ring("v", 2, [128, 4, 1024], BF16)
        kd_r = P.ring("kd", 2, [128, 4, 128], BF16)
        psU_r = P.pring("psU", 1, [128, 512], F32)
        St = P.sb("St", [128, 4, 256], F32)
        sbb_r = P.ring("sbb", 3, [128, 1024], BF16)

        kfb = {}
        for col0 in (0, 4):
            t_ = P.sb("kfb%d" % col0, [128, 4, 128], F32)
            for hh in range(4):
                P.act(t_[:, hh, :], cf[:, C_ONE:C_ONE + 128], AF.Copy, ["cf", "kcol"], ["kfb%d" % col0],
                      scale=kcol[:, col0 + hh:col0 + hh + 1])
            kfb[col0] = t_

        def kdec(kt, ktk, j, col0):
            kd, kdk = kd_r.next()
            P.tt(kd[:], kt[:, j, :].rearrange("p (h d) -> p h d", d=128), kfb[col0][:], ALU.mult,
                 [ktk, "kfb%d" % col0], [kdk], eng="gpsimd")
            return kd, [kdk]

        def state_update(kd, kdk, v, vk, j, gcol0):
            for hp in range(2):
                psU, psUk = psU_r.next()
                for h2 in range(2):
                    hh = hp * 2 + h2
                    P.mm(psU[:, h2 * 256:(h2 + 1) * 256], [(kd[:, hh, :], v[:, j, hh * 256:(hh + 1) * 256])], kdk + [vk], [psUk])
                for h2 in range(2):
                    hh = hp * 2 + h2
                    P.stt(St[:, hh, :], St[:, hh, :], kcol[:, gcol0 + hh:gcol0 + hh + 1], psU[:, h2 * 256:(h2 + 1) * 256],
                          ALU.mult, ALU.add, ["St%d" % hh, psUk, "kcol"], ["St%d" % hh])

        for si, S in enumerate(SEQS):
            sc_ = SC[si]
            NG = S // 512
            P.memset(St[:], 0.0, ["St%d" % i_ for i_ in range(4)])
            for g in range(NG - 1, -1, -1):
                t0 = g * 512
                kt, ktk = kt_r.next()
                P.load(kt[:], sc_["ktok"][t0:t0 + 512, :].rearrange("(j p) c -> p j c", p=128), ktk)
                v, vk = v_r.next()
                P.load(v[:], sc_["vtok"][t0:t0 + 512, :].rearrange("(j p) c -> p j c", p=128), vk)
                for j in range(3, -1, -1):
                    n = g * 4 + j
                    sbb, sbk = sbb_r.next()
                    P.cp(sbb[:], St[:].rearrange("p h d -> p (h d)"), ["St%d" % i_ for i_ in range(4)], [sbk])
                    P.store(sc_["sb"][n, :, :], sbb[:], sbk, dkey="sbd%d_%d" % (si, n))
                    if n > 0:
                        kd, kdk = kdec(kt, ktk, j, 4)
                        state_update(kd, kdk, v, vk, j, 12)
            P.memset(St[:], 0.0, ["St%d" % i_ for i_ in range(4)])
            fw = getattr(P, "_fw", None)
            if fw is None:
                fw = dict(
                    qT=P.ring("qT", 2, [128, 4, 512], BF16), kT=P.ring("kT", 2, [128, 4, 512], BF16),
                    sbl=P.ring("sbl", 2, [128, 4, 1024], BF16), rgs=P.ring("rgs", 2, [128, 8, 512], BF16),
                    gr=P.ring("gr", 2, [128, 8, 512], BF16),
                    psS=P.pring("psS", 1, [128, 4, 128], F32), psO=P.pring("psO", 2, [128, 1024], F32),
                    tp=P.pring("tp", 1, [128, 8, 128], BF16), psP=psP_r,
                    pT=P.ring("pT", 2, [128, 4, 128], BF16), qf=P.ring("qf", 2, [128, 4, 128], BF16),
                    qb=P.ring("qb", 2, [128, 4, 128], BF16), Sfb=P.ring("Sfb", 3, [128, 4, 256], BF16),
                    st=P.ring("st", 2, [128, 32], F32), retn=P.ring("retn", 2, [128, 1024], BF16),
                    retg=P.ring("retg", 2, [128, 8, 512], BF16), m1=P.ring("m1", 2, [128, 8, 512], BF16),
                    junk=P.sb("junk", [128, 256], BF16),
                )
                P._fw = fw
            Sfb, Sfk = fw["Sfb"].next()
            P.cp(Sfb[:], St[:], ["St%d" % i_ for i_ in range(4)], [Sfk])
            tails = []
            for g in range(NG):
                t0 = g * 512
                qT, qTk = fw["qT"].next()
                P.load(qT[:], sc_["qrT"].rearrange("(h p) s -> p h s", p=128)[:, :, t0:t0 + 512], qTk)
                kT, kTk = fw["kT"].next()
                P.load(kT[:], sc_["krT"].rearrange("(h p) s -> p h s", p=128)[:, :, t0:t0 + 512], kTk)
                kt, ktk = kt_r.next()
                P.load(kt[:], sc_["ktok"][t0:t0 + 512, :].rearrange("(j p) c -> p j c", p=128), ktk)
                v, vk = v_r.next()
                P.load(v[:], sc_["vtok"][t0:t0 + 512, :].rearrange("(j p) c -> p j c", p=128), vk)
                sbl, sblk = fw["sbl"].next()
                P.dma(sbl[:], sc_["sb"][g * 4:(g + 1) * 4, :, :].rearrange("j p c -> p j c"),
                      ["sbd%d_%d" % (si, g * 4 + j) for j in range(4)], [sblk], chan="L" + sblk)
                rgs, rgsk = fw["rgs"].next()
                P.load(rgs[:], sc_["rgsT"].rearrange("(c p) s -> p c s", p=128)[:, :, t0:t0 + 512], rgsk)
                gr, grk = fw["gr"].next()
                P.load(gr[:], sc_["grT"].rearrange("(c p) s -> p c s", p=128)[:, :, t0:t0 + 512], grk)
                retg, retgk = fw["retg"].next()
                for j in range(4):
                    sl = slice(j * 128, (j + 1) * 128)
                    psS, psSk = fw["psS"].next()
                    for hh in range(4):
                        P.mm(psS[:, hh, :], [(kT[:, hh, sl], qT[:, hh, sl])], [kTk, qTk], [psSk])
                    pT, pTk = fw["pT"].next()
                    P.tt(pT[:], psS[:], DT[:], ALU.mult, [psSk, "DT"], [pTk])
                    qf, qfk = fw["qf"].next()
                    P.tt(qf[:], qT[:, :, sl], decf[:], ALU.mult, [qTk, "decf"], [qfk], eng="gpsimd")
                    qb, qbk = fw["qb"].next()
                    P.tt(qb[:], qT[:, :, sl], decb[:], ALU.mult, [qTk, "decb"], [qbk], eng="gpsimd")
                    Sfb_old, Sfk_old = Sfb, Sfk
                    kd, kdk = kdec(kt, ktk, j, 0)
                    state_update(kd, kdk, v, vk, j, 8)
                    Sfb, Sfk = fw["Sfb"].next()
                    P.cp(Sfb[:], St[:], ["St%d" % i_ for i_ in range(4)], [Sfk])
                    psO, psOk = fw["psO"].next()
                    for hh in range(4):
                        vs = v[:, j, hh * 256:(hh + 1) * 256]
                        P.mm(psO[:, hh * 256:(hh + 1) * 256],
                             [(pT[:, hh, :], vs), (qf[:, hh, :], Sfb_old[:, hh, :]),
                              (qb[:, hh, :], sbl[:, j, hh * 256:(hh + 1) * 256])],
                             [pTk, vk, qfk, Sfk_old, qbk, sblk], [psOk])
                    while tails:
                        tails.pop(0)()
                    st, stk = fw["st"].next()
                    for hh in range(4):
                        o = psO[:, hh * 256:(hh + 1) * 256]
                        P.act(fw["junk"][:], o, AF.Copy, [psOk], ["rjunk", stk + "a"], accum=st[:, hh:hh + 1])
                    for hh in range(4):
                        o = psO[:, hh * 256:(hh + 1) * 256]
                        P.act(fw["junk"][:], o, AF.Square, [psOk], ["rjunk", stk + "a2"], accum=st[:, 4 + hh:5 + hh])
                    P.ts(st[:, 8:12], st[:, 0:4], 1.0 / 256, None, ALU.mult, ALU.bypass, [stk + "a"], [stk + "b"])
                    P.tt(st[:, 12:16], st[:, 8:12], st[:, 8:12], ALU.mult, [stk + "b"], [stk + "c"])
                    P.stt(st[:, 16:20], st[:, 4:8], 1.0 / 256, st[:, 12:16], ALU.mult, ALU.subtract, [stk + "a2", stk + "c"], [stk + "d"])
                    P.act(st[:, 20:24], st[:, 16:20], AF.Sqrt, [stk + "d"], [stk + "e"], bias=EPS)
                    P.recip(st[:, 24:28], st[:, 20:24], [stk + "e"], [stk + "f"])
                    P.stt(st[:, 28:32], st[:, 8:12], -1.0, st[:, 24:28], ALU.mult, ALU.mult, [stk + "b", stk + "f"], [stk + "g"])
                    retn, retnk = fw["retn"].next()
                    for hh in range(4):
                        P.act(retn[:, hh * 256:(hh + 1) * 256], psO[:, hh * 256:(hh + 1) * 256], AF.Identity,
                              [psOk, stk + "f", stk + "g"], [retnk], scale=st[:, 24 + hh:25 + hh], bias=st[:, 28 + hh:29 + hh])

                    def tail(j=j, sl=sl, retn=retn, retnk=retnk, retg=retg, retgk=retgk, rgs=rgs, rgsk=rgsk,
                             gr=gr, grk=grk, t0=t0):
                        tp, tpk = fw["tp"].next()
                        for c in range(8):
                            P.tr(tp[:, c, :], retn[:, c * 128:(c + 1) * 128], ident, [retnk, "cb"], [tpk])
                        P.tt(retg[:, :, sl], tp[:], rgs[:, :, sl], ALU.mult, [tpk, rgsk], [retgk + "j%d" % j])
                        if j == 3:
                            m1, m1k = fw["m1"].next()
                            for c in range(8):
                                psP, psPk = fw["psP"].next()
                                P.mm(psP[:], [(Wro[:, k, c * 128:(c + 1) * 128], retg[:, k, :]) for k in range(8)],
                                     [retgk + "j%d" % jj for jj in range(4)] + ["Wro"], [psPk])
                                P.tt(m1[:, c, :], psP[:], gr[:, c, :], ALU.mult, [psPk, grk], [m1k + "c%d" % c])
                            P.dma(sc_["m1T"].rearrange("(c p) s -> p c s", p=128)[:, :, t0:t0 + 512], m1[:],
                                  [m1k + "c%d" % c for c in range(8)], [], chan="T" + m1k, q="gpsimd")
                    tails.append(tail)
            while tails:
                tails.pop(0)()
        P.finish()

    def phase3():
        P = Phase(nc, "m")
        cf, cb = consts(P)
        ones = cb[:, 128:256]
        onesf = cf[:, C_ONE:C_ONE + 128]
        Kn_r = P.ring("Kn", 2, [128, SMAX], BF16)
        V_r = P.ring("V", 2, [128, SMAX // 128, 128], BF16)
        Kr = P.sb("Kr", [128, SMAX], BF16)
        P.memset(Kr[64:128, :], 0.0, ["Krz"])
        rk = P.sb("rk", [128, SMAX // 128, 8], F32)
        Qn_r = P.ring("Qn", 3, [128, 1024], BF16)
        Qr_r = P.ring("Qr", 3, [128, 1024], BF16)
        for i_, t_ in enumerate(Qr_r.tiles):
            P.memset(t_[64:128, :], 0.0, ["Qrz%d" % i_])
        st_r = P.pring("st", 2, [128, 1024], F32)
        o_r = P.pring("o", 1, [128, 1024], F32)
        l_r = P.pring("l", 1, [128, 1024], F32)
        p_r = P.ring("p", 4, [128, 1024], BF16)
        acc_r = [P.ring("acc0", 2, [128, 1024], F32), P.ring("acc1", 2, [128, 1024], F32)]
        accs_r = P.ring("accs", 2, [128, 1024], F32)
        rl_r = P.ring("rl", 2, [128, 1024], F32)
        at_r = P.ring("at", 2, [128, 1024], BF16)
        LAG = 2
        units = [(si, hh, qp, kt) for si, S in enumerate(SEQS) for hh in range(8)
                 for qp in range(S // 1024) for kt in range(S // 128)]
        cur = {}
        pend = []

        def stage_a(u):
            si, hh, qp, kt = u
            S = SEQS[si]; sc_ = SC[si]; NK = S // 128
            if hh == 0 and qp == 0 and kt == 0:
                P.load(Kr[0:64, 0:S], sc_["kmrT"][:, :], "Kr")
                P.load(rk[:, 0:NK, :], sc_["rstdk"].rearrange("(t p) h -> p t h", p=128), "rk")
            if qp == 0 and kt == 0:
                Kn, Knk = Kn_r.next()
                P.load(Kn[:, 0:S], sc_["kmnT"][hh, :, :], Knk)
                V, Vk = V_r.next()
                P.load(V[:, 0:NK, :], sc_["vmtok"][:, hh * 128:(hh + 1) * 128].rearrange("(t p) c -> p t c", p=128), Vk)
                cur["K"] = (Kn, Knk, V, Vk)
            if kt == 0:
                q0 = qp * 1024
                Qn, Qnk = Qn_r.next()
                P.load(Qn[:], sc_["qmnT"][hh, :, q0:q0 + 1024], Qnk)
                Qr, Qrk = Qr_r.next()
                P.load(Qr[0:64, :], sc_["qmrT"][hh, :, q0:q0 + 1024], Qrk)
                cur["Q"] = (Qn, Qnk, Qr, Qrk)
            Kn, Knk, V, Vk = cur["K"]
            Qn, Qnk, Qr, Qrk = cur["Q"]
            ks = slice(kt * 128, (kt + 1) * 128)
            st, stk = st_r.next()
            for g2 in range(2):
                gs = slice(g2 * 512, (g2 + 1) * 512)
                P.mm(st[:, gs], [(Kn[:, ks], Qn[:, gs]), (Kr[:, ks], Qr[:, gs])],
                     [Knk, "Kr", "Krz", Qnk, Qrk] + ["Qrz%d" % i_ for i_ in range(3)], [stk])
            p, pk = p_r.next()
            P.act(p[:], st[:], AF.Exp, [stk, "rk"], [pk], scale=rk[:, kt, hh:hh + 1])
            pend.append((u, p, pk, V, Vk))

        def stage_b():
            u, p, pk, V, Vk = pend.pop(0)
            si, hh, qp, kt = u
            S = SEQS[si]; sc_ = SC[si]; NK = S // 128
            if kt == 0:
                cur["o"] = o_r.next()
                cur["l"] = l_r.next()
                cur["acc"] = [acc_r[0].next(), acc_r[1].next()]
                cur["na"] = 0
            o, ok = cur["o"]
            l, lk = cur["l"]
            for g2 in range(2):
                gs = slice(g2 * 512, (g2 + 1) * 512)
                P.S.op("tensor", lambda e, o=o, V=V, kt=kt, p=p, NK=NK, gs=gs: e.matmul(
                    o[:, gs], V[:, kt, :], p[:, gs], start=(kt == 0), stop=(kt == NK - 1)), [Vk, pk], [ok])
            if kt % 4 == 3:
                for g2 in range(2):
                    gs = slice(g2 * 512, (g2 + 1) * 512)
                    P.S.op("tensor", lambda e, l=l, p=p, kt=kt, gs=gs: e.matmul(
                        l[:, gs], ones, p[:, gs], start=(kt == 3), stop=False), [pk, "cb"], [lk])
            else:
                na = cur["na"]; cur["na"] = na + 1
                a, ak = cur["acc"][na % 2]
                if na < 2:
                    P.cp(a[:], p[:], [pk], [ak])
                else:
                    P.tt(a[:], a[:], p[:], ALU.add, [ak, pk], [ak])
            if kt == NK - 1:
                q0 = qp * 1024
                (a0, a0k), (a1, a1k) = cur["acc"]
                asum, asumk = accs_r.next()
                P.tt(asum[:], a0[:], a1[:], ALU.add, [a0k, a1k], [asumk])
                for g2 in range(2):
                    gs = slice(g2 * 512, (g2 + 1) * 512)
                    P.S.op("tensor", lambda e, l=l, asum=asum, gs=gs: e.matmul(
                        l[:, gs], onesf, asum[:, gs], start=False, stop=True), [asumk, "cf"], [lk])
                rl, rlk = rl_r.next()
                P.recip(rl[:], l[:], [lk], [rlk])
                at, atk = at_r.next()
                P.tt(at[:], o[:], rl[:], ALU.mult, [ok, rlk], [atk])
                P.store(sc_["attnT"][hh * 128:(hh + 1) * 128, q0:q0 + 1024], at[:], atk)

        for i in range(len(units) + LAG):
            if i < len(units):
                stage_a(units[i])
            if i >= LAG:
                stage_b()
        P.finish()

    def phase4():
        P = Phase(nc, "o")
        cf, cb = consts(P, need_cf=False)
        ident = cb[:, 0:128]
        Wmo = P.sb("Wmo", [128, 8, 1024], BF16)
        Wo = P.sb("Wo", [128, 8, 1024], BF16)
        stg = P.ring("stg", 2, [128, 1024], F32)
        for (Wt, src, wk) in ((Wmo, w_mla_o, "Wmo"), (Wo, w_out, "Wo")):
            for k in range(8):
                st, sk = stg.next()
                P.load(st[:], src[k * 128:(k + 1) * 128, :], sk)
                P.wcast(Wt[:, k, :], st[:], None, 1.0, [sk], [wk])
        at_r = P.ring("at", 2, [128, 8, 512], BF16)
        ga_r = P.ring("ga", 2, [128, 8, 512], BF16)
        m1_r = P.ring("m1", 2, [128, 8, 512], BF16)
        xt_r = P.ring("xt", 4, [128, D], F32)
        ps_r = P.pring("ps", 6, [128, 512], F32)
        pT_r = P.pring("pT", 2, [128, 8, 128], BF16)
        tm_r = P.ring("tm", 2, [128, 512], F32)
        mg_r = P.ring("mg", 2, [128, 8, 512], BF16)
        x1_r = P.ring("x1", 2, [128, D], F32)
        junk = P.sb("junk", [128, D], BF16)
        st_r = P.ring("stat", 2, [128, 4], F32)
        h_r = P.ring("h", 2, [128, D], BF16)
        hT_r = P.ring("hT", 2, [128, 8, 512], BF16)
        for si, S in enumerate(SEQS):
            sc_ = SC[si]
            h2v = sc_["h2T"].rearrange("(k p) s -> p k s", p=128)
            for g in range(S // 512):
                t0 = g * 512
                at, atk = at_r.next()
                P.load(at[:], sc_["attnT"].rearrange("(c p) s -> p c s", p=128)[:, :, t0:t0 + 512], atk)
                ga, gak = ga_r.next()
                P.load(ga[:], sc_["gaT"].rearrange("(c p) s -> p c s", p=128)[:, :, t0:t0 + 512], gak)
                m1, m1k = m1_r.next()
                P.load(m1[:], sc_["m1T"].rearrange("(c p) s -> p c s", p=128)[:, :, t0:t0 + 512], m1k)
                xs = []
                for j in range(4):
                    xt, xk = xt_r.next()
                    P.load(xt[:], X[si][t0 + j * 128:t0 + (j + 1) * 128, :], xk)
                    xs.append((xt, xk))
                mg, mgk = mg_r.next()
                for c in range(8):
                    ps, pk = ps_r.next()
                    P.mm(ps[:], [(Wmo[:, k, c * 128:(c + 1) * 128], at[:, k, :]) for k in range(8)], [atk, "Wmo"], [pk])
                    tm, tmk = tm_r.next()
                    P.tt(tm[:], ps[:], ga[:, c, :], ALU.mult, [pk, gak], [tmk])
                    P.tt(mg[:, c, :], tm[:], m1[:, c, :], ALU.add, [tmk, m1k], [mgk + "c%d" % c], eng="gpsimd")
                mgks = [mgk + "c%d" % c for c in range(8)]
                hT, hk = hT_r.next()
                pend_tr = []
                for j in range(4):
                    xt, xk = xs[j]
                    x1, x1k = x1_r.next()
                    for n in range(2):
                        ps, pk = ps_r.next()
                        P.mm(ps[:], [(mg[:, k, j * 128:(j + 1) * 128], Wo[:, k, n * 512:(n + 1) * 512]) for k in range(8)],
                             mgks + ["Wo"], [pk])
                        P.tt(x1[:, n * 512:(n + 1) * 512], ps[:], xt[:, n * 512:(n + 1) * 512], ALU.add, [pk, xk], [x1k + "n%d" % n])
                    x1ks = [x1k + "n0", x1k + "n1"]
                    P.dma(Y[si][t0 + j * 128:t0 + (j + 1) * 128, :], x1[:], x1ks, [], chan="T" + x1k, q="gpsimd")
                    st, stk = st_r.next()
                    P.act(junk[:], x1[:], AF.Square, x1ks, ["junk", stk + "a"], accum=st[:, 0:1])
                    P.act(st[:, 1:2], st[:, 0:1], AF.Sqrt, [stk + "a"], [stk + "b"], scale=1.0 / D, bias=EPS)
                    P.recip(st[:, 2:3], st[:, 1:2], [stk + "b"], [stk + "c"])
                    h, hhk = h_r.next()
                    P.act(h[:], x1[:], AF.Copy, x1ks + [stk + "c"], [hhk], scale=st[:, 2:3])

                    def tr_tail(j=j, h=h, hhk=hhk, hT=hT, hk=hk):
                        pT, pk = pT_r.next()
                        for k in range(8):
                            P.tr(pT[:, k, :], h[:, k * 128:(k + 1) * 128], ident, [hhk, "cb"], [pk])
                        P.cp(hT[:, :, j * 128:(j + 1) * 128], pT[:], [pk], [hk])
                    if pend_tr:
                        pend_tr.pop(0)()
                    pend_tr.append(tr_tail)
                while pend_tr:
                    pend_tr.pop(0)()
                P.store(h2v[:, :, t0:t0 + 512], hT[:], hk)
        P.finish()

    def phase5():
        P = Phase(nc, "f")
        cf, cb = consts(P)
        pu_r = P.pring("pu", 4, [128, 512], F32)
        cols = make_cols(P, cf, [[(0, v2(g_ffn))]], pu_r.next())
        cwp = P.es.enter_context(nc.psum_tensor("f_cwp", [128, 44, 4], F32))
        stg = P.ring("stg", 2, [128, 1408], F32)
        for n in range(4):
            st, sk = stg.next()
            P.dma(st[0:3, :], conv_w[:, n * 1408:(n + 1) * 1408], [], [sk], chan="Lcw0" + sk)
            P.dma(st[3:4, :], v1(conv_b)[:, n * 1408:(n + 1) * 1408], [], [sk], chan="Lcw1" + sk)
            for c in range(11):
                P.mm(cwp[:, n * 11 + c, :], [(st[0:4, c * 128:(c + 1) * 128], cf[0:4, C_ID:C_ID + 4])], [sk, "cf"], ["cwp"])
        cw = P.sb("cw", [128, 44, 4], F32)
        P.cp(cw[:], cwp[:], ["cwp"], ["cw"])
        Wup = P.sb("Wup", [128, 8, 5632], BF16)
        Wd = P.sb("Wd", [128, 22, 1024], BF16)
        for k in range(8):
            for n in range(4):
                st, sk = stg.next()
                P.load(st[:], w_up[k * 128:(k + 1) * 128, n * 1408:(n + 1) * 1408], sk)
                P.wcast(Wup[:, k, n * 1408:(n + 1) * 1408], st[:], cols[:, k:k + 1], 1.0, [sk, "cols"], ["Wup"])
        for k in range(22):
            st, sk = stg.next()
            P.load(st[:, 0:1024], w_down[k * 128:(k + 1) * 128, :], sk)
            P.wcast(Wd[:, k, :], st[:, 0:1024], None, 1.0, [sk], ["Wd"])
        hT_r = P.ring("hT", 1, [128, 8, 512], BF16)
        pd_r = P.pring("pd", 3, [128, 512], F32)
        ta_r = P.ring("ta", 2, [128, 512], F32)
        tb_r = P.ring("tb", 2, [128, 512], F32)
        sa_r = P.ring("sa", 2, [128, 512], F32)
        act_r = P.ring("act", 1, [128, 22, 512], BF16)
        x1_r = P.ring("x1", 2, [128, D], F32)
        yo_r = P.ring("yo", 2, [128, D], F32)
        for si, S in enumerate(SEQS):
            sc_ = SC[si]
            h2v = sc_["h2T"].rearrange("(k p) s -> p k s", p=128)
            for (t0, n) in ffn_groups(S):
                hT, hk = hT_r.next()
                lo = max(t0 - 1, 0); hi = min(t0 + n + 1, S)
                P.load(hT[:, :, lo - (t0 - 1):hi - (t0 - 1)], h2v[:, :, lo:hi], hk)
                if t0 == 0:
                    P.memset(hT[:, :, 0:1], 0.0, [hk])
                if t0 + n == S:
                    P.memset(hT[:, :, n + 1:n + 2], 0.0, [hk])
                act, actk = act_r.next()
                for c in range(22):
                    def up(ch):
                        pu, puk = pu_r.next()
                        P.mm(pu[:, 0:n + 2], [(Wup[:, k, ch * 128:(ch + 1) * 128], hT[:, k, 0:n + 2]) for k in range(8)],
                             [hk, "Wup"], [puk])
                        return pu, puk

                    def conv(pu, puk, ch, ring):
                        t, tk = ring.next()
                        P.act(t[:, 0:n], pu[:, 1:n + 1], AF.Identity, [puk, "cw"], [tk], scale=cw[:, ch, 1:2], bias=cw[:, ch, 3:4])
                        P.stt(t[:, 0:n], pu[:, 0:n], cw[:, ch, 0:1], t[:, 0:n], ALU.mult, ALU.add, [puk, "cw", tk], [tk])
                        P.stt(t[:, 0:n], pu[:, 2:n + 2], cw[:, ch, 2:3], t[:, 0:n], ALU.mult, ALU.add, [puk, "cw", tk], [tk])
                        return t, tk
                    pa, pak = up(c)
                    pb, pbk = up(22 + c)
                    ta, tak = conv(pa, pak, c, ta_r)
                    tb, tbk = conv(pb, pbk, 22 + c, tb_r)
                    sa, sak = sa_r.next()
                    P.act(sa[:, 0:n], ta[:, 0:n], AF.Silu, [tak], [sak])
                    P.tt(act[:, c, 0:n], sa[:, 0:n], tb[:, 0:n], ALU.mult, [sak, tbk], [actk + "c%d" % c], eng="gpsimd")
                actks = [actk + "c%d" % c for c in range(22)]
                m0 = 0
                while m0 < n:
                    m = min(128, n - m0)
                    x1, x1k = x1_r.next()
                    P.load(x1[0:m, :], Y[si][t0 + m0:t0 + m0 + m, :], x1k)
                    yo, yok = yo_r.next()
                    for nn in range(2):
                        pd, pdk = pd_r.next()
                        P.mm(pd[0:m, :], [(act[:, k, m0:m0 + m], Wd[:, k, nn * 512:(nn + 1) * 512]) for k in range(22)],
                             actks + ["Wd"], [pdk])
                        P.tt(yo[0:m, nn * 512:(nn + 1) * 512], pd[0:m, :], x1[0:m, nn * 512:(nn + 1) * 512], ALU.add,
                             [pdk, x1k], [yok + "n%d" % nn])
                    P.dma(Y[si][t0 + m0:t0 + m0 + m, :], yo[0:m, :], [yok + "n0", yok + "n1"], [], chan="T" + yok, q="gpsimd")
                    m0 += m
        P.finish()

    import os
    nph = int(os.environ.get("KPH", "6"))
    for ph in (phase1a, phase1b, phase2, phase3, phase4, phase5)[:nph]:
        ph()
    return nc


def host_consts(SMAX):
    i = np.arange(128, dtype=np.float32)
    cf = np.zeros((128, NCF), np.float32)
    diff = i[None, :] - i[:, None]
    cf[:, C_A:C_A + 128] = np.maximum(diff, 0)
    cf[:, C_B:C_B + 128] = np.maximum(-diff, 0)
    cf[:, C_MF:C_MF + 128] = (diff >= 0)
    cf[:, C_MB:C_MB + 128] = (diff < 0)
    cf[:, C_C1:C_C1 + 128] = (i + 1.0)[None, :]
    cf[:, C_C2:C_C2 + 128] = (128.0 - i)[None, :]
    cf[:, C_ID:C_ID + 128] = np.eye(128, dtype=np.float32)
    cf[:, C_ONE:C_ONE + 128] = 1.0
    cf[:, C_SM] = 127.0 - i
    cf[:, C_SM + 1] = i
    cf[:, C_SM + 2] = 128.0
    cb = np.zeros((128, 256), np.float32)
    cb[:, 0:128] = np.eye(128)
    cb[:, 128:256] = 1.0
    cb = cb.astype(ml_dtypes.bfloat16)
    pos = np.arange(SMAX, dtype=np.float32)

    def tab(d):
        inv = (np.float32(10000.0) ** (-np.arange(0, d, 2, dtype=np.float32) / np.float32(d))).astype(np.float32)
        ang = (pos[:, None] * inv[None, :]).astype(np.float32)
        c = np.cos(ang).astype(np.float32).T
        s = np.sin(ang).astype(np.float32).T
        return (np.ascontiguousarray(np.concatenate([c, c], 0)), np.ascontiguousarray(np.concatenate([s, s], 0)))
    cosr, sinr = tab(128)
    cosm, sinm = tab(64)
    return dict(cf=cf, cb=cb, cosr=cosr, sinr=sinr, cosm=cosm, sinm=sinm)


_CACHE = {}


def run(x_list, weights, n_cores=8):
    SEQS = tuple(int(x.shape[1]) for x in x_list)
    if SEQS not in _CACHE:
        _CACHE[SEQS] = build(SEQS)
    nc = _CACHE[SEQS]
    hc = host_consts(max(SEQS))
    w = {}
    for k, v in weights.items():
        a = np.asarray(v, dtype=np.float32)
        w[k] = np.ascontiguousarray(a.reshape(a.shape[1:]))
    in_maps = []
    for c in range(n_cores):
        m = dict(w)
        m.update(hc)
        for i, x in enumerate(x_list):
            m["x%d" % i] = np.ascontiguousarray(np.asarray(x[c], dtype=np.float32))
        in_maps.append(m)
    res = run_bass_kernel_spmd(nc, in_maps, core_ids=list(range(n_cores)))
    global LAST_RES
    LAST_RES = res
    outs = []
    for i in range(len(x_list)):
        outs.append(np.stack([np.asarray(res.results[c]["y%d" % i], dtype=np.float32) for c in range(n_cores)], 0))
    return tuple(outs)


def kernel(x_prompt, x_sample, **weights):
    return run([np.asarray(x_prompt), np.asarray(x_sample)], weights)
```

```python
import contextlib
import math
import numpy as np
import ml_dtypes
import concourse.bass as bass
import concourse.mybir as mybir
from concourse.bass_utils import run_bass_kernel_spmd

F32 = mybir.dt.float32
BF16 = mybir.dt.bfloat16
AF = mybir.ActivationFunctionType
ALU = mybir.AluOpType

D = 1024
IN_W = 5824
EPS = 1e-6
SAME_ENGINE_SYNC = True
ENGS = ("sync", "scalar", "vector", "gpsimd", "tensor")

C_A, C_B, C_MF, C_MB, C_C1, C_C2, C_ID, C_ONE, C_SM = 0, 128, 256, 384, 512, 640, 768, 896, 1024
NCF = 1032


class Op:
    __slots__ = ("eng", "fn", "chan", "inc", "waits", "signal", "val", "idx")

    def __init__(self, eng, fn, chan, inc):
        self.eng = eng; self.fn = fn; self.chan = chan; self.inc = inc
        self.waits = {}; self.signal = False; self.val = None; self.idx = None


class Sched:
    def __init__(self, nc, tag=""):
        self.nc = nc
        self.tag = tag
        self.ops = {e: [] for e in ENGS}
        self.chan_ops = {}
        self.last_w = {}
        self.readers = {}
        self.n = 0

    def op(self, eng, fn, reads=(), writes=(), chan=None):
        is_dma = chan is not None
        if chan is None:
            chan = "E_" + eng
        o = Op(eng, fn, chan, 16 if is_dma else 1)
        o.idx = self.n; self.n += 1
        deps = []
        for b in reads:
            w = self.last_w.get(b)
            if w is not None:
                deps.append(w)
        for b in writes:
            w = self.last_w.get(b)
            if w is not None:
                deps.append(w)
            deps.extend(self.readers.get(b, ()))
        for d in deps:
            if d is o:
                continue
            if d.eng == eng and d.chan == chan:
                if eng == "tensor" or not SAME_ENGINE_SYNC:
                    continue
            cur = o.waits.get(d.chan)
            if cur is None or cur.idx < d.idx:
                o.waits[d.chan] = d
        for b in writes:
            self.last_w[b] = o
            self.readers[b] = []
        for b in reads:
            self.readers.setdefault(b, []).append(o)
        self.ops[eng].append(o)
        self.chan_ops.setdefault(chan, []).append(o)
        return o

    def emit(self, es):
        nc = self.nc
        for e in ENGS:
            for o in self.ops[e]:
                for d in o.waits.values():
                    d.signal = True
        for c, lst in self.chan_ops.items():
            lst[-1].signal = True
            if lst[0].inc == 16:
                for o in lst:
                    o.signal = True
        sems = {}
        finals = {}
        for c, lst in self.chan_ops.items():
            sems[c] = nc.alloc_semaphore(name="s_" + self.tag + "_" + c)
            v = 0
            for o in lst:
                if o.signal:
                    v += o.inc
                    o.val = v
            finals[c] = v
        block = es.enter_context(nc.Block())

        def run(engname):
            def body(eng):
                waited = {}
                for o in self.ops[engname]:
                    for c, d in o.waits.items():
                        if waited.get(c, 0) >= d.val:
                            continue
                        eng.wait_ge(sems[c], d.val)
                        waited[c] = d.val
                    ins = o.fn(eng)
                    if o.signal:
                        ins.then_inc(sems[o.chan], o.inc)
                for c, v in finals.items():
                    if v > 0 and waited.get(c, 0) < v:
                        eng.wait_ge(sems[c], v)
            return body
        block.sync(run("sync"))
        block.scalar(run("scalar"))
        block.vector(run("vector"))
        block.gpsimd(run("gpsimd"))
        block.tensor(run("tensor"))


class Ring:
    def __init__(self, name, tiles):
        self.name = name; self.tiles = tiles; self.i = -1

    def next(self):
        self.i = (self.i + 1) % len(self.tiles)
        return self.tiles[self.i], "%s%d" % (self.name, self.i)


class Phase:
    def __init__(self, nc, name):
        self.nc = nc; self.name = name
        self.es = contextlib.ExitStack()
        self.es.enter_context(nc.cleanup_on_exit())
        self.es.callback(nc.all_engine_barrier)
        self.S = Sched(nc, name)
        self.wtog = 0

    def sb(self, name, shape, dt):
        return self.es.enter_context(self.nc.sbuf_tensor(self.name + name, shape, dt))

    def ring(self, name, n, shape, dt):
        return Ring(name, [self.sb("%s_%d" % (name, i), shape, dt) for i in range(n)])

    def pring(self, name, n, shape, dt):
        return Ring(name, [self.es.enter_context(self.nc.psum_tensor("%s%s_%d" % (self.name, name, i), shape, dt))
                           for i in range(n)])

    def dma(self, out, in_, reads, writes, chan, q="sync", slow=False):
        if slow:
            f = lambda e: e.dma_start(out=out, in_=in_, allow_slow_non_contiguous=True)
        else:
            f = lambda e: e.dma_start(out=out, in_=in_)
        return self.S.op(q, f, reads, writes, chan=chan)

    def load(self, out, in_, key, slow=False):
        return self.dma(out, in_, (), [key], chan="L" + key, q="sync", slow=slow)

    def store(self, out, in_, key, dkey=None):
        return self.dma(out, in_, [key], [dkey] if dkey else (), chan="T" + key, q="gpsimd")

    def act(self, out, in_, func, reads, writes, scale=None, bias=None, accum=None):
        kw = {}
        if scale is not None:
            kw["scale"] = scale
        if bias is not None:
            kw["bias"] = bias
        if accum is not None:
            kw["accum_out"] = accum
        return self.S.op("scalar", lambda e: e.activation(out=out, in_=in_, func=func, **kw), reads, writes)

    def ts(self, out, in0, s1, s2, op0, op1, reads, writes, eng="vector"):
        return self.S.op(eng, lambda e: e.tensor_scalar(out=out, in0=in0, scalar1=s1, scalar2=s2, op0=op0, op1=op1),
                         reads, writes)

    def tt(self, out, in0, in1, op, reads, writes, eng="vector"):
        return self.S.op(eng, lambda e: e.tensor_tensor(out=out, in0=in0, in1=in1, op=op), reads, writes)

    def stt(self, out, in0, scalar, in1, op0, op1, reads, writes):
        return self.S.op("vector", lambda e: e.scalar_tensor_tensor(out=out, in0=in0, scalar=scalar, in1=in1,
                                                                    op0=op0, op1=op1), reads, writes)

    def cp(self, out, in_, reads, writes, eng="vector"):
        return self.S.op(eng, lambda e: e.tensor_copy(out=out, in_=in_), reads, writes)

    def recip(self, out, in_, reads, writes):
        return self.S.op("vector", lambda e: e.reciprocal(out=out, in_=in_), reads, writes)

    def memset(self, ap, val, writes, eng="vector"):
        return self.S.op(eng, lambda e: e.memset(ap, val), (), writes)

    def mm(self, out, pairs, reads, writes):
        n = len(pairs)
        for i, (l, r) in enumerate(pairs):
            self.S.op("tensor", lambda e, l=l, r=r, i=i: e.matmul(out, l, r, start=(i == 0), stop=(i == n - 1)),
                      reads, writes)

    def tr(self, out, in_, ident, reads, writes):
        return self.S.op("tensor", lambda e: e.transpose(out, in_, ident), reads, writes)

    def finish(self):
        self.S.emit(self.es)
        self.es.close()

    def wcast(self, out, in_, scol, const, reads, writes):
        if scol is None:
            return self.ts(out, in_, float(const), None, ALU.mult, ALU.bypass, reads, writes)
        return self.ts(out, in_, scol, float(const), ALU.mult, ALU.mult, reads, writes)


def r3(ap, pat, **kw):
    return ap.rearrange(pat, **kw)


def ffn_groups(S):
    ng = -(-S // 510)
    base, rem = divmod(S, ng)
    out = []
    t = 0
    for i in range(ng):
        n = base + (1 if i < rem else 0)
        out.append((t, n))
        t += n
    return out


def build(SEQS):
    nc = bass.Bass("TRN2", target_bir_lowering=False)
    NS = len(SEQS)
    SMAX = max(SEQS)

    def din(name, shape, dt=F32):
        return nc.dram_tensor(name, list(shape), dt, kind="ExternalInput").ap()

    import os
    DBG = os.environ.get("KDBG", "") != ""

    def dscr(name, shape, dt=BF16):
        return nc.dram_tensor(name, list(shape), dt, kind="ExternalOutput" if DBG else "Internal").ap()

    X = [din("x%d" % i, [S, D]) for i, S in enumerate(SEQS)]
    Y = [nc.dram_tensor("y%d" % i, [S, D], F32, kind="ExternalOutput").ap() for i, S in enumerate(SEQS)]
    g_mix = din("g_mix", [D]); w_in = din("w_in", [D, IN_W])
    dec_f = din("ret_decay_fwd", [4]); dec_b = din("ret_decay_bwd", [4])
    ret_gn_g = din("ret_gn_g", [1024]); w_ret_o = din("w_ret_o", [1024, D])
    g_cq = din("g_cq", [384]); w_uq = din("w_uq", [384, 1536])
    g_ckv = din("g_ckv", [256]); w_ukv = din("w_ukv", [256, 2048])
    g_qn = din("g_qn", [192]); g_kn = din("g_kn", [192])
    w_mla_o = din("w_mla_o", [1024, D]); w_out = din("w_out", [D, D])
    g_ffn = din("g_ffn", [D]); w_up = din("w_up", [D, 5632])
    conv_w = din("conv_w", [3, 5632]); conv_b = din("conv_b", [5632])
    w_down = din("w_down", [2816, D])
    cf_d = din("cf", [128, NCF]); cb_d = din("cb", [128, 256], BF16)
    cosr_d = din("cosr", [128, SMAX]); sinr_d = din("sinr", [128, SMAX])
    cosm_d = din("cosm", [64, SMAX]); sinm_d = din("sinm", [64, SMAX])

    SC = []
    for i, S in enumerate(SEQS):
        p = "s%d_" % i
        SC.append(dict(
            hT=dscr(p + "hT", [D, S]), qrT=dscr(p + "qrT", [512, S]), krT=dscr(p + "krT", [512, S]),
            ktok=dscr(p + "ktok", [S, 512]), vtok=dscr(p + "vtok", [S, 1024]), rgsT=dscr(p + "rgsT", [1024, S]),
            grT=dscr(p + "grT", [1024, S]), gaT=dscr(p + "gaT", [1024, S]),
            qmnT=dscr(p + "qmnT", [8, 128, S]), qmrT=dscr(p + "qmrT", [8, 64, S]),
            kmnT=dscr(p + "kmnT", [8, 128, S]), kmrT=dscr(p + "kmrT", [64, S]),
            vmtok=dscr(p + "vmtok", [S, 1024]), rstdk=dscr(p + "rstdk", [S, 8], F32),
            sb=dscr(p + "sb", [S // 128, 128, 1024]), m1T=dscr(p + "m1T", [1024, S]),
            attnT=dscr(p + "attnT", [1024, S]), h2T=dscr(p + "h2T", [D, S]),
        ))

    def consts(P, need_cf=True):
        cb = P.sb("cb", [128, 256], BF16)
        P.load(cb[:], cb_d[:, :], "cb")
        cf = None
        if need_cf:
            cf = P.sb("cf", [128, NCF], F32)
            P.load(cf[:], cf_d[:, :], "cf")
        return cf, cb

    def make_cols(P, cf, rows, psk, name="cols"):
        R = sum(max(s.shape[0] for _, s in segs) for segs in rows)
        vst = P.sb(name + "_st", [R, 128], F32)
        P.memset(vst[:], 0.0, [name + "_st"])
        r = 0
        k = 0
        for segs in rows:
            nr = max(s.shape[0] for _, s in segs)
            for c0, src in segs:
                P.dma(vst[r:r + src.shape[0], c0:c0 + src.shape[1]], src, [], [name + "_st"],
                      chan="L%s%d" % (name, k), q="sync")
                k += 1
            r += nr
        ps, pkey = psk
        P.mm(ps[:, 0:R], [(vst[0:R, :], cf[0:R, C_ID:C_ID + R])], [name + "_st", "cf"], [pkey])
        cols = P.sb(name, [128, R], F32)
        P.cp(cols[:], ps[:, 0:R], [pkey], [name])
        return cols

    def v2(ap, p=128):
        return ap.rearrange("(k p) -> k p", p=p)

    def v1(ap):
        return ap.rearrange("(o n) -> o n", o=1)

    def phase1a():
        P = Phase(nc, "a")
        cf, cb = consts(P)
        ident = cb[:, 0:128]
        ps_r = P.pring("ps", 6, [128, 512], F32)
        cols = make_cols(P, cf, [[(0, v2(g_mix))]], ps_r.next())
        W = P.sb("W", [128, 8, 4096], BF16)
        stg = P.ring("stg", 2, [128, 3072], F32)
        for k in range(8):
            st, sk = stg.next()
            P.load(st[:], w_in[k * 128:(k + 1) * 128, 0:3072], sk)
            g = cols[:, k:k + 1]
            wk = "W"
            s4 = st[:, 0:512].rearrange("p (h d) -> p h d", d=128)
            P.wcast(W[:, k, 0:512], st[:, 0:512], g, 1.0, [sk, "cols"], [wk])
            o4 = W[:, k, 512:1024].rearrange("p (h d) -> p h d", d=128)
            P.wcast(o4[:, :, 0:64], s4[:, :, 64:128], g, -1.0, [sk, "cols"], [wk])
            P.wcast(o4[:, :, 64:128], s4[:, :, 0:64], g, 1.0, [sk, "cols"], [wk])
            sc = 128.0 ** -0.5
            s4 = st[:, 512:1024].rearrange("p (h d) -> p h d", d=128)
            P.wcast(W[:, k, 1024:1536], st[:, 512:1024], g, sc, [sk, "cols"], [wk])
            o4 = W[:, k, 1536:2048].rearrange("p (h d) -> p h d", d=128)
            P.wcast(o4[:, :, 0:64], s4[:, :, 64:128], g, -sc, [sk, "cols"], [wk])
            P.wcast(o4[:, :, 64:128], s4[:, :, 0:64], g, sc, [sk, "cols"], [wk])
            P.wcast(W[:, k, 2048:4096], st[:, 1024:3072], g, 1.0, [sk, "cols"], [wk])

        xt_r = P.ring("xt", 4, [128, D], F32)
        junk = P.sb("junk", [128, D], BF16)
        st_r = P.ring("stat", 2, [128, 4], F32)
        h_r = P.ring("h", 2, [128, D], BF16)
        pT_r = P.pring("pT", 1, [128, 8, 128], BF16)
        hT_r = P.ring("hT", 2, [128, 8, 512], BF16)
        cs_r = P.ring("cs", 2, [128, 2, 512], F32)
        ktp_r = P.pring("ktp", 1, [128, 4, 128], BF16)
        t1_r = P.ring("t1", 2, [128, 512], F32)
        t2_r = P.ring("t2", 2, [128, 512], F32)
        qo_r = P.ring("qo", 2, [128, 4, 512], BF16)
        ko_r = P.ring("ko", 2, [128, 4, 512], BF16)
        kt_r = P.ring("kt", 2, [128, 4, 512], BF16)
        rg_r = P.ring("rg", 2, [128, 8, 512], BF16)
        vo_r = P.ring("vo", 2, [128, 4, 1024], BF16)

        glist = [(si, g) for si, S in enumerate(SEQS) for g in range(S // 512)]

        def norm_group(si, g):
            sc_ = SC[si]
            t0 = g * 512
            xs = []
            for j in range(4):
                xt, xk = xt_r.next()
                P.load(xt[:], X[si][t0 + j * 128:t0 + (j + 1) * 128, :], xk)
                xs.append((xt, xk))
            hT, hk = hT_r.next()
            for j in range(4):
                xt, xk = xs[j]
                st, stk = st_r.next()
                P.act(junk[:], xt[:], AF.Square, [xk], ["junk", stk + "a"], accum=st[:, 0:1])
                P.act(st[:, 1:2], st[:, 0:1], AF.Sqrt, [stk + "a"], [stk + "b"], scale=1.0 / D, bias=EPS)
                P.recip(st[:, 2:3], st[:, 1:2], [stk + "b"], [stk + "c"])
                h, hhk = h_r.next()
                P.act(h[:], xt[:], AF.Copy, [xk, stk + "c"], [hhk], scale=st[:, 2:3])
                pT, pk = pT_r.next()
                for k in range(8):
                    P.tr(pT[:, k, :], h[:, k * 128:(k + 1) * 128], ident, [hhk, "cb"], [pk])
                P.cp(hT[:, :, j * 128:(j + 1) * 128], pT[:], [pk], [hk])
            P.store(sc_["hT"].rearrange("(k p) s -> p k s", p=128)[:, :, t0:t0 + 512], hT[:], hk)
            return hT, hk

        nxt = norm_group(*glist[0])
        for gi, (si, g) in enumerate(glist):
            if True:
                sc_ = SC[si]
                t0 = g * 512
                hT, hk = nxt
                cs, ck = cs_r.next()
                P.load(cs[:, 0, :], cosr_d[:, t0:t0 + 512], ck + "c")
                P.load(cs[:, 1, :], sinr_d[:, t0:t0 + 512], ck + "s")
                def proj(col0):
                    ps, pk = ps_r.next()
                    P.mm(ps[:], [(W[:, k, col0:col0 + 128], hT[:, k, :]) for k in range(8)], [hk, "W"], [pk])
                    return ps, pk

                qo, qk = qo_r.next()
                ko, kk = ko_r.next()
                kt, ktk = kt_r.next()
                for (base, dst, dk) in ((0, qo, qk), (1024, ko, kk)):
                    for hh in range(4):
                        pa, pak = proj(base + hh * 128)
                        pb, pbk = proj(base + 512 + hh * 128)
                        t1, t1k = t1_r.next()
                        t2, t2k = t2_r.next()
                        P.tt(t1[:], pa[:], cs[:, 0, :], ALU.mult, [pak, ck + "c"], [t1k])
                        P.tt(t2[:], pb[:], cs[:, 1, :], ALU.mult, [pbk, ck + "s"], [t2k])
                        P.tt(dst[:, hh, :], t1[:], t2[:], ALU.add, [t1k, t2k], [dk + "h%d" % hh], eng="gpsimd")
                        if base == 1024:
                            ktp, ktpk = ktp_r.next()
                            for j in range(4):
                                P.tr(ktp[:, j, :], ko[:, hh, j * 128:(j + 1) * 128], ident, [dk + "h%d" % hh, "cb"], [ktpk])
                            P.cp(kt[:, :, hh * 128:(hh + 1) * 128], ktp[:], [ktpk], [ktk + "h%d" % hh])
                hkeys = lambda kk_: [kk_ + "h%d" % i for i in range(4)]
                P.dma(sc_["qrT"].rearrange("(h p) s -> p h s", p=128)[:, :, t0:t0 + 512], qo[:], hkeys(qk), [],
                      chan="T" + qk, q="gpsimd")
                P.dma(sc_["krT"].rearrange("(h p) s -> p h s", p=128)[:, :, t0:t0 + 512], ko[:], hkeys(kk), [],
                      chan="T" + kk, q="gpsimd")
                P.dma(sc_["ktok"][t0:t0 + 512, :].rearrange("(j p) c -> p j c", p=128), kt[:], hkeys(ktk), [],
                      chan="T" + ktk, q="gpsimd")
                if gi + 1 < len(glist):
                    nxt = norm_group(*glist[gi + 1])
                rg, rgk = rg_r.next()
                for c in range(8):
                    ps, pk = proj(3072 + c * 128)
                    P.act(rg[:, c, :], ps[:], AF.Silu, [pk], [rgk + "c%d" % c])
                P.dma(sc_["rgsT"].rearrange("(c p) s -> p c s", p=128)[:, :, t0:t0 + 512], rg[:],
                      [rgk + "c%d" % c for c in range(8)], [], chan="T" + rgk, q="gpsimd")
                vo, vk = vo_r.next()
                for j in range(4):
                    for n in range(2):
                        ps, pk = ps_r.next()
                        P.mm(ps[:], [(hT[:, k, j * 128:(j + 1) * 128], W[:, k, 2048 + n * 512:2048 + (n + 1) * 512])
                                     for k in range(8)], [hk, "W"], [pk])
                        P.act(vo[:, j, n * 512:(n + 1) * 512], ps[:], AF.Copy, [pk], [vk + "p%d" % (j * 2 + n)])
                P.dma(sc_["vtok"][t0:t0 + 512, :].rearrange("(j p) c -> p j c", p=128), vo[:],
                      [vk + "p%d" % i for i in range(8)], [], chan="T" + vk, q="gpsimd")
        P.finish()

    def phase1b():
        P = Phase(nc, "b")
        cf, cb = consts(P)
        ones = cb[:, 128:256]
        rows = [[(0, v2(g_mix))], [(0, v2(g_cq))], [(0, v2(g_ckv))],
                [(0, v1(g_qn[0:128]))], [(0, v1(g_kn[0:128]))],
                [(0, v1(g_qn[128:192]))], [(0, v1(g_qn[160:192])), (32, v1(g_qn[128:160]))],
                [(0, v1(g_kn[128:192]))], [(0, v1(g_kn[160:192])), (32, v1(g_kn[128:160]))]]
        ps_r = P.pring("ps", 6, [128, 512], F32)
        cols = make_cols(P, cf, rows, ps_r.next())
        GCQ, GCKV, GQN, GKN, GQR, GQRS, GKR, GKRS = 8, 11, 13, 14, 15, 16, 17, 18
        gx = P.sb("gx", [128, 4], F32)
        qs = 192.0 ** -0.5
        P.stt(gx[:, 0:1], cols[:, GQN:GQN + 1], qs, cols[:, GKN:GKN + 1], ALU.mult, ALU.mult, ["cols"], ["gx"])
        P.ts(gx[:, 1:3], cols[:, GQR:GQR + 2], qs, None, ALU.mult, ALU.bypass, ["cols"], ["gx"])
        W = P.sb("W", [128, 8, 2816], BF16)
        stg = P.ring("stg", 2, [128, 3072], F32)
        for k in range(8):
            st, sk = stg.next()
            P.load(st[:, 0:2752], w_in[k * 128:(k + 1) * 128, 3072:5824], sk)
            g = cols[:, k:k + 1]
            P.wcast(W[:, k, 0:704], st[:, 0:704], g, 1.0, [sk, "cols"], ["W"])
            P.wcast(W[:, k, 704:736], st[:, 672:704], g, -1.0, [sk, "cols"], ["W"])
            P.wcast(W[:, k, 736:768], st[:, 640:672], g, 1.0, [sk, "cols"], ["W"])
            P.wcast(W[:, k, 768:2816], st[:, 704:2752], g, 1.0, [sk, "cols"], ["W"])
        Wuq = P.sb("Wuq", [128, 3, 2112], BF16)
        P.memset(Wuq[:, :, 2048:2112], 0.0, ["Wuqz"])
        for k in range(3):
            st, sk = stg.next()
            P.load(st[:, 0:1536], w_uq[k * 128:(k + 1) * 128, :], sk)
            s3 = st[:, 0:1536].rearrange("p (h d) -> p h d", d=192)
            P.wcast(Wuq[:, k, 0:1024].rearrange("p (h d) -> p h d", d=128), s3[:, :, 0:128], None, 1.0, [sk], ["Wuq"])
            P.wcast(Wuq[:, k, 1024:1536].rearrange("p (h d) -> p h d", d=64), s3[:, :, 128:192], None, 1.0, [sk], ["Wuq"])
            o3 = Wuq[:, k, 1536:2048].rearrange("p (h d) -> p h d", d=64)
            P.wcast(o3[:, :, 0:32], s3[:, :, 160:192], None, -1.0, [sk], ["Wuq"])
            P.wcast(o3[:, :, 32:64], s3[:, :, 128:160], None, 1.0, [sk], ["Wuq"])
        Wuk = P.sb("Wuk", [128, 2, 1024], BF16)
        Wuv = P.sb("Wuv", [128, 2, 1024], BF16)
        for k in range(2):
            st, sk = stg.next()
            P.load(st[:, 0:2048], w_ukv[k * 128:(k + 1) * 128, :], sk)
            s3 = st[:, 0:2048].rearrange("p (h d) -> p h d", d=256)
            P.wcast(Wuk[:, k, :].rearrange("p (h d) -> p h d", d=128), s3[:, :, 0:128], None, 1.0, [sk], ["Wuk"])
            P.wcast(Wuv[:, k, :].rearrange("p (h d) -> p h d", d=128), s3[:, :, 128:256], None, 1.0, [sk], ["Wuv"])

        hT_r = P.ring("hT", 2, [128, 8, 512], BF16)
        cs_r = P.ring("cs", 2, [64, 2, 512], F32)
        pss_r = P.pring("pss", 1, [128, 512], F32)
        pst_r = P.pring("pst", 1, [128, 4, 8], F32)
        gr_r = P.ring("gr", 2, [128, 8, 512], BF16)
        ga_r = gr_r
        sq_r = P.ring("sq", 3, [128, 512], BF16)
        sqr_r = P.ring("sqr", 2, [128, 512], BF16)
        sqkr_r = P.ring("sqkr", 2, [128, 512], BF16)
        for r_ in (sqr_r, sqkr_r):
            for i_, t_ in enumerate(r_.tiles):
                P.memset(t_[64:128, :], 0.0, ["%sz%d" % (r_.name, i_)])
        ZK = ["sqrz0", "sqrz1", "sqkrz0", "sqkrz1"]
        sd_r = P.ring("sd", 2, [128, 512], F32)
        rs_r = P.ring("rs", 2, [128, 512], F32)
        cqn_r = P.ring("cqn", 2, [128, 3, 512], BF16)
        ckvn_r = P.ring("ckvn", 2, [128, 2, 512], BF16)
        t1_r = P.ring("t1", 2, [64, 512], F32)
        t2_r = P.ring("t2", 2, [64, 512], F32)
        t3_r = P.ring("t3", 1, [64, 512], F32)
        kro_r = P.ring("kro", 2, [64, 512], BF16)
        ka1 = P.sb("ka1", [64, 512], F32)
        ka2 = P.sb("ka2", [64, 512], F32)
        qno_r = P.ring("qno", 1, [128, 8, 512], BF16)
        qro_r = P.ring("qro", 1, [64, 8, 512], BF16)
        kno_r = P.ring("kno", 1, [128, 8, 512], BF16)
        sdk_r = P.ring("sdk", 2, [128, 32], F32)
        rk_r = P.ring("rk", 2, [128, 4, 8], F32)
        vmo_r = P.ring("vmo", 1, [128, 4, 1024], BF16)

        for si, S in enumerate(SEQS):
            sc_ = SC[si]
            for g in range(S // 512):
                t0 = g * 512
                hT, hk = hT_r.next()
                P.load(hT[:], sc_["hT"].rearrange("(k p) s -> p k s", p=128)[:, :, t0:t0 + 512], hk)
                cs, ck = cs_r.next()
                P.load(cs[:, 0, :], cosm_d[:, t0:t0 + 512], ck + "c")
                P.load(cs[:, 1, :], sinm_d[:, t0:t0 + 512], ck + "s")

                def proj(col0, m=128):
                    ps, pk = ps_r.next()
                    P.mm(ps[0:m, :], [(W[:, k, col0:col0 + m], hT[:, k, :]) for k in range(8)], [hk, "W"], [pk])
                    return ps, pk
                for (base, ring, dst) in ((768, gr_r, "grT"), (1792, ga_r, "gaT")):
                    gt, gk = ring.next()
                    for c in range(8):
                        ps, pk = proj(base + c * 128)
                        P.act(gt[:, c, :], ps[:], AF.Sigmoid, [pk], [gk + "c%d" % c])
                    P.dma(sc_[dst].rearrange("(c p) s -> p c s", p=128)[:, :, t0:t0 + 512], gt[:],
                          [gk + "c%d" % c for c in range(8)], [], chan="T" + gk, q="gpsimd")

                def latent(col0, nch, gcol0, ring, inv_n):
                    pcs = []
                    pss, pssk = pss_r.next()
                    sqs = []
                    for c in range(nch):
                        ps, pk = proj(col0 + c * 128)
                        sq, sqk = sq_r.next()
                        P.act(sq[:], ps[:], AF.Square, [pk], [sqk])
                        pcs.append((ps, pk)); sqs.append((sq, sqk))
                    P.mm(pss[:], [(ones, sq[:]) for sq, _ in sqs], [k_ for _, k_ in sqs] + ["cb"], [pssk])
                    sd, sdk_ = sd_r.next()
                    P.act(sd[:], pss[:], AF.Sqrt, [pssk], [sdk_], scale=inv_n, bias=EPS)
                    rs, rsk = rs_r.next()
                    P.recip(rs[:], sd[:], [sdk_], [rsk])
                    o, ok = ring.next()
                    for c in range(nch):
                        ps, pk = pcs[c]
                        P.stt(o[:, c, :], ps[:], cols[:, gcol0 + c:gcol0 + c + 1], rs[:], ALU.mult, ALU.mult,
                              [pk, rsk, "cols"], [ok + "c%d" % c])
                    return o, [ok + "c%d" % c for c in range(nch)]
                cqn, cqk = latent(0, 3, GCQ, cqn_r, 1.0 / 384)
                ckvn, ckvk = latent(384, 2, GCKV, ckvn_r, 1.0 / 256)

                pkr, pkrk = proj(640, 128)
                pkrr, pkrrk = proj(704, 128)
                sqkr, sqkrk = sqkr_r.next()
                P.act(sqkr[0:64, :], pkr[0:64, :], AF.Square, [pkrk], [sqkrk])
                u1, u1k = t1_r.next(); u2, u2k = t2_r.next()
                P.tt(u1[:], pkr[0:64, :], cs[:, 0, :], ALU.mult, [pkrk, ck + "c"], [u1k])
                P.tt(u2[:], pkrr[0:64, :], cs[:, 1, :], ALU.mult, [pkrrk, ck + "s"], [u2k])
                P.tt(ka1[:], u1[:], u2[:], ALU.add, [u1k, u2k], ["ka1"], eng="gpsimd")
                u3, u3k = t1_r.next(); u4, u4k = t2_r.next()
                P.tt(u3[:], pkrr[0:64, :], cs[:, 0, :], ALU.mult, [pkrrk, ck + "c"], [u3k])
                P.tt(u4[:], pkr[0:64, :], cs[:, 1, :], ALU.mult, [pkrk, ck + "s"], [u4k])
                P.tt(ka2[:], u3[:], u4[:], ALU.subtract, [u3k, u4k], ["ka2"], eng="gpsimd")
                t1, t1k = t1_r.next(); t2, t2k = t2_r.next()
                P.stt(t1[:], ka1[:], cols[0:64, GKR:GKR + 1], cs[:, 0, :], ALU.mult, ALU.mult,
                      ["ka1", ck + "c", "cols"], [t1k])
                P.stt(t2[:], ka2[:], cols[0:64, GKRS:GKRS + 1], cs[:, 1, :], ALU.mult, ALU.mult,
                      ["ka2", ck + "s", "cols"], [t2k])
                kro, krok = kro_r.next()
                P.tt(kro[:], t1[:], t2[:], ALU.add, [t1k, t2k], [krok], eng="gpsimd")
                P.store(sc_["kmrT"][:, t0:t0 + 512], kro[:], krok)

                qno, qnk = qno_r.next()
                qro, qrk = qro_r.next()
                for hh in range(8):
                    psn, psnk = ps_r.next()
                    P.mm(psn[:], [(Wuq[:, k, hh * 128:(hh + 1) * 128], cqn[:, k, :]) for k in range(3)], cqk + ["Wuq"], [psnk])
                    psr, psrk = ps_r.next()
                    P.mm(psr[:, :], [(Wuq[:, k, 1024 + hh * 64:1024 + hh * 64 + 128], cqn[:, k, :]) for k in range(3)],
                         cqk + ["Wuq"], [psrk])
                    psrr, psrrk = ps_r.next()
                    P.mm(psrr[:, :], [(Wuq[:, k, 1536 + hh * 64:1536 + hh * 64 + 128], cqn[:, k, :]) for k in range(3)],
                         cqk + ["Wuq", "Wuqz"], [psrrk])
                    sq, sqk = sq_r.next()
                    P.act(sq[:], psn[:], AF.Square, [psnk], [sqk])
                    sqr, sqrk = sqr_r.next()
                    P.act(sqr[0:64, :], psr[0:64, :], AF.Square, [psrk], [sqrk])
                    pss, pssk = pss_r.next()
                    P.mm(pss[:], [(ones, sq[:]), (ones, sqr[:])], [sqk, sqrk, "cb"] + ZK, [pssk])
                    sd, sdk_ = sd_r.next()
                    P.act(sd[:], pss[:], AF.Sqrt, [pssk], [sdk_], scale=1.0 / 192, bias=EPS)
                    rs, rsk = rs_r.next()
                    P.recip(rs[:], sd[:], [sdk_], [rsk])
                    P.stt(qno[:, hh, :], psn[:], gx[:, 0:1], rs[:], ALU.mult, ALU.mult, [psnk, rsk, "gx"], [qnk + "h%d" % hh])
                    t1, t1k = t1_r.next(); t2, t2k = t2_r.next(); t3, t3k = t3_r.next()
                    P.stt(t1[:], psr[0:64, :], gx[0:64, 1:2], cs[:, 0, :], ALU.mult, ALU.mult, [psrk, ck + "c", "gx"], [t1k])
                    P.stt(t2[:], psrr[0:64, :], gx[0:64, 2:3], cs[:, 1, :], ALU.mult, ALU.mult, [psrrk, ck + "s", "gx"], [t2k])
                    P.tt(t3[:], t1[:], t2[:], ALU.add, [t1k, t2k], [t3k], eng="gpsimd")
                    P.tt(qro[:, hh, :], t3[:], rs[0:64, :], ALU.mult, [t3k, rsk], [qrk + "h%d" % hh], eng="gpsimd")
                P.dma(sc_["qmnT"].rearrange("h p s -> p h s")[:, :, t0:t0 + 512], qno[:],
                      [qnk + "h%d" % i for i in range(8)], [], chan="T" + qnk, q="gpsimd")
                P.dma(sc_["qmrT"].rearrange("h p s -> p h s")[:, :, t0:t0 + 512], qro[:],
                      [qrk + "h%d" % i for i in range(8)], [], chan="T" + qrk, q="gpsimd")

                kno, knk = kno_r.next()
                pst, pstk = pst_r.next()
                pend_ks = []
                for hh in range(8):
                    ps, pk = ps_r.next()
                    P.mm(ps[:], [(Wuk[:, k, hh * 128:(hh + 1) * 128], ckvn[:, k, :]) for k in range(2)], ckvk + ["Wuk"], [pk])
                    P.act(kno[:, hh, :], ps[:], AF.Copy, [pk], [knk + "h%d" % hh])
                    sq, sqk = sq_r.next()
                    P.act(sq[:], ps[:], AF.Square, [pk], [sqk])

                    def kstat(hh=hh, sq=sq, sqk=sqk):
                        for j in range(4):
                            P.mm(pst[:, j, hh:hh + 1], [(sq[:, j * 128:(j + 1) * 128], cb[:, 128:129]),
                                                        (sqkr[:, j * 128:(j + 1) * 128], cb[:, 128:129])],
                                 [sqk, sqkrk, "cb"] + ZK, [pstk])
                    pend_ks.append(kstat)
                    if len(pend_ks) > 2:
                        pend_ks.pop(0)()
                while pend_ks:
                    pend_ks.pop(0)()
                P.dma(sc_["kmnT"].rearrange("h p s -> p h s")[:, :, t0:t0 + 512], kno[:],
                      [knk + "h%d" % i for i in range(8)], [], chan="T" + knk, q="gpsimd")
                sdk, sdkk = sdk_r.next()
                P.act(sdk[:], pst[:].rearrange("p j h -> p (j h)"), AF.Sqrt, [pstk], [sdkk], scale=1.0 / 192, bias=EPS)
                rk, rkk = rk_r.next()
                P.recip(rk[:].rearrange("p j h -> p (j h)"), sdk[:], [sdkk], [rkk])
                P.store(sc_["rstdk"][t0:t0 + 512, :].rearrange("(j p) h -> p j h", p=128), rk[:], rkk)

                vmo, vmk = vmo_r.next()
                for j in range(4):
                    for n in range(2):
                        ps, pk = ps_r.next()
                        P.mm(ps[:], [(ckvn[:, k, j * 128:(j + 1) * 128], Wuv[:, k, n * 512:(n + 1) * 512]) for k in range(2)],
                             ckvk + ["Wuv"], [pk])
                        P.act(vmo[:, j, n * 512:(n + 1) * 512], ps[:], AF.Copy, [pk], [vmk + "p%d" % (j * 2 + n)])
                P.dma(sc_["vmtok"][t0:t0 + 512, :].rearrange("(j p) c -> p j c", p=128), vmo[:],
                      [vmk + "p%d" % i for i in range(8)], [], chan="T" + vmk, q="gpsimd")
        P.finish()

    def phase2():
        P = Phase(nc, "r")
        cf, cb = consts(P)
        ident = cb[:, 0:128]
        psP_r = P.pring("psP", 1, [128, 512], F32)
        cols = make_cols(P, cf, [[(0, v2(ret_gn_g))]], psP_r.next())
        dst = P.sb("dst", [1, 8], F32)
        P.dma(dst[0:1, 0:4], v1(dec_f), [], ["dst"], chan="Ldst0")
        P.dma(dst[0:1, 4:8], v1(dec_b), [], ["dst"], chan="Ldst1")
        pdc, pdck = psP_r.next()
        P.mm(pdc[:, 0:8], [(cf[0:1, C_ONE:C_ONE + 128], dst[0:1, 0:8])], ["dst", "cf"], [pdck])
        lg = P.sb("lg", [128, 8], F32)
        P.act(lg[:], pdc[:, 0:8], AF.Exp, [pdck], ["lg"], scale=-1.0)
        P.ts(lg[:], lg[:], 1.0, None, ALU.add, ALU.bypass, ["lg"], ["lg"])
        P.act(lg[:], lg[:], AF.Ln, ["lg"], ["lg"])
        P.ts(lg[:], lg[:], -1.0, None, ALU.mult, ALU.bypass, ["lg"], ["lg"])
        DT = P.sb("DT", [128, 4, 128], F32)
        decf = P.sb("decf", [128, 4, 128], F32)
        decb = P.sb("decb", [128, 4, 128], F32)
        kcol = P.sb("kcol", [128, 16], F32)
        e1 = P.sb("e1", [128, 128], F32)
        e2 = P.sb("e2", [128, 128], F32)
        for hh in range(4):
            lf = lg[:, hh:hh + 1]; lb = lg[:, 4 + hh:5 + hh]
            P.act(e1[:], cf[:, C_A:C_A + 128], AF.Exp, ["cf", "lg"], ["e1"], scale=lf)
            P.tt(e1[:], e1[:], cf[:, C_MF:C_MF + 128], ALU.mult, ["e1", "cf"], ["e1"])
            P.act(e2[:], cf[:, C_B:C_B + 128], AF.Exp, ["cf", "lg"], ["e2"], scale=lb)
            P.tt(e2[:], e2[:], cf[:, C_MB:C_MB + 128], ALU.mult, ["e2", "cf"], ["e2"])
            P.tt(DT[:, hh, :], e1[:], e2[:], ALU.add, ["e1", "e2"], ["DT"])
            P.act(decf[:, hh, :], cf[:, C_C1:C_C1 + 128], AF.Exp, ["cf", "lg"], ["decf"], scale=lf)
            P.act(decb[:, hh, :], cf[:, C_C2:C_C2 + 128], AF.Exp, ["cf", "lg"], ["decb"], scale=lb)
            P.act(kcol[:, hh:hh + 1], cf[:, C_SM:C_SM + 1], AF.Exp, ["cf", "lg"], ["kcol"], scale=lf)
            P.act(kcol[:, 4 + hh:5 + hh], cf[:, C_SM + 1:C_SM + 2], AF.Exp, ["cf", "lg"], ["kcol"], scale=lb)
            P.act(kcol[:, 8 + hh:9 + hh], cf[:, C_SM + 2:C_SM + 3], AF.Exp, ["cf", "lg"], ["kcol"], scale=lf)
            P.act(kcol[:, 12 + hh:13 + hh], cf[:, C_SM + 2:C_SM + 3], AF.Exp, ["cf", "lg"], ["kcol"], scale=lb)
        Wro = P.sb("Wro", [128, 8, 1024], BF16)
        stg = P.ring("stg", 2, [128, 1024], F32)
        for k in range(8):
            st, sk = stg.next()
            P.load(st[:], w_ret_o[k * 128:(k + 1) * 128, :], sk)
            P.wcast(Wro[:, k, :], st[:], cols[:, k:k + 1], 1.0, [sk, "cols"], ["Wro"])

        kt_r = P.ring("kt", 2, [128, 4, 512], BF16)
        v_r = P.ring("v", 2, [128, 4, 1024], BF16)
        kd_r = P.ring("kd", 2, [128, 4, 128], BF16)
        psU_r = P.pring("psU", 1, [128, 512], F32)
        St = P.sb("St", [128, 4, 256], F32)
        sbb_r = P.ring("sbb", 3, [128, 1024], BF16)

        kfb = {}
        for col0 in (0, 4):
            t_ = P.sb("kfb%d" % col0, [128, 4, 128], F32)
            for hh in range(4):
                P.act(t_[:, hh, :], cf[:, C_ONE:C_ONE + 128], AF.Copy, ["cf", "kcol"], ["kfb%d" % col0],
                      scale=kcol[:, col0 + hh:col0 + hh + 1])
            kfb[col0] = t_

        def kdec(kt, ktk, j, col0):
            kd, kdk = kd_r.next()
            P.tt(kd[:], kt[:, j, :].rearrange("p (h d) -> p h d", d=128), kfb[col0][:], ALU.mult,
                 [ktk, "kfb%d" % col0], [kdk], eng="gpsimd")
            return kd, [kdk]

        def state_update(kd, kdk, v, vk, j, gcol0):
            for hp in range(2):
                psU, psUk = psU_r.next()
                for h2 in range(2):
                    hh = hp * 2 + h2
                    P.mm(psU[:, h2 * 256:(h2 + 1) * 256], [(kd[:, hh, :], v[:, j, hh * 256:(hh + 1) * 256])], kdk + [vk], [psUk])
                for h2 in range(2):
                    hh = hp * 2 + h2
                    P.stt(St[:, hh, :], St[:, hh, :], kcol[:, gcol0 + hh:gcol0 + hh + 1], psU[:, h2 * 256:(h2 + 1) * 256],
                          ALU.mult, ALU.add, ["St%d" % hh, psUk, "kcol"], ["St%d" % hh])

        for si, S in enumerate(SEQS):
            sc_ = SC[si]
            NG = S // 512
            P.memset(St[:], 0.0, ["St%d" % i_ for i_ in range(4)])
            for g in range(NG - 1, -1, -1):
                t0 = g * 512
                kt, ktk = kt_r.next()
                P.load(kt[:], sc_["ktok"][t0:t0 + 512, :].rearrange("(j p) c -> p j c", p=128), ktk)
                v, vk = v_r.next()
                P.load(v[:], sc_["vtok"][t0:t0 + 512, :].rearrange("(j p) c -> p j c", p=128), vk)
                for j in range(3, -1, -1):
                    n = g * 4 + j
                    sbb, sbk = sbb_r.next()
                    P.cp(sbb[:], St[:].rearrange("p h d -> p (h d)"), ["St%d" % i_ for i_ in range(4)], [sbk])
                    P.store(sc_["sb"][n, :, :], sbb[:], sbk, dkey="sbd%d_%d" % (si, n))
                    if n > 0:
                        kd, kdk = kdec(kt, ktk, j, 4)
                        state_update(kd, kdk, v, vk, j, 12)
            P.memset(St[:], 0.0, ["St%d" % i_ for i_ in range(4)])
            fw = getattr(P, "_fw", None)
            if fw is None:
                fw = dict(
                    qT=P.ring("qT", 2, [128, 4, 512], BF16), kT=P.ring("kT", 2, [128, 4, 512], BF16),
                    sbl=P.ring("sbl", 2, [128, 4, 1024], BF16), rgs=P.ring("rgs", 2, [128, 8, 512], BF16),
                    gr=P.ring("gr", 2, [128, 8, 512], BF16),
                    psS=P.pring("psS", 1, [128, 4, 128], F32), psO=P.pring("psO", 2, [128, 1024], F32),
                    tp=P.pring("tp", 1, [128, 8, 128], BF16), psP=psP_r,
                    pT=P.ring("pT", 2, [128, 4, 128], BF16), qf=P.ring("qf", 2, [128, 4, 128], BF16),
                    qb=P.ring("qb", 2, [128, 4, 128], BF16), Sfb=P.ring("Sfb", 3, [128, 4, 256], BF16),
                    st=P.ring("st", 2, [128, 32], F32), retn=P.ring("retn", 2, [128, 1024], BF16),
                    retg=P.ring("retg", 2, [128, 8, 512], BF16), m1=P.ring("m1", 2, [128, 8, 512], BF16),
                    junk=P.sb("junk", [128, 256], BF16),
                )
                P._fw = fw
            Sfb, Sfk = fw["Sfb"].next()
            P.cp(Sfb[:], St[:], ["St%d" % i_ for i_ in range(4)], [Sfk])
            tails = []
            for g in range(NG):
                t0 = g * 512
                qT, qTk = fw["qT"].next()
                P.load(qT[:], sc_["qrT"].rearrange("(h p) s -> p h s", p=128)[:, :, t0:t0 + 512], qTk)
                kT, kTk = fw["kT"].next()
                P.load(kT[:], sc_["krT"].rearrange("(h p) s -> p h s", p=128)[:, :, t0:t0 + 512], kTk)
                kt, ktk = kt_r.next()
                P.load(kt[:], sc_["ktok"][t0:t0 + 512, :].rearrange("(j p) c -> p j c", p=128), ktk)
                v, vk = v_r.next()
                P.load(v[:], sc_["vtok"][t0:t0 + 512, :].rearrange("(j p) c -> p j c", p=128), vk)
                sbl, sblk = fw["sbl"].next()
                P.dma(sbl[:], sc_["sb"][g * 4:(g + 1) * 4, :, :].rearrange("j p c -> p j c"),
                      ["sbd%d_%d" % (si, g * 4 + j) for j in range(4)], [sblk], chan="L" + sblk)
                rgs, rgsk = fw["rgs"].next()
                P.load(rgs[:], sc_["rgsT"].rearrange("(c p) s -> p c s", p=128)[:, :, t0:t0 + 512], rgsk)
                gr, grk = fw["gr"].next()
                P.load(gr[:], sc_["grT"].rearrange("(c p) s -> p c s", p=128)[:, :, t0:t0 + 512], grk)
                retg, retgk = fw["retg"].next()
                for j in range(4):
                    sl = slice(j * 128, (j + 1) * 128)
                    psS, psSk = fw["psS"].next()
                    for hh in range(4):
                        P.mm(psS[:, hh, :], [(kT[:, hh, sl], qT[:, hh, sl])], [kTk, qTk], [psSk])
                    pT, pTk = fw["pT"].next()
                    P.tt(pT[:], psS[:], DT[:], ALU.mult, [psSk, "DT"], [pTk])
                    qf, qfk = fw["qf"].next()
                    P.tt(qf[:], qT[:, :, sl], decf[:], ALU.mult, [qTk, "decf"], [qfk], eng="gpsimd")
                    qb, qbk = fw["qb"].next()
                    P.tt(qb[:], qT[:, :, sl], decb[:], ALU.mult, [qTk, "decb"], [qbk], eng="gpsimd")
                    Sfb_old, Sfk_old = Sfb, Sfk
                    kd, kdk = kdec(kt, ktk, j, 0)
                    state_update(kd, kdk, v, vk, j, 8)
                    Sfb, Sfk = fw["Sfb"].next()
                    P.cp(Sfb[:], St[:], ["St%d" % i_ for i_ in range(4)], [Sfk])
                    psO, psOk = fw["psO"].next()
                    for hh in range(4):
                        vs = v[:, j, hh * 256:(hh + 1) * 256]
                        P.mm(psO[:, hh * 256:(hh + 1) * 256],
                             [(pT[:, hh, :], vs), (qf[:, hh, :], Sfb_old[:, hh, :]),
                              (qb[:, hh, :], sbl[:, j, hh * 256:(hh + 1) * 256])],
                             [pTk, vk, qfk, Sfk_old, qbk, sblk], [psOk])
                    while tails:
                        tails.pop(0)()
                    st, stk = fw["st"].next()
                    for hh in range(4):
                        o = psO[:, hh * 256:(hh + 1) * 256]
                        P.act(fw["junk"][:], o, AF.Copy, [psOk], ["rjunk", stk + "a"], accum=st[:, hh:hh + 1])
                    for hh in range(4):
                        o = psO[:, hh * 256:(hh + 1) * 256]
                        P.act(fw["junk"][:], o, AF.Square, [psOk], ["rjunk", stk + "a2"], accum=st[:, 4 + hh:5 + hh])
                    P.ts(st[:, 8:12], st[:, 0:4], 1.0 / 256, None, ALU.mult, ALU.bypass, [stk + "a"], [stk + "b"])
                    P.tt(st[:, 12:16], st[:, 8:12], st[:, 8:12], ALU.mult, [stk + "b"], [stk + "c"])
                    P.stt(st[:, 16:20], st[:, 4:8], 1.0 / 256, st[:, 12:16], ALU.mult, ALU.subtract, [stk + "a2", stk + "c"], [stk + "d"])
                    P.act(st[:, 20:24], st[:, 16:20], AF.Sqrt, [stk + "d"], [stk + "e"], bias=EPS)
                    P.recip(st[:, 24:28], st[:, 20:24], [stk + "e"], [stk + "f"])
                    P.stt(st[:, 28:32], st[:, 8:12], -1.0, st[:, 24:28], ALU.mult, ALU.mult, [stk + "b", stk + "f"], [stk + "g"])
                    retn, retnk = fw["retn"].next()
                    for hh in range(4):
                        P.act(retn[:, hh * 256:(hh + 1) * 256], psO[:, hh * 256:(hh + 1) * 256], AF.Identity,
                              [psOk, stk + "f", stk + "g"], [retnk], scale=st[:, 24 + hh:25 + hh], bias=st[:, 28 + hh:29 + hh])

                    def tail(j=j, sl=sl, retn=retn, retnk=retnk, retg=retg, retgk=retgk, rgs=rgs, rgsk=rgsk,
                             gr=gr, grk=grk, t0=t0):
                        tp, tpk = fw["tp"].next()
                        for c in range(8):
                            P.tr(tp[:, c, :], retn[:, c * 128:(c + 1) * 128], ident, [retnk, "cb"], [tpk])
                        P.tt(retg[:, :, sl], tp[:], rgs[:, :, sl], ALU.mult, [tpk, rgsk], [retgk + "j%d" % j])
                        if j == 3:
                            m1, m1k = fw["m1"].next()
                            for c in range(8):
                                psP, psPk = fw["psP"].next()
                                P.mm(psP[:], [(Wro[:, k, c * 128:(c + 1) * 128], retg[:, k, :]) for k in range(8)],
                                     [retgk + "j%d" % jj for jj in range(4)] + ["Wro"], [psPk])
                                P.tt(m1[:, c, :], psP[:], gr[:, c, :], ALU.mult, [psPk, grk], [m1k + "c%d" % c])
                            P.dma(sc_["m1T"].rearrange("(c p) s -> p c s", p=128)[:, :, t0:t0 + 512], m1[:],
                                  [m1k + "c%d" % c for c in range(8)], [], chan="T" + m1k, q="gpsimd")
                    tails.append(tail)
            while tails:
                tails.pop(0)()
        P.finish()

    def phase3():
        P = Phase(nc, "m")
        cf, cb = consts(P)
        ones = cb[:, 128:256]
        onesf = cf[:, C_ONE:C_ONE + 128]
        Kn_r = P.ring("Kn", 2, [128, SMAX], BF16)
        V_r = P.ring("V", 2, [128, SMAX // 128, 128], BF16)
        Kr = P.sb("Kr", [128, SMAX], BF16)
        P.memset(Kr[64:128, :], 0.0, ["Krz"])
        rk = P.sb("rk", [128, SMAX // 128, 8], F32)
        Qn_r = P.ring("Qn", 3, [128, 512], BF16)
        Qr_r = P.ring("Qr", 3, [128, 512], BF16)
        for i_, t_ in enumerate(Qr_r.tiles):
            P.memset(t_[64:128, :], 0.0, ["Qrz%d" % i_])
        st_r = P.pring("st", 4, [128, 512], F32)
        o_r = P.pring("o", 2, [128, 512], F32)
        l_r = P.pring("l", 2, [128, 512], F32)
        p_r = P.ring("p", 6, [128, 512], BF16)
        acc_r = [P.ring("acc0", 2, [128, 512], F32), P.ring("acc1", 2, [128, 512], F32)]
        accs_r = P.ring("accs", 2, [128, 512], F32)
        rl_r = P.ring("rl", 2, [128, 512], F32)
        at_r = P.ring("at", 3, [128, 512], BF16)
        LAG = 2
        units = [(si, hh, qg, kt) for si, S in enumerate(SEQS) for hh in range(8)
                 for qg in range(S // 512) for kt in range(S // 128)]
        cur = {}
        pend = []

        def stage_a(u):
            si, hh, qg, kt = u
            S = SEQS[si]; sc_ = SC[si]; NK = S // 128
            if hh == 0 and qg == 0 and kt == 0:
                P.load(Kr[0:64, 0:S], sc_["kmrT"][:, :], "Kr")
                P.load(rk[:, 0:NK, :], sc_["rstdk"].rearrange("(t p) h -> p t h", p=128), "rk")
            if qg == 0 and kt == 0:
                Kn, Knk = Kn_r.next()
                P.load(Kn[:, 0:S], sc_["kmnT"][hh, :, :], Knk)
                V, Vk = V_r.next()
                P.load(V[:, 0:NK, :], sc_["vmtok"][:, hh * 128:(hh + 1) * 128].rearrange("(t p) c -> p t c", p=128), Vk)
                cur["K"] = (Kn, Knk, V, Vk)
            if kt == 0:
                q0 = qg * 512
                Qn, Qnk = Qn_r.next()
                P.load(Qn[:], sc_["qmnT"][hh, :, q0:q0 + 512], Qnk)
                Qr, Qrk = Qr_r.next()
                P.load(Qr[0:64, :], sc_["qmrT"][hh, :, q0:q0 + 512], Qrk)
                cur["Q"] = (Qn, Qnk, Qr, Qrk)
            Kn, Knk, V, Vk = cur["K"]
            Qn, Qnk, Qr, Qrk = cur["Q"]
            ks = slice(kt * 128, (kt + 1) * 128)
            st, stk = st_r.next()
            P.mm(st[:], [(Kn[:, ks], Qn[:]), (Kr[:, ks], Qr[:])], [Knk, "Kr", "Krz", Qnk, Qrk] + ["Qrz%d" % i_ for i_ in range(3)], [stk])
            p, pk = p_r.next()
            P.act(p[:], st[:], AF.Exp, [stk, "rk"], [pk], scale=rk[:, kt, hh:hh + 1])
            pend.append((u, p, pk, V, Vk))

        def stage_b():
            u, p, pk, V, Vk = pend.pop(0)
            si, hh, qg, kt = u
            S = SEQS[si]; sc_ = SC[si]; NK = S // 128
            if kt == 0:
                cur["o"] = o_r.next()
                cur["l"] = l_r.next()
                cur["acc"] = [acc_r[0].next(), acc_r[1].next()]
                cur["na"] = 0
            o, ok = cur["o"]
            l, lk = cur["l"]
            P.S.op("tensor", lambda e, o=o, V=V, kt=kt, p=p, NK=NK: e.matmul(
                o[:], V[:, kt, :], p[:], start=(kt == 0), stop=(kt == NK - 1)), [Vk, pk], [ok])
            if kt % 4 == 3:
                P.S.op("tensor", lambda e, l=l, p=p, kt=kt: e.matmul(
                    l[:], ones, p[:], start=(kt == 3), stop=False), [pk, "cb"], [lk])
            else:
                na = cur["na"]; cur["na"] = na + 1
                a, ak = cur["acc"][na % 2]
                if na < 2:
                    P.cp(a[:], p[:], [pk], [ak])
                else:
                    P.tt(a[:], a[:], p[:], ALU.add, [ak, pk], [ak])
            if kt == NK - 1:
                q0 = qg * 512
                (a0, a0k), (a1, a1k) = cur["acc"]
                asum, asumk = accs_r.next()
                P.tt(asum[:], a0[:], a1[:], ALU.add, [a0k, a1k], [asumk])
                P.S.op("tensor", lambda e, l=l, asum=asum: e.matmul(
                    l[:], onesf, asum[:], start=False, stop=True), [asumk, "cf"], [lk])
                rl, rlk = rl_r.next()
                P.recip(rl[:], l[:], [lk], [rlk])
                at, atk = at_r.next()
                P.tt(at[:], o[:], rl[:], ALU.mult, [ok, rlk], [atk])
                P.store(sc_["attnT"][hh * 128:(hh + 1) * 128, q0:q0 + 512], at[:], atk)

        for i in range(len(units) + LAG):
            if i < len(units):
                stage_a(units[i])
            if i >= LAG:
                stage_b()
        P.finish()

    def phase4():
        P = Phase(nc, "o")
        cf, cb = consts(P, need_cf=False)
        ident = cb[:, 0:128]
        Wmo = P.sb("Wmo", [128, 8, 1024], BF16)
        Wo = P.sb("Wo", [128, 8, 1024], BF16)
        stg = P.ring("stg", 2, [128, 1024], F32)
        for (Wt, src, wk) in ((Wmo, w_mla_o, "Wmo"), (Wo, w_out, "Wo")):
            for k in range(8):
                st, sk = stg.next()
                P.load(st[:], src[k * 128:(k + 1) * 128, :], sk)
                P.wcast(Wt[:, k, :], st[:], None, 1.0, [sk], [wk])
        at_r = P.ring("at", 2, [128, 8, 512], BF16)
        ga_r = P.ring("ga", 2, [128, 8, 512], BF16)
        m1_r = P.ring("m1", 2, [128, 8, 512], BF16)
        xt_r = P.ring("xt", 4, [128, D], F32)
        ps_r = P.pring("ps", 6, [128, 512], F32)
        pT_r = P.pring("pT", 2, [128, 8, 128], BF16)
        tm_r = P.ring("tm", 2, [128, 512], F32)
        mg_r = P.ring("mg", 2, [128, 8, 512], BF16)
        x1_r = P.ring("x1", 2, [128, D], F32)
        junk = P.sb("junk", [128, D], BF16)
        st_r = P.ring("stat", 2, [128, 4], F32)
        h_r = P.ring("h", 2, [128, D], BF16)
        hT_r = P.ring("hT", 2, [128, 8, 512], BF16)
        for si, S in enumerate(SEQS):
            sc_ = SC[si]
            h2v = sc_["h2T"].rearrange("(k p) s -> p k s", p=128)
            for g in range(S // 512):
                t0 = g * 512
                at, atk = at_r.next()
                P.load(at[:], sc_["attnT"].rearrange("(c p) s -> p c s", p=128)[:, :, t0:t0 + 512], atk)
                ga, gak = ga_r.next()
                P.load(ga[:], sc_["gaT"].rearrange("(c p) s -> p c s", p=128)[:, :, t0:t0 + 512], gak)
                m1, m1k = m1_r.next()
                P.load(m1[:], sc_["m1T"].rearrange("(c p) s -> p c s", p=128)[:, :, t0:t0 + 512], m1k)
                xs = []
                for j in range(4):
                    xt, xk = xt_r.next()
                    P.load(xt[:], X[si][t0 + j * 128:t0 + (j + 1) * 128, :], xk)
                    xs.append((xt, xk))
                mg, mgk = mg_r.next()
                for c in range(8):
                    ps, pk = ps_r.next()
                    P.mm(ps[:], [(Wmo[:, k, c * 128:(c + 1) * 128], at[:, k, :]) for k in range(8)], [atk, "Wmo"], [pk])
                    tm, tmk = tm_r.next()
                    P.tt(tm[:], ps[:], ga[:, c, :], ALU.mult, [pk, gak], [tmk])
                    P.tt(mg[:, c, :], tm[:], m1[:, c, :], ALU.add, [tmk, m1k], [mgk + "c%d" % c], eng="gpsimd")
                mgks = [mgk + "c%d" % c for c in range(8)]
                hT, hk = hT_r.next()
                pend_tr = []
                for j in range(4):
                    xt, xk = xs[j]
                    x1, x1k = x1_r.next()
                    for n in range(2):
                        ps, pk = ps_r.next()
                        P.mm(ps[:], [(mg[:, k, j * 128:(j + 1) * 128], Wo[:, k, n * 512:(n + 1) * 512]) for k in range(8)],
                             mgks + ["Wo"], [pk])
                        P.tt(x1[:, n * 512:(n + 1) * 512], ps[:], xt[:, n * 512:(n + 1) * 512], ALU.add, [pk, xk], [x1k + "n%d" % n])
                    x1ks = [x1k + "n0", x1k + "n1"]
                    P.dma(Y[si][t0 + j * 128:t0 + (j + 1) * 128, :], x1[:], x1ks, [], chan="T" + x1k, q="gpsimd")
                    st, stk = st_r.next()
                    P.act(junk[:], x1[:], AF.Square, x1ks, ["junk", stk + "a"], accum=st[:, 0:1])
                    P.act(st[:, 1:2], st[:, 0:1], AF.Sqrt, [stk + "a"], [stk + "b"], scale=1.0 / D, bias=EPS)
                    P.recip(st[:, 2:3], st[:, 1:2], [stk + "b"], [stk + "c"])
                    h, hhk = h_r.next()
                    P.act(h[:], x1[:], AF.Copy, x1ks + [stk + "c"], [hhk], scale=st[:, 2:3])

                    def tr_tail(j=j, h=h, hhk=hhk, hT=hT, hk=hk):
                        pT, pk = pT_r.next()
                        for k in range(8):
                            P.tr(pT[:, k, :], h[:, k * 128:(k + 1) * 128], ident, [hhk, "cb"], [pk])
                        P.cp(hT[:, :, j * 128:(j + 1) * 128], pT[:], [pk], [hk])
                    if pend_tr:
                        pend_tr.pop(0)()
                    pend_tr.append(tr_tail)
                while pend_tr:
                    pend_tr.pop(0)()
                P.store(h2v[:, :, t0:t0 + 512], hT[:], hk)
        P.finish()

    def phase5():
        P = Phase(nc, "f")
        cf, cb = consts(P)
        pu_r = P.pring("pu", 4, [128, 512], F32)
        cols = make_cols(P, cf, [[(0, v2(g_ffn))]], pu_r.next())
        cwp = P.es.enter_context(nc.psum_tensor("f_cwp", [128, 44, 4], F32))
        stg = P.ring("stg", 2, [128, 1408], F32)
        for n in range(4):
            st, sk = stg.next()
            P.dma(st[0:3, :], conv_w[:, n * 1408:(n + 1) * 1408], [], [sk], chan="Lcw0" + sk)
            P.dma(st[3:4, :], v1(conv_b)[:, n * 1408:(n + 1) * 1408], [], [sk], chan="Lcw1" + sk)
            for c in range(11):
                P.mm(cwp[:, n * 11 + c, :], [(st[0:4, c * 128:(c + 1) * 128], cf[0:4, C_ID:C_ID + 4])], [sk, "cf"], ["cwp"])
        cw = P.sb("cw", [128, 44, 4], F32)
        P.cp(cw[:], cwp[:], ["cwp"], ["cw"])
        Wup = P.sb("Wup", [128, 8, 5632], BF16)
        Wd = P.sb("Wd", [128, 22, 1024], BF16)
        for k in range(8):
            for n in range(4):
                st, sk = stg.next()
                P.load(st[:], w_up[k * 128:(k + 1) * 128, n * 1408:(n + 1) * 1408], sk)
                P.wcast(Wup[:, k, n * 1408:(n + 1) * 1408], st[:], cols[:, k:k + 1], 1.0, [sk, "cols"], ["Wup"])
        for k in range(22):
            st, sk = stg.next()
            P.load(st[:, 0:1024], w_down[k * 128:(k + 1) * 128, :], sk)
            P.wcast(Wd[:, k, :], st[:, 0:1024], None, 1.0, [sk], ["Wd"])
        hT_r = P.ring("hT", 1, [128, 8, 512], BF16)
        pd_r = P.pring("pd", 3, [128, 512], F32)
        ta_r = P.ring("ta", 2, [128, 512], F32)
        tb_r = P.ring("tb", 2, [128, 512], F32)
        sa_r = P.ring("sa", 2, [128, 512], F32)
        act_r = P.ring("act", 1, [128, 22, 512], BF16)
        x1_r = P.ring("x1", 2, [128, D], F32)
        yo_r = P.ring("yo", 2, [128, D], F32)
        for si, S in enumerate(SEQS):
            sc_ = SC[si]
            h2v = sc_["h2T"].rearrange("(k p) s -> p k s", p=128)
            for (t0, n) in ffn_groups(S):
                hT, hk = hT_r.next()
                lo = max(t0 - 1, 0); hi = min(t0 + n + 1, S)
                P.load(hT[:, :, lo - (t0 - 1):hi - (t0 - 1)], h2v[:, :, lo:hi], hk)
                if t0 == 0:
                    P.memset(hT[:, :, 0:1], 0.0, [hk])
                if t0 + n == S:
                    P.memset(hT[:, :, n + 1:n + 2], 0.0, [hk])
                act, actk = act_r.next()
                for c in range(22):
                    def up(ch):
                        pu, puk = pu_r.next()
                        P.mm(pu[:, 0:n + 2], [(Wup[:, k, ch * 128:(ch + 1) * 128], hT[:, k, 0:n + 2]) for k in range(8)],
                             [hk, "Wup"], [puk])
                        return pu, puk

                    def conv(pu, puk, ch, ring):
                        t, tk = ring.next()
                        P.act(t[:, 0:n], pu[:, 1:n + 1], AF.Identity, [puk, "cw"], [tk], scale=cw[:, ch, 1:2], bias=cw[:, ch, 3:4])
                        P.stt(t[:, 0:n], pu[:, 0:n], cw[:, ch, 0:1], t[:, 0:n], ALU.mult, ALU.add, [puk, "cw", tk], [tk])
                        P.stt(t[:, 0:n], pu[:, 2:n + 2], cw[:, ch, 2:3], t[:, 0:n], ALU.mult, ALU.add, [puk, "cw", tk], [tk])
                        return t, tk
                    pa, pak = up(c)
                    pb, pbk = up(22 + c)
                    ta, tak = conv(pa, pak, c, ta_r)
                    tb, tbk = conv(pb, pbk, 22 + c, tb_r)
                    sa, sak = sa_r.next()
                    P.act(sa[:, 0:n], ta[:, 0:n], AF.Silu, [tak], [sak])
                    P.tt(act[:, c, 0:n], sa[:, 0:n], tb[:, 0:n], ALU.mult, [sak, tbk], [actk + "c%d" % c], eng="gpsimd")
                actks = [actk + "c%d" % c for c in range(22)]
                m0 = 0
                while m0 < n:
                    m = min(128, n - m0)
                    x1, x1k = x1_r.next()
                    P.load(x1[0:m, :], Y[si][t0 + m0:t0 + m0 + m, :], x1k)
                    yo, yok = yo_r.next()
                    for nn in range(2):
                        pd, pdk = pd_r.next()
                        P.mm(pd[0:m, :], [(act[:, k, m0:m0 + m], Wd[:, k, nn * 512:(nn + 1) * 512]) for k in range(22)],
                             actks + ["Wd"], [pdk])
                        P.tt(yo[0:m, nn * 512:(nn + 1) * 512], pd[0:m, :], x1[0:m, nn * 512:(nn + 1) * 512], ALU.add,
                             [pdk, x1k], [yok + "n%d" % nn])
                    P.dma(Y[si][t0 + m0:t0 + m0 + m, :], yo[0:m, :], [yok + "n0", yok + "n1"], [], chan="T" + yok, q="gpsimd")
                    m0 += m
        P.finish()

    import os
    nph = int(os.environ.get("KPH", "6"))
    for ph in (phase1a, phase1b, phase2, phase3, phase4, phase5)[:nph]:
        ph()
    return nc


def host_consts(SMAX):
    i = np.arange(128, dtype=np.float32)
    cf = np.zeros((128, NCF), np.float32)
    diff = i[None, :] - i[:, None]
    cf[:, C_A:C_A + 128] = np.maximum(diff, 0)
    cf[:, C_B:C_B + 128] = np.maximum(-diff, 0)
    cf[:, C_MF:C_MF + 128] = (diff >= 0)
    cf[:, C_MB:C_MB + 128] = (diff < 0)
    cf[:, C_C1:C_C1 + 128] = (i + 1.0)[None, :]
    cf[:, C_C2:C_C2 + 128] = (128.0 - i)[None, :]
    cf[:, C_ID:C_ID + 128] = np.eye(128, dtype=np.float32)
    cf[:, C_ONE:C_ONE + 128] = 1.0
    cf[:, C_SM] = 127.0 - i
    cf[:, C_SM + 1] = i
    cf[:, C_SM + 2] = 128.0
    cb = np.zeros((128, 256), np.float32)
    cb[:, 0:128] = np.eye(128)
    cb[:, 128:256] = 1.0
    cb = cb.astype(ml_dtypes.bfloat16)
    pos = np.arange(SMAX, dtype=np.float32)

    def tab(d):
        inv = (np.float32(10000.0) ** (-np.arange(0, d, 2, dtype=np.float32) / np.float32(d))).astype(np.float32)
        ang = (pos[:, None] * inv[None, :]).astype(np.float32)
        c = np.cos(ang).astype(np.float32).T
        s = np.sin(ang).astype(np.float32).T
        return (np.ascontiguousarray(np.concatenate([c, c], 0)), np.ascontiguousarray(np.concatenate([s, s], 0)))
    cosr, sinr = tab(128)
    cosm, sinm = tab(64)
    return dict(cf=cf, cb=cb, cosr=cosr, sinr=sinr, cosm=cosm, sinm=sinm)


_CACHE = {}


def run(x_list, weights, n_cores=8):
    SEQS = tuple(int(x.shape[1]) for x in x_list)
    if SEQS not in _CACHE:
        _CACHE[SEQS] = build(SEQS)
    nc = _CACHE[SEQS]
    hc = host_consts(max(SEQS))
    w = {}
    for k, v in weights.items():
        a = np.asarray(v, dtype=np.float32)
        w[k] = np.ascontiguousarray(a.reshape(a.shape[1:]))
    in_maps = []
    for c in range(n_cores):
        m = dict(w)
        m.update(hc)
        for i, x in enumerate(x_list):
            m["x%d" % i] = np.ascontiguousarray(np.asarray(x[c], dtype=np.float32))
        in_maps.append(m)
    res = run_bass_kernel_spmd(nc, in_maps, core_ids=list(range(n_cores)))
    global LAST_RES
    LAST_RES = res
    outs = []
    for i in range(len(x_list)):
        outs.append(np.stack([np.asarray(res.results[c]["y%d" % i], dtype=np.float32) for c in range(n_cores)], 0))
    return tuple(outs)


def kernel(x_prompt, x_sample, **weights):
    return run([np.asarray(x_prompt), np.asarray(x_sample)], weights)
```

```python
import contextlib
import math
import numpy as np
import ml_dtypes
import concourse.bass as bass
import concourse.mybir as mybir
from concourse.bass_utils import run_bass_kernel_spmd

F32 = mybir.dt.float32
BF16 = mybir.dt.bfloat16
AF = mybir.ActivationFunctionType
ALU = mybir.AluOpType

D = 1024
IN_W = 5824
EPS = 1e-6
SAME_ENGINE_SYNC = True
ENGS = ("sync", "scalar", "vector", "gpsimd", "tensor")

C_A, C_B, C_MF, C_MB, C_C1, C_C2, C_ID, C_ONE, C_SM = 0, 128, 256, 384, 512, 640, 768, 896, 1024
NCF = 1032


class Op:
    __slots__ = ("eng", "fn", "chan", "inc", "waits", "signal", "val", "idx")

    def __init__(self, eng, fn, chan, inc):
        self.eng = eng; self.fn = fn; self.chan = chan; self.inc = inc
        self.waits = {}; self.signal = False; self.val = None; self.idx = None


class Sched:
    def __init__(self, nc, tag=""):
        self.nc = nc
        self.tag = tag
        self.ops = {e: [] for e in ENGS}
        self.chan_ops = {}
        self.last_w = {}
        self.readers = {}
        self.n = 0

    def op(self, eng, fn, reads=(), writes=(), chan=None):
        is_dma = chan is not None
        if chan is None:
            chan = "E_" + eng
        o = Op(eng, fn, chan, 16 if is_dma else 1)
        o.idx = self.n; self.n += 1
        deps = []
        for b in reads:
            w = self.last_w.get(b)
            if w is not None:
                deps.append(w)
        for b in writes:
            w = self.last_w.get(b)
            if w is not None:
                deps.append(w)
            deps.extend(self.readers.get(b, ()))
        for d in deps:
            if d is o:
                continue
            if d.eng == eng and d.chan == chan:
                if eng == "tensor" or not SAME_ENGINE_SYNC:
                    continue
            cur = o.waits.get(d.chan)
            if cur is None or cur.idx < d.idx:
                o.waits[d.chan] = d
        for b in writes:
            self.last_w[b] = o
            self.readers[b] = []
        for b in reads:
            self.readers.setdefault(b, []).append(o)
        self.ops[eng].append(o)
        self.chan_ops.setdefault(chan, []).append(o)
        return o

    def emit(self, es):
        nc = self.nc
        for e in ENGS:
            for o in self.ops[e]:
                for d in o.waits.values():
                    d.signal = True
        for c, lst in self.chan_ops.items():
            lst[-1].signal = True
            if lst[0].inc == 16:
                for o in lst:
                    o.signal = True
        sems = {}
        finals = {}
        for c, lst in self.chan_ops.items():
            sems[c] = nc.alloc_semaphore(name="s_" + self.tag + "_" + c)
            v = 0
            for o in lst:
                if o.signal:
                    v += o.inc
                    o.val = v
            finals[c] = v
        block = es.enter_context(nc.Block())

        def run(engname):
            def body(eng):
                waited = {}
                for o in self.ops[engname]:
                    for c, d in o.waits.items():
                        if waited.get(c, 0) >= d.val:
                            continue
                        eng.wait_ge(sems[c], d.val)
                        waited[c] = d.val
                    ins = o.fn(eng)
                    if o.signal:
                        ins.then_inc(sems[o.chan], o.inc)
                for c, v in finals.items():
                    if v > 0 and waited.get(c, 0) < v:
                        eng.wait_ge(sems[c], v)
            return body
        block.sync(run("sync"))
        block.scalar(run("scalar"))
        block.vector(run("vector"))
        block.gpsimd(run("gpsimd"))
        block.tensor(run("tensor"))


class Ring:
    def __init__(self, name, tiles):
        self.name = name; self.tiles = tiles; self.i = -1

    def next(self):
        self.i = (self.i + 1) % len(self.tiles)
        return self.tiles[self.i], "%s%d" % (self.name, self.i)


class Phase:
    def __init__(self, nc, name):
        self.nc = nc; self.name = name
        self.es = contextlib.ExitStack()
        self.es.enter_context(nc.cleanup_on_exit())
        self.es.callback(nc.all_engine_barrier)
        self.S = Sched(nc, name)
        self.wtog = 0

    def sb(self, name, shape, dt):
        return self.es.enter_context(self.nc.sbuf_tensor(self.name + name, shape, dt))

    def ring(self, name, n, shape, dt):
        return Ring(name, [self.sb("%s_%d" % (name, i), shape, dt) for i in range(n)])

    def pring(self, name, n, shape, dt):
        return Ring(name, [self.es.enter_context(self.nc.psum_tensor("%s%s_%d" % (self.name, name, i), shape, dt))
                           for i in range(n)])

    def dma(self, out, in_, reads, writes, chan, q="sync", slow=False):
        if slow:
            f = lambda e: e.dma_start(out=out, in_=in_, allow_slow_non_contiguous=True)
        else:
            f = lambda e: e.dma_start(out=out, in_=in_)
        return self.S.op(q, f, reads, writes, chan=chan)

    def load(self, out, in_, key, slow=False):
        return self.dma(out, in_, (), [key], chan="L" + key, q="sync", slow=slow)

    def store(self, out, in_, key, dkey=None):
        return self.dma(out, in_, [key], [dkey] if dkey else (), chan="T" + key, q="gpsimd")

    def act(self, out, in_, func, reads, writes, scale=None, bias=None, accum=None):
        kw = {}
        if scale is not None:
            kw["scale"] = scale
        if bias is not None:
            kw["bias"] = bias
        if accum is not None:
            kw["accum_out"] = accum
        return self.S.op("scalar", lambda e: e.activation(out=out, in_=in_, func=func, **kw), reads, writes)

    def ts(self, out, in0, s1, s2, op0, op1, reads, writes, eng="vector"):
        return self.S.op(eng, lambda e: e.tensor_scalar(out=out, in0=in0, scalar1=s1, scalar2=s2, op0=op0, op1=op1),
                         reads, writes)

    def tt(self, out, in0, in1, op, reads, writes, eng="vector"):
        return self.S.op(eng, lambda e: e.tensor_tensor(out=out, in0=in0, in1=in1, op=op), reads, writes)

    def stt(self, out, in0, scalar, in1, op0, op1, reads, writes):
        return self.S.op("vector", lambda e: e.scalar_tensor_tensor(out=out, in0=in0, scalar=scalar, in1=in1,
                                                                    op0=op0, op1=op1), reads, writes)

    def cp(self, out, in_, reads, writes, eng="vector"):
        return self.S.op(eng, lambda e: e.tensor_copy(out=out, in_=in_), reads, writes)

    def recip(self, out, in_, reads, writes):
        return self.S.op("vector", lambda e: e.reciprocal(out=out, in_=in_), reads, writes)

    def memset(self, ap, val, writes, eng="vector"):
        return self.S.op(eng, lambda e: e.memset(ap, val), (), writes)

    def mm(self, out, pairs, reads, writes):
        n = len(pairs)
        for i, (l, r) in enumerate(pairs):
            self.S.op("tensor", lambda e, l=l, r=r, i=i: e.matmul(out, l, r, start=(i == 0), stop=(i == n - 1)),
                      reads, writes)

    def tr(self, out, in_, ident, reads, writes):
        return self.S.op("tensor", lambda e: e.transpose(out, in_, ident), reads, writes)

    def finish(self):
        self.S.emit(self.es)
        self.es.close()

    def wcast(self, out, in_, scol, const, reads, writes):
        if scol is None:
            return self.ts(out, in_, float(const), None, ALU.mult, ALU.bypass, reads, writes)
        return self.ts(out, in_, scol, float(const), ALU.mult, ALU.mult, reads, writes)


def r3(ap, pat, **kw):
    return ap.rearrange(pat, **kw)


def ffn_groups(S):
    ng = -(-S // 510)
    base, rem = divmod(S, ng)
    out = []
    t = 0
    for i in range(ng):
        n = base + (1 if i < rem else 0)
        out.append((t, n))
        t += n
    return out


def build(SEQS):
    nc = bass.Bass("TRN2", target_bir_lowering=False)
    NS = len(SEQS)
    SMAX = max(SEQS)

    def din(name, shape, dt=F32):
        return nc.dram_tensor(name, list(shape), dt, kind="ExternalInput").ap()

    import os
    DBG = os.environ.get("KDBG", "") != ""

    def dscr(name, shape, dt=BF16):
        return nc.dram_tensor(name, list(shape), dt, kind="ExternalOutput" if DBG else "Internal").ap()

    X = [din("x%d" % i, [S, D]) for i, S in enumerate(SEQS)]
    Y = [nc.dram_tensor("y%d" % i, [S, D], F32, kind="ExternalOutput").ap() for i, S in enumerate(SEQS)]
    g_mix = din("g_mix", [D]); w_in = din("w_in", [D, IN_W])
    dec_f = din("ret_decay_fwd", [4]); dec_b = din("ret_decay_bwd", [4])
    ret_gn_g = din("ret_gn_g", [1024]); w_ret_o = din("w_ret_o", [1024, D])
    g_cq = din("g_cq", [384]); w_uq = din("w_uq", [384, 1536])
    g_ckv = din("g_ckv", [256]); w_ukv = din("w_ukv", [256, 2048])
    g_qn = din("g_qn", [192]); g_kn = din("g_kn", [192])
    w_mla_o = din("w_mla_o", [1024, D]); w_out = din("w_out", [D, D])
    g_ffn = din("g_ffn", [D]); w_up = din("w_up", [D, 5632])
    conv_w = din("conv_w", [3, 5632]); conv_b = din("conv_b", [5632])
    w_down = din("w_down", [2816, D])
    cf_d = din("cf", [128, NCF]); cb_d = din("cb", [128, 256], BF16)
    cosr_d = din("cosr", [128, SMAX]); sinr_d = din("sinr", [128, SMAX])
    cosm_d = din("cosm", [64, SMAX]); sinm_d = din("sinm", [64, SMAX])

    SC = []
    for i, S in enumerate(SEQS):
        p = "s%d_" % i
        SC.append(dict(
            hT=dscr(p + "hT", [D, S]), qrT=dscr(p + "qrT", [512, S]), krT=dscr(p + "krT", [512, S]),
            ktok=dscr(p + "ktok", [S, 512]), vtok=dscr(p + "vtok", [S, 1024]), rgsT=dscr(p + "rgsT", [1024, S]),
            grT=dscr(p + "grT", [1024, S]), gaT=dscr(p + "gaT", [1024, S]),
            qmnT=dscr(p + "qmnT", [8, 128, S]), qmrT=dscr(p + "qmrT", [8, 64, S]),
            kmnT=dscr(p + "kmnT", [8, 128, S]), kmrT=dscr(p + "kmrT", [64, S]),
            vmtok=dscr(p + "vmtok", [S, 1024]), rstdk=dscr(p + "rstdk", [S, 8], F32),
            sb=dscr(p + "sb", [S // 128, 128, 1024]), m1T=dscr(p + "m1T", [1024, S]),
            attnT=dscr(p + "attnT", [1024, S]), h2T=dscr(p + "h2T", [D, S]),
        ))

    def consts(P, need_cf=True):
        cb = P.sb("cb", [128, 256], BF16)
        P.load(cb[:], cb_d[:, :], "cb")
        cf = None
        if need_cf:
            cf = P.sb("cf", [128, NCF], F32)
            P.load(cf[:], cf_d[:, :], "cf")
        return cf, cb

    def make_cols(P, cf, rows, psk, name="cols"):
        R = sum(max(s.shape[0] for _, s in segs) for segs in rows)
        vst = P.sb(name + "_st", [R, 128], F32)
        P.memset(vst[:], 0.0, [name + "_st"])
        r = 0
        k = 0
        for segs in rows:
            nr = max(s.shape[0] for _, s in segs)
            for c0, src in segs:
                P.dma(vst[r:r + src.shape[0], c0:c0 + src.shape[1]], src, [], [name + "_st"],
                      chan="L%s%d" % (name, k), q="sync")
                k += 1
            r += nr
        ps, pkey = psk
        P.mm(ps[:, 0:R], [(vst[0:R, :], cf[0:R, C_ID:C_ID + R])], [name + "_st", "cf"], [pkey])
        cols = P.sb(name, [128, R], F32)
        P.cp(cols[:], ps[:, 0:R], [pkey], [name])
        return cols

    def v2(ap, p=128):
        return ap.rearrange("(k p) -> k p", p=p)

    def v1(ap):
        return ap.rearrange("(o n) -> o n", o=1)

    def phase1a():
        P = Phase(nc, "a")
        cf, cb = consts(P)
        ident = cb[:, 0:128]
        ps_r = P.pring("ps", 6, [128, 512], F32)
        cols = make_cols(P, cf, [[(0, v2(g_mix))]], ps_r.next())
        W = P.sb("W", [128, 8, 4096], BF16)
        stg = P.ring("stg", 2, [128, 3072], F32)
        for k in range(8):
            st, sk = stg.next()
            P.load(st[:], w_in[k * 128:(k + 1) * 128, 0:3072], sk)
            g = cols[:, k:k + 1]
            wk = "W"
            s4 = st[:, 0:512].rearrange("p (h d) -> p h d", d=128)
            P.wcast(W[:, k, 0:512], st[:, 0:512], g, 1.0, [sk, "cols"], [wk])
            o4 = W[:, k, 512:1024].rearrange("p (h d) -> p h d", d=128)
            P.wcast(o4[:, :, 0:64], s4[:, :, 64:128], g, -1.0, [sk, "cols"], [wk])
            P.wcast(o4[:, :, 64:128], s4[:, :, 0:64], g, 1.0, [sk, "cols"], [wk])
            sc = 128.0 ** -0.5
            s4 = st[:, 512:1024].rearrange("p (h d) -> p h d", d=128)
            P.wcast(W[:, k, 1024:1536], st[:, 512:1024], g, sc, [sk, "cols"], [wk])
            o4 = W[:, k, 1536:2048].rearrange("p (h d) -> p h d", d=128)
            P.wcast(o4[:, :, 0:64], s4[:, :, 64:128], g, -sc, [sk, "cols"], [wk])
            P.wcast(o4[:, :, 64:128], s4[:, :, 0:64], g, sc, [sk, "cols"], [wk])
            P.wcast(W[:, k, 2048:4096], st[:, 1024:3072], g, 1.0, [sk, "cols"], [wk])

        xt_r = P.ring("xt", 4, [128, D], F32)
        junk = P.sb("junk", [128, D], BF16)
        st_r = P.ring("stat", 2, [128, 4], F32)
        h_r = P.ring("h", 2, [128, D], BF16)
        pT_r = P.pring("pT", 1, [128, 8, 128], BF16)
        hT_r = P.ring("hT", 2, [128, 8, 512], BF16)
        cs_r = P.ring("cs", 2, [128, 2, 512], F32)
        ktp_r = P.pring("ktp", 1, [128, 4, 128], BF16)
        t1_r = P.ring("t1", 2, [128, 512], F32)
        t2_r = P.ring("t2", 2, [128, 512], F32)
        qo_r = P.ring("qo", 2, [128, 4, 512], BF16)
        ko_r = P.ring("ko", 2, [128, 4, 512], BF16)
        kt_r = P.ring("kt", 2, [128, 4, 512], BF16)
        rg_r = P.ring("rg", 2, [128, 8, 512], BF16)
        vo_r = P.ring("vo", 2, [128, 4, 1024], BF16)

        glist = [(si, g) for si, S in enumerate(SEQS) for g in range(S // 512)]

        def norm_group(si, g):
            sc_ = SC[si]
            t0 = g * 512
            xs = []
            for j in range(4):
                xt, xk = xt_r.next()
                P.load(xt[:], X[si][t0 + j * 128:t0 + (j + 1) * 128, :], xk)
                xs.append((xt, xk))
            hT, hk = hT_r.next()
            for j in range(4):
                xt, xk = xs[j]
                st, stk = st_r.next()
                P.act(junk[:], xt[:], AF.Square, [xk], ["junk", stk + "a"], accum=st[:, 0:1])
                P.act(st[:, 1:2], st[:, 0:1], AF.Sqrt, [stk + "a"], [stk + "b"], scale=1.0 / D, bias=EPS)
                P.recip(st[:, 2:3], st[:, 1:2], [stk + "b"], [stk + "c"])
                h, hhk = h_r.next()
                P.act(h[:], xt[:], AF.Copy, [xk, stk + "c"], [hhk], scale=st[:, 2:3])
                pT, pk = pT_r.next()
                for k in range(8):
                    P.tr(pT[:, k, :], h[:, k * 128:(k + 1) * 128], ident, [hhk, "cb"], [pk])
                P.cp(hT[:, :, j * 128:(j + 1) * 128], pT[:], [pk], [hk])
            P.store(sc_["hT"].rearrange("(k p) s -> p k s", p=128)[:, :, t0:t0 + 512], hT[:], hk)
            return hT, hk

        nxt = norm_group(*glist[0])
        for gi, (si, g) in enumerate(glist):
            if True:
                sc_ = SC[si]
                t0 = g * 512
                hT, hk = nxt
                cs, ck = cs_r.next()
                P.load(cs[:, 0, :], cosr_d[:, t0:t0 + 512], ck + "c")
                P.load(cs[:, 1, :], sinr_d[:, t0:t0 + 512], ck + "s")
                def proj(col0):
                    ps, pk = ps_r.next()
                    P.mm(ps[:], [(W[:, k, col0:col0 + 128], hT[:, k, :]) for k in range(8)], [hk, "W"], [pk])
                    return ps, pk

                qo, qk = qo_r.next()
                ko, kk = ko_r.next()
                kt, ktk = kt_r.next()
                pend_kt = []
                for (base, dst, dk) in ((0, qo, qk), (1024, ko, kk)):
                    for hh in range(4):
                        pa, pak = proj(base + hh * 128)
                        pb, pbk = proj(base + 512 + hh * 128)
                        t1, t1k = t1_r.next()
                        t2, t2k = t2_r.next()
                        P.tt(t1[:], pa[:], cs[:, 0, :], ALU.mult, [pak, ck + "c"], [t1k])
                        P.tt(t2[:], pb[:], cs[:, 1, :], ALU.mult, [pbk, ck + "s"], [t2k])
                        P.tt(dst[:, hh, :], t1[:], t2[:], ALU.add, [t1k, t2k], [dk + "h%d" % hh], eng="gpsimd")
                        if base == 1024:
                            def ktail(hh=hh, dk=dk):
                                ktp, ktpk = ktp_r.next()
                                for j in range(4):
                                    P.tr(ktp[:, j, :], ko[:, hh, j * 128:(j + 1) * 128], ident, [dk + "h%d" % hh, "cb"], [ktpk])
                                P.cp(kt[:, :, hh * 128:(hh + 1) * 128], ktp[:], [ktpk], [ktk + "h%d" % hh])
                            if pend_kt:
                                pend_kt.pop(0)()
                            pend_kt.append(ktail)
                while pend_kt:
                    pend_kt.pop(0)()
                hkeys = lambda kk_: [kk_ + "h%d" % i for i in range(4)]
                P.dma(sc_["qrT"].rearrange("(h p) s -> p h s", p=128)[:, :, t0:t0 + 512], qo[:], hkeys(qk), [],
                      chan="T" + qk, q="gpsimd")
                P.dma(sc_["krT"].rearrange("(h p) s -> p h s", p=128)[:, :, t0:t0 + 512], ko[:], hkeys(kk), [],
                      chan="T" + kk, q="gpsimd")
                P.dma(sc_["ktok"][t0:t0 + 512, :].rearrange("(j p) c -> p j c", p=128), kt[:], hkeys(ktk), [],
                      chan="T" + ktk, q="gpsimd")
                if gi + 1 < len(glist):
                    nxt = norm_group(*glist[gi + 1])
                rg, rgk = rg_r.next()
                for c in range(8):
                    ps, pk = proj(3072 + c * 128)
                    P.act(rg[:, c, :], ps[:], AF.Silu, [pk], [rgk + "c%d" % c])
                P.dma(sc_["rgsT"].rearrange("(c p) s -> p c s", p=128)[:, :, t0:t0 + 512], rg[:],
                      [rgk + "c%d" % c for c in range(8)], [], chan="T" + rgk, q="gpsimd")
                vo, vk = vo_r.next()
                for j in range(4):
                    for n in range(2):
                        ps, pk = ps_r.next()
                        P.mm(ps[:], [(hT[:, k, j * 128:(j + 1) * 128], W[:, k, 2048 + n * 512:2048 + (n + 1) * 512])
                                     for k in range(8)], [hk, "W"], [pk])
                        P.act(vo[:, j, n * 512:(n + 1) * 512], ps[:], AF.Copy, [pk], [vk + "p%d" % (j * 2 + n)])
                P.dma(sc_["vtok"][t0:t0 + 512, :].rearrange("(j p) c -> p j c", p=128), vo[:],
                      [vk + "p%d" % i for i in range(8)], [], chan="T" + vk, q="gpsimd")
        P.finish()

    def phase1b():
        P = Phase(nc, "b")
        cf, cb = consts(P)
        ones = cb[:, 128:256]
        rows = [[(0, v2(g_mix))], [(0, v2(g_cq))], [(0, v2(g_ckv))],
                [(0, v1(g_qn[0:128]))], [(0, v1(g_kn[0:128]))],
                [(0, v1(g_qn[128:192]))], [(0, v1(g_qn[160:192])), (32, v1(g_qn[128:160]))],
                [(0, v1(g_kn[128:192]))], [(0, v1(g_kn[160:192])), (32, v1(g_kn[128:160]))]]
        ps_r = P.pring("ps", 6, [128, 512], F32)
        cols = make_cols(P, cf, rows, ps_r.next())
        GCQ, GCKV, GQN, GKN, GQR, GQRS, GKR, GKRS = 8, 11, 13, 14, 15, 16, 17, 18
        gx = P.sb("gx", [128, 4], F32)
        qs = 192.0 ** -0.5
        P.stt(gx[:, 0:1], cols[:, GQN:GQN + 1], qs, cols[:, GKN:GKN + 1], ALU.mult, ALU.mult, ["cols"], ["gx"])
        P.ts(gx[:, 1:3], cols[:, GQR:GQR + 2], qs, None, ALU.mult, ALU.bypass, ["cols"], ["gx"])
        W = P.sb("W", [128, 8, 2816], BF16)
        stg = P.ring("stg", 2, [128, 3072], F32)
        for k in range(8):
            st, sk = stg.next()
            P.load(st[:, 0:2752], w_in[k * 128:(k + 1) * 128, 3072:5824], sk)
            g = cols[:, k:k + 1]
            P.wcast(W[:, k, 0:704], st[:, 0:704], g, 1.0, [sk, "cols"], ["W"])
            P.wcast(W[:, k, 704:736], st[:, 672:704], g, -1.0, [sk, "cols"], ["W"])
            P.wcast(W[:, k, 736:768], st[:, 640:672], g, 1.0, [sk, "cols"], ["W"])
            P.wcast(W[:, k, 768:2816], st[:, 704:2752], g, 1.0, [sk, "cols"], ["W"])
        Wuq = P.sb("Wuq", [128, 3, 2112], BF16)
        P.memset(Wuq[:, :, 2048:2112], 0.0, ["Wuqz"])
        for k in range(3):
            st, sk = stg.next()
            P.load(st[:, 0:1536], w_uq[k * 128:(k + 1) * 128, :], sk)
            s3 = st[:, 0:1536].rearrange("p (h d) -> p h d", d=192)
            P.wcast(Wuq[:, k, 0:1024].rearrange("p (h d) -> p h d", d=128), s3[:, :, 0:128], None, 1.0, [sk], ["Wuq"])
            P.wcast(Wuq[:, k, 1024:1536].rearrange("p (h d) -> p h d", d=64), s3[:, :, 128:192], None, 1.0, [sk], ["Wuq"])
            o3 = Wuq[:, k, 1536:2048].rearrange("p (h d) -> p h d", d=64)
            P.wcast(o3[:, :, 0:32], s3[:, :, 160:192], None, -1.0, [sk], ["Wuq"])
            P.wcast(o3[:, :, 32:64], s3[:, :, 128:160], None, 1.0, [sk], ["Wuq"])
        Wuk = P.sb("Wuk", [128, 2, 1024], BF16)
        Wuv = P.sb("Wuv", [128, 2, 1024], BF16)
        for k in range(2):
            st, sk = stg.next()
            P.load(st[:, 0:2048], w_ukv[k * 128:(k + 1) * 128, :], sk)
            s3 = st[:, 0:2048].rearrange("p (h d) -> p h d", d=256)
            P.wcast(Wuk[:, k, :].rearrange("p (h d) -> p h d", d=128), s3[:, :, 0:128], None, 1.0, [sk], ["Wuk"])
            P.wcast(Wuv[:, k, :].rearrange("p (h d) -> p h d", d=128), s3[:, :, 128:256], None, 1.0, [sk], ["Wuv"])

        hT_r = P.ring("hT", 2, [128, 8, 512], BF16)
        cs_r = P.ring("cs", 2, [64, 2, 512], F32)
        pss_r = P.pring("pss", 1, [128, 512], F32)
        pst_r = P.pring("pst", 1, [128, 4, 8], F32)
        gr_r = P.ring("gr", 2, [128, 8, 512], BF16)
        ga_r = gr_r
        sq_r = P.ring("sq", 3, [128, 512], BF16)
        sqr_r = P.ring("sqr", 2, [128, 512], BF16)
        sqkr_r = P.ring("sqkr", 2, [128, 512], BF16)
        for r_ in (sqr_r, sqkr_r):
            for i_, t_ in enumerate(r_.tiles):
                P.memset(t_[64:128, :], 0.0, ["%sz%d" % (r_.name, i_)])
        ZK = ["sqrz0", "sqrz1", "sqkrz0", "sqkrz1"]
        sd_r = P.ring("sd", 2, [128, 512], F32)
        rs_r = P.ring("rs", 2, [128, 512], F32)
        cqn_r = P.ring("cqn", 2, [128, 3, 512], BF16)
        ckvn_r = P.ring("ckvn", 2, [128, 2, 512], BF16)
        t1_r = P.ring("t1", 2, [64, 512], F32)
        t2_r = P.ring("t2", 2, [64, 512], F32)
        t3_r = P.ring("t3", 1, [64, 512], F32)
        kro_r = P.ring("kro", 2, [64, 512], BF16)
        ka1 = P.sb("ka1", [64, 512], F32)
        ka2 = P.sb("ka2", [64, 512], F32)
        qno_r = P.ring("qno", 1, [128, 8, 512], BF16)
        qro_r = P.ring("qro", 1, [64, 8, 512], BF16)
        kno_r = P.ring("kno", 1, [128, 8, 512], BF16)
        sdk_r = P.ring("sdk", 2, [128, 32], F32)
        rk_r = P.ring("rk", 2, [128, 4, 8], F32)
        vmo_r = P.ring("vmo", 1, [128, 4, 1024], BF16)

        for si, S in enumerate(SEQS):
            sc_ = SC[si]
            for g in range(S // 512):
                t0 = g * 512
                hT, hk = hT_r.next()
                P.load(hT[:], sc_["hT"].rearrange("(k p) s -> p k s", p=128)[:, :, t0:t0 + 512], hk)
                cs, ck = cs_r.next()
                P.load(cs[:, 0, :], cosm_d[:, t0:t0 + 512], ck + "c")
                P.load(cs[:, 1, :], sinm_d[:, t0:t0 + 512], ck + "s")

                def proj(col0, m=128):
                    ps, pk = ps_r.next()
                    P.mm(ps[0:m, :], [(W[:, k, col0:col0 + m], hT[:, k, :]) for k in range(8)], [hk, "W"], [pk])
                    return ps, pk
                for (base, ring, dst) in ((768, gr_r, "grT"), (1792, ga_r, "gaT")):
                    gt, gk = ring.next()
                    for c in range(8):
                        ps, pk = proj(base + c * 128)
                        P.act(gt[:, c, :], ps[:], AF.Sigmoid, [pk], [gk + "c%d" % c])
                    P.dma(sc_[dst].rearrange("(c p) s -> p c s", p=128)[:, :, t0:t0 + 512], gt[:],
                          [gk + "c%d" % c for c in range(8)], [], chan="T" + gk, q="gpsimd")

                def latent(col0, nch, gcol0, ring, inv_n):
                    pcs = []
                    pss, pssk = pss_r.next()
                    sqs = []
                    for c in range(nch):
                        ps, pk = proj(col0 + c * 128)
                        sq, sqk = sq_r.next()
                        P.act(sq[:], ps[:], AF.Square, [pk], [sqk])
                        pcs.append((ps, pk)); sqs.append((sq, sqk))
                    P.mm(pss[:], [(ones, sq[:]) for sq, _ in sqs], [k_ for _, k_ in sqs] + ["cb"], [pssk])
                    sd, sdk_ = sd_r.next()
                    P.act(sd[:], pss[:], AF.Sqrt, [pssk], [sdk_], scale=inv_n, bias=EPS)
                    rs, rsk = rs_r.next()
                    P.recip(rs[:], sd[:], [sdk_], [rsk])
                    o, ok = ring.next()
                    for c in range(nch):
                        ps, pk = pcs[c]
                        P.stt(o[:, c, :], ps[:], cols[:, gcol0 + c:gcol0 + c + 1], rs[:], ALU.mult, ALU.mult,
                              [pk, rsk, "cols"], [ok + "c%d" % c])
                    return o, [ok + "c%d" % c for c in range(nch)]
                cqn, cqk = latent(0, 3, GCQ, cqn_r, 1.0 / 384)
                ckvn, ckvk = latent(384, 2, GCKV, ckvn_r, 1.0 / 256)

                pkr, pkrk = proj(640, 128)
                pkrr, pkrrk = proj(704, 128)
                sqkr, sqkrk = sqkr_r.next()
                P.act(sqkr[0:64, :], pkr[0:64, :], AF.Square, [pkrk], [sqkrk])
                u1, u1k = t1_r.next(); u2, u2k = t2_r.next()
                P.tt(u1[:], pkr[0:64, :], cs[:, 0, :], ALU.mult, [pkrk, ck + "c"], [u1k])
                P.tt(u2[:], pkrr[0:64, :], cs[:, 1, :], ALU.mult, [pkrrk, ck + "s"], [u2k])
                P.tt(ka1[:], u1[:], u2[:], ALU.add, [u1k, u2k], ["ka1"], eng="gpsimd")
                u3, u3k = t1_r.next(); u4, u4k = t2_r.next()
                P.tt(u3[:], pkrr[0:64, :], cs[:, 0, :], ALU.mult, [pkrrk, ck + "c"], [u3k])
                P.tt(u4[:], pkr[0:64, :], cs[:, 1, :], ALU.mult, [pkrk, ck + "s"], [u4k])
                P.tt(ka2[:], u3[:], u4[:], ALU.subtract, [u3k, u4k], ["ka2"], eng="gpsimd")
                t1, t1k = t1_r.next(); t2, t2k = t2_r.next()
                P.stt(t1[:], ka1[:], cols[0:64, GKR:GKR + 1], cs[:, 0, :], ALU.mult, ALU.mult,
                      ["ka1", ck + "c", "cols"], [t1k])
                P.stt(t2[:], ka2[:], cols[0:64, GKRS:GKRS + 1], cs[:, 1, :], ALU.mult, ALU.mult,
                      ["ka2", ck + "s", "cols"], [t2k])
                kro, krok = kro_r.next()
                P.tt(kro[:], t1[:], t2[:], ALU.add, [t1k, t2k], [krok], eng="gpsimd")
                P.store(sc_["kmrT"][:, t0:t0 + 512], kro[:], krok)

                qno, qnk = qno_r.next()
                qro, qrk = qro_r.next()
                for hh in range(8):
                    psn, psnk = ps_r.next()
                    P.mm(psn[:], [(Wuq[:, k, hh * 128:(hh + 1) * 128], cqn[:, k, :]) for k in range(3)], cqk + ["Wuq"], [psnk])
                    psr, psrk = ps_r.next()
                    P.mm(psr[:, :], [(Wuq[:, k, 1024 + hh * 64:1024 + hh * 64 + 128], cqn[:, k, :]) for k in range(3)],
                         cqk + ["Wuq"], [psrk])
                    psrr, psrrk = ps_r.next()
                    P.mm(psrr[:, :], [(Wuq[:, k, 1536 + hh * 64:1536 + hh * 64 + 128], cqn[:, k, :]) for k in range(3)],
                         cqk + ["Wuq", "Wuqz"], [psrrk])
                    sq, sqk = sq_r.next()
                    P.act(sq[:], psn[:], AF.Square, [psnk], [sqk])
                    sqr, sqrk = sqr_r.next()
                    P.act(sqr[0:64, :], psr[0:64, :], AF.Square, [psrk], [sqrk])
                    pss, pssk = pss_r.next()
                    P.mm(pss[:], [(ones, sq[:]), (ones, sqr[:])], [sqk, sqrk, "cb"] + ZK, [pssk])
                    sd, sdk_ = sd_r.next()
                    P.act(sd[:], pss[:], AF.Sqrt, [pssk], [sdk_], scale=1.0 / 192, bias=EPS)
                    rs, rsk = rs_r.next()
                    P.recip(rs[:], sd[:], [sdk_], [rsk])
                    P.stt(qno[:, hh, :], psn[:], gx[:, 0:1], rs[:], ALU.mult, ALU.mult, [psnk, rsk, "gx"], [qnk + "h%d" % hh])
                    t1, t1k = t1_r.next(); t2, t2k = t2_r.next(); t3, t3k = t3_r.next()
                    P.stt(t1[:], psr[0:64, :], gx[0:64, 1:2], cs[:, 0, :], ALU.mult, ALU.mult, [psrk, ck + "c", "gx"], [t1k])
                    P.stt(t2[:], psrr[0:64, :], gx[0:64, 2:3], cs[:, 1, :], ALU.mult, ALU.mult, [psrrk, ck + "s", "gx"], [t2k])
                    P.tt(t3[:], t1[:], t2[:], ALU.add, [t1k, t2k], [t3k], eng="gpsimd")
                    P.tt(qro[:, hh, :], t3[:], rs[0:64, :], ALU.mult, [t3k, rsk], [qrk + "h%d" % hh], eng="gpsimd")
                P.dma(sc_["qmnT"].rearrange("h p s -> p h s")[:, :, t0:t0 + 512], qno[:],
                      [qnk + "h%d" % i for i in range(8)], [], chan="T" + qnk, q="gpsimd")
                P.dma(sc_["qmrT"].rearrange("h p s -> p h s")[:, :, t0:t0 + 512], qro[:],
                      [qrk + "h%d" % i for i in range(8)], [], chan="T" + qrk, q="gpsimd")

                kno, knk = kno_r.next()
                pst, pstk = pst_r.next()
                pend_ks = []
                for hh in range(8):
                    ps, pk = ps_r.next()
                    P.mm(ps[:], [(Wuk[:, k, hh * 128:(hh + 1) * 128], ckvn[:, k, :]) for k in range(2)], ckvk + ["Wuk"], [pk])
                    P.act(kno[:, hh, :], ps[:], AF.Copy, [pk], [knk + "h%d" % hh])
                    sq, sqk = sq_r.next()
                    P.act(sq[:], ps[:], AF.Square, [pk], [sqk])

                    def kstat(hh=hh, sq=sq, sqk=sqk):
                        for j in range(4):
                            P.mm(pst[:, j, hh:hh + 1], [(sq[:, j * 128:(j + 1) * 128], cb[:, 128:129]),
                                                        (sqkr[:, j * 128:(j + 1) * 128], cb[:, 128:129])],
                                 [sqk, sqkrk, "cb"] + ZK, [pstk])
                    pend_ks.append(kstat)
                    if len(pend_ks) > 2:
                        pend_ks.pop(0)()
                while pend_ks:
                    pend_ks.pop(0)()
                P.dma(sc_["kmnT"].rearrange("h p s -> p h s")[:, :, t0:t0 + 512], kno[:],
                      [knk + "h%d" % i for i in range(8)], [], chan="T" + knk, q="gpsimd")
                sdk, sdkk = sdk_r.next()
                P.act(sdk[:], pst[:].rearrange("p j h -> p (j h)"), AF.Sqrt, [pstk], [sdkk], scale=1.0 / 192, bias=EPS)
                rk, rkk = rk_r.next()
                P.recip(rk[:].rearrange("p j h -> p (j h)"), sdk[:], [sdkk], [rkk])
                P.store(sc_["rstdk"][t0:t0 + 512, :].rearrange("(j p) h -> p j h", p=128), rk[:], rkk)

                vmo, vmk = vmo_r.next()
                for j in range(4):
                    for n in range(2):
                        ps, pk = ps_r.next()
                        P.mm(ps[:], [(ckvn[:, k, j * 128:(j + 1) * 128], Wuv[:, k, n * 512:(n + 1) * 512]) for k in range(2)],
                             ckvk + ["Wuv"], [pk])
                        P.act(vmo[:, j, n * 512:(n + 1) * 512], ps[:], AF.Copy, [pk], [vmk + "p%d" % (j * 2 + n)])
                P.dma(sc_["vmtok"][t0:t0 + 512, :].rearrange("(j p) c -> p j c", p=128), vmo[:],
                      [vmk + "p%d" % i for i in range(8)], [], chan="T" + vmk, q="gpsimd")
        P.finish()

    def phase2():
        P = Phase(nc, "r")
        cf, cb = consts(P)
        ident = cb[:, 0:128]
        psP_r = P.pring("psP", 1, [128, 512], F32)
        cols = make_cols(P, cf, [[(0, v2(ret_gn_g))]], psP_r.next())
        dst = P.sb("dst", [1, 8], F32)
        P.dma(dst[0:1, 0:4], v1(dec_f), [], ["dst"], chan="Ldst0")
        P.dma(dst[0:1, 4:8], v1(dec_b), [], ["dst"], chan="Ldst1")
        pdc, pdck = psP_r.next()
        P.mm(pdc[:, 0:8], [(cf[0:1, C_ONE:C_ONE + 128], dst[0:1, 0:8])], ["dst", "cf"], [pdck])
        lg = P.sb("lg", [128, 8], F32)
        P.act(lg[:], pdc[:, 0:8], AF.Exp, [pdck], ["lg"], scale=-1.0)
        P.ts(lg[:], lg[:], 1.0, None, ALU.add, ALU.bypass, ["lg"], ["lg"])
        P.act(lg[:], lg[:], AF.Ln, ["lg"], ["lg"])
        P.ts(lg[:], lg[:], -1.0, None, ALU.mult, ALU.bypass, ["lg"], ["lg"])
        DT = P.sb("DT", [128, 4, 128], F32)
        decf = P.sb("decf", [128, 4, 128], F32)
        decb = P.sb("decb", [128, 4, 128], F32)
        kcol = P.sb("kcol", [128, 16], F32)
        e1 = P.sb("e1", [128, 128], F32)
        e2 = P.sb("e2", [128, 128], F32)
        for hh in range(4):
            lf = lg[:, hh:hh + 1]; lb = lg[:, 4 + hh:5 + hh]
            P.act(e1[:], cf[:, C_A:C_A + 128], AF.Exp, ["cf", "lg"], ["e1"], scale=lf)
            P.tt(e1[:], e1[:], cf[:, C_MF:C_MF + 128], ALU.mult, ["e1", "cf"], ["e1"])
            P.act(e2[:], cf[:, C_B:C_B + 128], AF.Exp, ["cf", "lg"], ["e2"], scale=lb)
            P.tt(e2[:], e2[:], cf[:, C_MB:C_MB + 128], ALU.mult, ["e2", "cf"], ["e2"])
            P.tt(DT[:, hh, :], e1[:], e2[:], ALU.add, ["e1", "e2"], ["DT"])
            P.act(decf[:, hh, :], cf[:, C_C1:C_C1 + 128], AF.Exp, ["cf", "lg"], ["decf"], scale=lf)
            P.act(decb[:, hh, :], cf[:, C_C2:C_C2 + 128], AF.Exp, ["cf", "lg"], ["decb"], scale=lb)
            P.act(kcol[:, hh:hh + 1], cf[:, C_SM:C_SM + 1], AF.Exp, ["cf", "lg"], ["kcol"], scale=lf)
            P.act(kcol[:, 4 + hh:5 + hh], cf[:, C_SM + 1:C_SM + 2], AF.Exp, ["cf", "lg"], ["kcol"], scale=lb)
            P.act(kcol[:, 8 + hh:9 + hh], cf[:, C_SM + 2:C_SM + 3], AF.Exp, ["cf", "lg"], ["kcol"], scale=lf)
            P.act(kcol[:, 12 + hh:13 + hh], cf[:, C_SM + 2:C_SM + 3], AF.Exp, ["cf", "lg"], ["kcol"], scale=lb)
        Wro = P.sb("Wro", [128, 8, 1024], BF16)
        stg = P.ring("stg", 2, [128, 1024], F32)
        for k in range(8):
            st, sk = stg.next()
            P.load(st[:], w_ret_o[k * 128:(k + 1) * 128, :], sk)
            P.wcast(Wro[:, k, :], st[:], cols[:, k:k + 1], 1.0, [sk, "cols"], ["Wro"])

        kt_r = P.ring("kt", 2, [128, 4, 512], BF16)
        v_r = P.ring("v", 2, [128, 4, 1024], BF16)
        kd_r = P.ring("kd", 2, [128, 4, 128], BF16)
        psU_r = P.pring("psU", 1, [128, 512], F32)
        St = P.sb("St", [128, 4, 256], F32)
        sbb_r = P.ring("sbb", 3, [128, 1024], BF16)

        kfb = {}
        for col0 in (0, 4):
            t_ = P.sb("kfb%d" % col0, [128, 4, 128], F32)
            for hh in range(4):
                P.act(t_[:, hh, :], cf[:, C_ONE:C_ONE + 128], AF.Copy, ["cf", "kcol"], ["kfb%d" % col0],
                      scale=kcol[:, col0 + hh:col0 + hh + 1])
            kfb[col0] = t_

        def kdec(kt, ktk, j, col0):
            kd, kdk = kd_r.next()
            P.tt(kd[:], kt[:, j, :].rearrange("p (h d) -> p h d", d=128), kfb[col0][:], ALU.mult,
                 [ktk, "kfb%d" % col0], [kdk], eng="gpsimd")
            return kd, [kdk]

        def state_update(kd, kdk, v, vk, j, gcol0):
            for hp in range(2):
                psU, psUk = psU_r.next()
                for h2 in range(2):
                    hh = hp * 2 + h2
                    P.mm(psU[:, h2 * 256:(h2 + 1) * 256], [(kd[:, hh, :], v[:, j, hh * 256:(hh + 1) * 256])], kdk + [vk], [psUk])
                for h2 in range(2):
                    hh = hp * 2 + h2
                    P.stt(St[:, hh, :], St[:, hh, :], kcol[:, gcol0 + hh:gcol0 + hh + 1], psU[:, h2 * 256:(h2 + 1) * 256],
                          ALU.mult, ALU.add, ["St%d" % hh, psUk, "kcol"], ["St%d" % hh])

        for si, S in enumerate(SEQS):
            sc_ = SC[si]
            NG = S // 512
            P.memset(St[:], 0.0, ["St%d" % i_ for i_ in range(4)])
            for g in range(NG - 1, -1, -1):
                t0 = g * 512
                kt, ktk = kt_r.next()
                P.load(kt[:], sc_["ktok"][t0:t0 + 512, :].rearrange("(j p) c -> p j c", p=128), ktk)
                v, vk = v_r.next()
                P.load(v[:], sc_["vtok"][t0:t0 + 512, :].rearrange("(j p) c -> p j c", p=128), vk)
                for j in range(3, -1, -1):
                    n = g * 4 + j
                    sbb, sbk = sbb_r.next()
                    P.cp(sbb[:], St[:].rearrange("p h d -> p (h d)"), ["St%d" % i_ for i_ in range(4)], [sbk])
                    P.store(sc_["sb"][n, :, :], sbb[:], sbk, dkey="sbd%d_%d" % (si, n))
                    if n > 0:
                        kd, kdk = kdec(kt, ktk, j, 4)
                        state_update(kd, kdk, v, vk, j, 12)
            P.memset(St[:], 0.0, ["St%d" % i_ for i_ in range(4)])
            fw = getattr(P, "_fw", None)
            if fw is None:
                fw = dict(
                    qT=P.ring("qT", 2, [128, 4, 512], BF16), kT=P.ring("kT", 2, [128, 4, 512], BF16),
                    sbl=P.ring("sbl", 2, [128, 4, 1024], BF16), rgs=P.ring("rgs", 2, [128, 8, 512], BF16),
                    gr=P.ring("gr", 2, [128, 8, 512], BF16),
                    psS=P.pring("psS", 1, [128, 4, 128], F32), psO=P.pring("psO", 2, [128, 1024], F32),
                    tp=P.pring("tp", 1, [128, 8, 128], BF16), psP=psP_r,
                    pT=P.ring("pT", 2, [128, 4, 128], BF16), qf=P.ring("qf", 2, [128, 4, 128], BF16),
                    qb=P.ring("qb", 2, [128, 4, 128], BF16), Sfb=P.ring("Sfb", 3, [128, 4, 256], BF16),
                    st=P.ring("st", 2, [128, 32], F32), retn=P.ring("retn", 2, [128, 1024], BF16),
                    retg=P.ring("retg", 2, [128, 8, 512], BF16), m1=P.ring("m1", 2, [128, 8, 512], BF16),
                    junk=P.sb("junk", [128, 256], BF16),
                )
                P._fw = fw
            Sfb, Sfk = fw["Sfb"].next()
            P.cp(Sfb[:], St[:], ["St%d" % i_ for i_ in range(4)], [Sfk])
            tails = []
            for g in range(NG):
                t0 = g * 512
                qT, qTk = fw["qT"].next()
                P.load(qT[:], sc_["qrT"].rearrange("(h p) s -> p h s", p=128)[:, :, t0:t0 + 512], qTk)
                kT, kTk = fw["kT"].next()
                P.load(kT[:], sc_["krT"].rearrange("(h p) s -> p h s", p=128)[:, :, t0:t0 + 512], kTk)
                kt, ktk = kt_r.next()
                P.load(kt[:], sc_["ktok"][t0:t0 + 512, :].rearrange("(j p) c -> p j c", p=128), ktk)
                v, vk = v_r.next()
                P.load(v[:], sc_["vtok"][t0:t0 + 512, :].rearrange("(j p) c -> p j c", p=128), vk)
                sbl, sblk = fw["sbl"].next()
                P.dma(sbl[:], sc_["sb"][g * 4:(g + 1) * 4, :, :].rearrange("j p c -> p j c"),
                      ["sbd%d_%d" % (si, g * 4 + j) for j in range(4)], [sblk], chan="L" + sblk)
                rgs, rgsk = fw["rgs"].next()
                P.load(rgs[:], sc_["rgsT"].rearrange("(c p) s -> p c s", p=128)[:, :, t0:t0 + 512], rgsk)
                gr, grk = fw["gr"].next()
                P.load(gr[:], sc_["grT"].rearrange("(c p) s -> p c s", p=128)[:, :, t0:t0 + 512], grk)
                retg, retgk = fw["retg"].next()
                for j in range(4):
                    sl = slice(j * 128, (j + 1) * 128)
                    psS, psSk = fw["psS"].next()
                    for hh in range(4):
                        P.mm(psS[:, hh, :], [(kT[:, hh, sl], qT[:, hh, sl])], [kTk, qTk], [psSk])
                    pT, pTk = fw["pT"].next()
                    P.tt(pT[:], psS[:], DT[:], ALU.mult, [psSk, "DT"], [pTk])
                    qf, qfk = fw["qf"].next()
                    P.tt(qf[:], qT[:, :, sl], decf[:], ALU.mult, [qTk, "decf"], [qfk], eng="gpsimd")
                    qb, qbk = fw["qb"].next()
                    P.tt(qb[:], qT[:, :, sl], decb[:], ALU.mult, [qTk, "decb"], [qbk], eng="gpsimd")
                    Sfb_old, Sfk_old = Sfb, Sfk
                    kd, kdk = kdec(kt, ktk, j, 0)
                    state_update(kd, kdk, v, vk, j, 8)
                    Sfb, Sfk = fw["Sfb"].next()
                    P.cp(Sfb[:], St[:], ["St%d" % i_ for i_ in range(4)], [Sfk])
                    psO, psOk = fw["psO"].next()
                    for hh in range(4):
                        vs = v[:, j, hh * 256:(hh + 1) * 256]
                        P.mm(psO[:, hh * 256:(hh + 1) * 256],
                             [(pT[:, hh, :], vs), (qf[:, hh, :], Sfb_old[:, hh, :]),
                              (qb[:, hh, :], sbl[:, j, hh * 256:(hh + 1) * 256])],
                             [pTk, vk, qfk, Sfk_old, qbk, sblk], [psOk])
                    while tails:
                        tails.pop(0)()
                    st, stk = fw["st"].next()
                    for hh in range(4):
                        o = psO[:, hh * 256:(hh + 1) * 256]
                        P.act(fw["junk"][:], o, AF.Copy, [psOk], ["rjunk", stk + "a"], accum=st[:, hh:hh + 1])
                    for hh in range(4):
                        o = psO[:, hh * 256:(hh + 1) * 256]
                        P.act(fw["junk"][:], o, AF.Square, [psOk], ["rjunk", stk + "a2"], accum=st[:, 4 + hh:5 + hh])
                    P.ts(st[:, 8:12], st[:, 0:4], 1.0 / 256, None, ALU.mult, ALU.bypass, [stk + "a"], [stk + "b"])
                    P.tt(st[:, 12:16], st[:, 8:12], st[:, 8:12], ALU.mult, [stk + "b"], [stk + "c"])
                    P.stt(st[:, 16:20], st[:, 4:8], 1.0 / 256, st[:, 12:16], ALU.mult, ALU.subtract, [stk + "a2", stk + "c"], [stk + "d"])
                    P.act(st[:, 20:24], st[:, 16:20], AF.Sqrt, [stk + "d"], [stk + "e"], bias=EPS)
                    P.recip(st[:, 24:28], st[:, 20:24], [stk + "e"], [stk + "f"])
                    P.stt(st[:, 28:32], st[:, 8:12], -1.0, st[:, 24:28], ALU.mult, ALU.mult, [stk + "b", stk + "f"], [stk + "g"])
                    retn, retnk = fw["retn"].next()
                    for hh in range(4):
                        P.act(retn[:, hh * 256:(hh + 1) * 256], psO[:, hh * 256:(hh + 1) * 256], AF.Identity,
                              [psOk, stk + "f", stk + "g"], [retnk], scale=st[:, 24 + hh:25 + hh], bias=st[:, 28 + hh:29 + hh])

                    def tail(j=j, sl=sl, retn=retn, retnk=retnk, retg=retg, retgk=retgk, rgs=rgs, rgsk=rgsk,
                             gr=gr, grk=grk, t0=t0):
                        tp, tpk = fw["tp"].next()
                        for c in range(8):
                            P.tr(tp[:, c, :], retn[:, c * 128:(c + 1) * 128], ident, [retnk, "cb"], [tpk])
                        P.tt(retg[:, :, sl], tp[:], rgs[:, :, sl], ALU.mult, [tpk, rgsk], [retgk + "j%d" % j])
                        if j == 3:
                            m1, m1k = fw["m1"].next()
                            for c in range(8):
                                psP, psPk = fw["psP"].next()
                                P.mm(psP[:], [(Wro[:, k, c * 128:(c + 1) * 128], retg[:, k, :]) for k in range(8)],
                                     [retgk + "j%d" % jj for jj in range(4)] + ["Wro"], [psPk])
                                P.tt(m1[:, c, :], psP[:], gr[:, c, :], ALU.mult, [psPk, grk], [m1k + "c%d" % c])
                            P.dma(sc_["m1T"].rearrange("(c p) s -> p c s", p=128)[:, :, t0:t0 + 512], m1[:],
                                  [m1k + "c%d" % c for c in range(8)], [], chan="T" + m1k, q="gpsimd")
                    tails.append(tail)
            while tails:
                tails.pop(0)()
        P.finish()

    def phase3():
        P = Phase(nc, "m")
        cf, cb = consts(P)
        ones = cb[:, 128:256]
        onesf = cf[:, C_ONE:C_ONE + 128]
        Kn_r = P.ring("Kn", 2, [128, SMAX], BF16)
        V_r = P.ring("V", 2, [128, SMAX // 128, 128], BF16)
        Kr = P.sb("Kr", [128, SMAX], BF16)
        P.memset(Kr[64:128, :], 0.0, ["Krz"])
        rk = P.sb("rk", [128, SMAX // 128, 8], F32)
        Qn_r = P.ring("Qn", 3, [128, 512], BF16)
        Qr_r = P.ring("Qr", 3, [128, 512], BF16)
        for i_, t_ in enumerate(Qr_r.tiles):
            P.memset(t_[64:128, :], 0.0, ["Qrz%d" % i_])
        st_r = P.pring("st", 4, [128, 512], F32)
        o_r = P.pring("o", 2, [128, 512], F32)
        l_r = P.pring("l", 2, [128, 512], F32)
        p_r = P.ring("p", 6, [128, 512], BF16)
        acc_r = [P.ring("acc0", 2, [128, 512], F32), P.ring("acc1", 2, [128, 512], F32)]
        accs_r = P.ring("accs", 2, [128, 512], F32)
        rl_r = P.ring("rl", 2, [128, 512], F32)
        at_r = P.ring("at", 3, [128, 512], BF16)
        LAG = 2
        units = [(si, hh, qg, kt) for si, S in enumerate(SEQS) for hh in range(8)
                 for qg in range(S // 512) for kt in range(S // 128)]
        cur = {}
        pend = []

        def stage_a(u):
            si, hh, qg, kt = u
            S = SEQS[si]; sc_ = SC[si]; NK = S // 128
            if hh == 0 and qg == 0 and kt == 0:
                P.load(Kr[0:64, 0:S], sc_["kmrT"][:, :], "Kr")
                P.load(rk[:, 0:NK, :], sc_["rstdk"].rearrange("(t p) h -> p t h", p=128), "rk")
            if qg == 0 and kt == 0:
                Kn, Knk = Kn_r.next()
                P.load(Kn[:, 0:S], sc_["kmnT"][hh, :, :], Knk)
                V, Vk = V_r.next()
                P.load(V[:, 0:NK, :], sc_["vmtok"][:, hh * 128:(hh + 1) * 128].rearrange("(t p) c -> p t c", p=128), Vk)
                cur["K"] = (Kn, Knk, V, Vk)
            if kt == 0:
                q0 = qg * 512
                Qn, Qnk = Qn_r.next()
                P.load(Qn[:], sc_["qmnT"][hh, :, q0:q0 + 512], Qnk)
                Qr, Qrk = Qr_r.next()
                P.load(Qr[0:64, :], sc_["qmrT"][hh, :, q0:q0 + 512], Qrk)
                cur["Q"] = (Qn, Qnk, Qr, Qrk)
            Kn, Knk, V, Vk = cur["K"]
            Qn, Qnk, Qr, Qrk = cur["Q"]
            ks = slice(kt * 128, (kt + 1) * 128)
            st, stk = st_r.next()
            P.mm(st[:], [(Kn[:, ks], Qn[:]), (Kr[:, ks], Qr[:])], [Knk, "Kr", "Krz", Qnk, Qrk] + ["Qrz%d" % i_ for i_ in range(3)], [stk])
            p, pk = p_r.next()
            P.act(p[:], st[:], AF.Exp, [stk, "rk"], [pk], scale=rk[:, kt, hh:hh + 1])
            pend.append((u, p, pk, V, Vk))

        def stage_b():
            u, p, pk, V, Vk = pend.pop(0)
            si, hh, qg, kt = u
            S = SEQS[si]; sc_ = SC[si]; NK = S // 128
            if kt == 0:
                cur["o"] = o_r.next()
                cur["l"] = l_r.next()
                cur["acc"] = [acc_r[0].next(), acc_r[1].next()]
                cur["na"] = 0
            o, ok = cur["o"]
            l, lk = cur["l"]
            P.S.op("tensor", lambda e, o=o, V=V, kt=kt, p=p, NK=NK: e.matmul(
                o[:], V[:, kt, :], p[:], start=(kt == 0), stop=(kt == NK - 1)), [Vk, pk], [ok])
            if kt % 4 == 3:
                P.S.op("tensor", lambda e, l=l, p=p, kt=kt: e.matmul(
                    l[:], ones, p[:], start=(kt == 3), stop=False), [pk, "cb"], [lk])
            else:
                na = cur["na"]; cur["na"] = na + 1
                a, ak = cur["acc"][na % 2]
                if na < 2:
                    P.cp(a[:], p[:], [pk], [ak])
                else:
                    P.tt(a[:], a[:], p[:], ALU.add, [ak, pk], [ak])
            if kt == NK - 1:
                q0 = qg * 512
                (a0, a0k), (a1, a1k) = cur["acc"]
                asum, asumk = accs_r.next()
                P.tt(asum[:], a0[:], a1[:], ALU.add, [a0k, a1k], [asumk])
                P.S.op("tensor", lambda e, l=l, asum=asum: e.matmul(
                    l[:], onesf, asum[:], start=False, stop=True), [asumk, "cf"], [lk])
                rl, rlk = rl_r.next()
                P.recip(rl[:], l[:], [lk], [rlk])
                at, atk = at_r.next()
                P.tt(at[:], o[:], rl[:], ALU.mult, [ok, rlk], [atk])
                P.store(sc_["attnT"][hh * 128:(hh + 1) * 128, q0:q0 + 512], at[:], atk)

        for i in range(len(units) + LAG):
            if i < len(units):
                stage_a(units[i])
            if i >= LAG:
                stage_b()
        P.finish()

    def phase4():
        P = Phase(nc, "o")
        cf, cb = consts(P, need_cf=False)
        ident = cb[:, 0:128]
        Wmo = P.sb("Wmo", [128, 8, 1024], BF16)
        Wo = P.sb("Wo", [128, 8, 1024], BF16)
        stg = P.ring("stg", 2, [128, 1024], F32)
        for (Wt, src, wk) in ((Wmo, w_mla_o, "Wmo"), (Wo, w_out, "Wo")):
            for k in range(8):
                st, sk = stg.next()
                P.load(st[:], src[k * 128:(k + 1) * 128, :], sk)
                P.wcast(Wt[:, k, :], st[:], None, 1.0, [sk], [wk])
        at_r = P.ring("at", 2, [128, 8, 512], BF16)
        ga_r = P.ring("ga", 2, [128, 8, 512], BF16)
        m1_r = P.ring("m1", 2, [128, 8, 512], BF16)
        xt_r = P.ring("xt", 4, [128, D], F32)
        ps_r = P.pring("ps", 6, [128, 512], F32)
        pT_r = P.pring("pT", 2, [128, 8, 128], BF16)
        tm_r = P.ring("tm", 2, [128, 512], F32)
        mg_r = P.ring("mg", 2, [128, 8, 512], BF16)
        x1_r = P.ring("x1", 2, [128, D], F32)
        junk = P.sb("junk", [128, D], BF16)
        st_r = P.ring("stat", 2, [128, 4], F32)
        h_r = P.ring("h", 2, [128, D], BF16)
        hT_r = P.ring("hT", 2, [128, 8, 512], BF16)
        for si, S in enumerate(SEQS):
            sc_ = SC[si]
            h2v = sc_["h2T"].rearrange("(k p) s -> p k s", p=128)
            for g in range(S // 512):
                t0 = g * 512
                at, atk = at_r.next()
                P.load(at[:], sc_["attnT"].rearrange("(c p) s -> p c s", p=128)[:, :, t0:t0 + 512], atk)
                ga, gak = ga_r.next()
                P.load(ga[:], sc_["gaT"].rearrange("(c p) s -> p c s", p=128)[:, :, t0:t0 + 512], gak)
                m1, m1k = m1_r.next()
                P.load(m1[:], sc_["m1T"].rearrange("(c p) s -> p c s", p=128)[:, :, t0:t0 + 512], m1k)
                xs = []
                for j in range(4):
                    xt, xk = xt_r.next()
                    P.load(xt[:], X[si][t0 + j * 128:t0 + (j + 1) * 128, :], xk)
                    xs.append((xt, xk))
                mg, mgk = mg_r.next()
                for c in range(8):
                    ps, pk = ps_r.next()
                    P.mm(ps[:], [(Wmo[:, k, c * 128:(c + 1) * 128], at[:, k, :]) for k in range(8)], [atk, "Wmo"], [pk])
                    tm, tmk = tm_r.next()
                    P.tt(tm[:], ps[:], ga[:, c, :], ALU.mult, [pk, gak], [tmk])
                    P.tt(mg[:, c, :], tm[:], m1[:, c, :], ALU.add, [tmk, m1k], [mgk + "c%d" % c], eng="gpsimd")
                mgks = [mgk + "c%d" % c for c in range(8)]
                hT, hk = hT_r.next()
                pend_tr = []
                for j in range(4):
                    xt, xk = xs[j]
                    x1, x1k = x1_r.next()
                    for n in range(2):
                        ps, pk = ps_r.next()
                        P.mm(ps[:], [(mg[:, k, j * 128:(j + 1) * 128], Wo[:, k, n * 512:(n + 1) * 512]) for k in range(8)],
                             mgks + ["Wo"], [pk])
                        P.tt(x1[:, n * 512:(n + 1) * 512], ps[:], xt[:, n * 512:(n + 1) * 512], ALU.add, [pk, xk], [x1k + "n%d" % n])
                    x1ks = [x1k + "n0", x1k + "n1"]
                    P.dma(Y[si][t0 + j * 128:t0 + (j + 1) * 128, :], x1[:], x1ks, [], chan="T" + x1k, q="gpsimd")
                    st, stk = st_r.next()
                    P.act(junk[:], x1[:], AF.Square, x1ks, ["junk", stk + "a"], accum=st[:, 0:1])
                    P.act(st[:, 1:2], st[:, 0:1], AF.Sqrt, [stk + "a"], [stk + "b"], scale=1.0 / D, bias=EPS)
                    P.recip(st[:, 2:3], st[:, 1:2], [stk + "b"], [stk + "c"])
                    h, hhk = h_r.next()
                    P.act(h[:], x1[:], AF.Copy, x1ks + [stk + "c"], [hhk], scale=st[:, 2:3])

                    def tr_tail(j=j, h=h, hhk=hhk, hT=hT, hk=hk):
                        pT, pk = pT_r.next()
                        for k in range(8):
                            P.tr(pT[:, k, :], h[:, k * 128:(k + 1) * 128], ident, [hhk, "cb"], [pk])
                        P.cp(hT[:, :, j * 128:(j + 1) * 128], pT[:], [pk], [hk])
                    if pend_tr:
                        pend_tr.pop(0)()
                    pend_tr.append(tr_tail)
                while pend_tr:
                    pend_tr.pop(0)()
                P.store(h2v[:, :, t0:t0 + 512], hT[:], hk)
        P.finish()

    def phase5():
        P = Phase(nc, "f")
        cf, cb = consts(P)
        pu_r = P.pring("pu", 4, [128, 512], F32)
        cols = make_cols(P, cf, [[(0, v2(g_ffn))]], pu_r.next())
        cwp = P.es.enter_context(nc.psum_tensor("f_cwp", [128, 44, 4], F32))
        stg = P.ring("stg", 2, [128, 1408], F32)
        for n in range(4):
            st, sk = stg.next()
            P.dma(st[0:3, :], conv_w[:, n * 1408:(n + 1) * 1408], [], [sk], chan="Lcw0" + sk)
            P.dma(st[3:4, :], v1(conv_b)[:, n * 1408:(n + 1) * 1408], [], [sk], chan="Lcw1" + sk)
            for c in range(11):
                P.mm(cwp[:, n * 11 + c, :], [(st[0:4, c * 128:(c + 1) * 128], cf[0:4, C_ID:C_ID + 4])], [sk, "cf"], ["cwp"])
        cw = P.sb("cw", [128, 44, 4], F32)
        P.cp(cw[:], cwp[:], ["cwp"], ["cw"])
        Wup = P.sb("Wup", [128, 8, 5632], BF16)
        Wd = P.sb("Wd", [128, 22, 1024], BF16)
        for k in range(8):
            for n in range(4):
                st, sk = stg.next()
                P.load(st[:], w_up[k * 128:(k + 1) * 128, n * 1408:(n + 1) * 1408], sk)
                P.wcast(Wup[:, k, n * 1408:(n + 1) * 1408], st[:], cols[:, k:k + 1], 1.0, [sk, "cols"], ["Wup"])
        for k in range(22):
            st, sk = stg.next()
            P.load(st[:, 0:1024], w_down[k * 128:(k + 1) * 128, :], sk)
            P.wcast(Wd[:, k, :], st[:, 0:1024], None, 1.0, [sk], ["Wd"])
        hT_r = P.ring("hT", 1, [128, 8, 512], BF16)
        pd_r = P.pring("pd", 3, [128, 512], F32)
        ta_r = P.ring("ta", 2, [128, 512], F32)
        tb_r = P.ring("tb", 2, [128, 512], F32)
        sa_r = P.ring("sa", 2, [128, 512], F32)
        act_r = P.ring("act", 1, [128, 22, 512], BF16)
        x1_r = P.ring("x1", 2, [128, D], F32)
        yo_r = P.ring("yo", 2, [128, D], F32)
        for si, S in enumerate(SEQS):
            sc_ = SC[si]
            h2v = sc_["h2T"].rearrange("(k p) s -> p k s", p=128)
            for (t0, n) in ffn_groups(S):
                hT, hk = hT_r.next()
                lo = max(t0 - 1, 0); hi = min(t0 + n + 1, S)
                P.load(hT[:, :, lo - (t0 - 1):hi - (t0 - 1)], h2v[:, :, lo:hi], hk)
                if t0 == 0:
                    P.memset(hT[:, :, 0:1], 0.0, [hk])
                if t0 + n == S:
                    P.memset(hT[:, :, n + 1:n + 2], 0.0, [hk])
                act, actk = act_r.next()
                for c in range(22):
                    def up(ch):
                        pu, puk = pu_r.next()
                        P.mm(pu[:, 0:n + 2], [(Wup[:, k, ch * 128:(ch + 1) * 128], hT[:, k, 0:n + 2]) for k in range(8)],
                             [hk, "Wup"], [puk])
                        return pu, puk

                    def conv(pu, puk, ch, ring):
                        t, tk = ring.next()
                        P.act(t[:, 0:n], pu[:, 1:n + 1], AF.Identity, [puk, "cw"], [tk], scale=cw[:, ch, 1:2], bias=cw[:, ch, 3:4])
                        P.stt(t[:, 0:n], pu[:, 0:n], cw[:, ch, 0:1], t[:, 0:n], ALU.mult, ALU.add, [puk, "cw", tk], [tk])
                        P.stt(t[:, 0:n], pu[:, 2:n + 2], cw[:, ch, 2:3], t[:, 0:n], ALU.mult, ALU.add, [puk, "cw", tk], [tk])
                        return t, tk
                    pa, pak = up(c)
                    pb, pbk = up(22 + c)
                    ta, tak = conv(pa, pak, c, ta_r)
                    tb, tbk = conv(pb, pbk, 22 + c, tb_r)
                    sa, sak = sa_r.next()
                    P.act(sa[:, 0:n], ta[:, 0:n], AF.Silu, [tak], [sak])
                    P.tt(act[:, c, 0:n], sa[:, 0:n], tb[:, 0:n], ALU.mult, [sak, tbk], [actk + "c%d" % c], eng="gpsimd")
                actks = [actk + "c%d" % c for c in range(22)]
                m0 = 0
                while m0 < n:
                    m = min(128, n - m0)
                    x1, x1k = x1_r.next()
                    P.load(x1[0:m, :], Y[si][t0 + m0:t0 + m0 + m, :], x1k)
                    yo, yok = yo_r.next()
                    for nn in range(2):
                        pd, pdk = pd_r.next()
                        P.mm(pd[0:m, :], [(act[:, k, m0:m0 + m], Wd[:, k, nn * 512:(nn + 1) * 512]) for k in range(22)],
                             actks + ["Wd"], [pdk])
                        P.tt(yo[0:m, nn * 512:(nn + 1) * 512], pd[0:m, :], x1[0:m, nn * 512:(nn + 1) * 512], ALU.add,
                             [pdk, x1k], [yok + "n%d" % nn])
                    P.dma(Y[si][t0 + m0:t0 + m0 + m, :], yo[0:m, :], [yok + "n0", yok + "n1"], [], chan="T" + yok, q="gpsimd")
                    m0 += m
        P.finish()

    import os
    nph = int(os.environ.get("KPH", "6"))
    for ph in (phase1a, phase1b, phase2, phase3, phase4, phase5)[:nph]:
        ph()
    return nc


def host_consts(SMAX):
    i = np.arange(128, dtype=np.float32)
    cf = np.zeros((128, NCF), np.float32)
    diff = i[None, :] - i[:, None]
    cf[:, C_A:C_A + 128] = np.maximum(diff, 0)
    cf[:, C_B:C_B + 128] = np.maximum(-diff, 0)
    cf[:, C_MF:C_MF + 128] = (diff >= 0)
    cf[:, C_MB:C_MB + 128] = (diff < 0)
    cf[:, C_C1:C_C1 + 128] = (i + 1.0)[None, :]
    cf[:, C_C2:C_C2 + 128] = (128.0 - i)[None, :]
    cf[:, C_ID:C_ID + 128] = np.eye(128, dtype=np.float32)
    cf[:, C_ONE:C_ONE + 128] = 1.0
    cf[:, C_SM] = 127.0 - i
    cf[:, C_SM + 1] = i
    cf[:, C_SM + 2] = 128.0
    cb = np.zeros((128, 256), np.float32)
    cb[:, 0:128] = np.eye(128)
    cb[:, 128:256] = 1.0
    cb = cb.astype(ml_dtypes.bfloat16)
    pos = np.arange(SMAX, dtype=np.float32)

    def tab(d):
        inv = (np.float32(10000.0) ** (-np.arange(0, d, 2, dtype=np.float32) / np.float32(d))).astype(np.float32)
        ang = (pos[:, None] * inv[None, :]).astype(np.float32)
        c = np.cos(ang).astype(np.float32).T
        s = np.sin(ang).astype(np.float32).T
        return (np.ascontiguousarray(np.concatenate([c, c], 0)), np.ascontiguousarray(np.concatenate([s, s], 0)))
    cosr, sinr = tab(128)
    cosm, sinm = tab(64)
    return dict(cf=cf, cb=cb, cosr=cosr, sinr=sinr, cosm=cosm, sinm=sinm)


_CACHE = {}


def run(x_list, weights, n_cores=8):
    SEQS = tuple(int(x.shape[1]) for x in x_list)
    if SEQS not in _CACHE:
        _CACHE[SEQS] = build(SEQS)
    nc = _CACHE[SEQS]
    hc = host_consts(max(SEQS))
    w = {}
    for k, v in weights.items():
        a = np.asarray(v, dtype=np.float32)
        w[k] = np.ascontiguousarray(a.reshape(a.shape[1:]))
    in_maps = []
    for c in range(n_cores):
        m = dict(w)
        m.update(hc)
        for i, x in enumerate(x_list):
            m["x%d" % i] = np.ascontiguousarray(np.asarray(x[c], dtype=np.float32))
        in_maps.append(m)
    res = run_bass_kernel_spmd(nc, in_maps, core_ids=list(range(n_cores)))
    global LAST_RES
    LAST_RES = res
    outs = []
    for i in range(len(x_list)):
        outs.append(np.stack([np.asarray(res.results[c]["y%d" % i], dtype=np.float32) for c in range(n_cores)], 0))
    return tuple(outs)


def kernel(x_prompt, x_sample, **weights):
    return run([np.asarray(x_prompt), np.asarray(x_sample)], weights)
```

```python
import contextlib
import math
import numpy as np
import ml_dtypes
import concourse.bass as bass
import concourse.mybir as mybir
from concourse.bass_utils import run_bass_kernel_spmd

F32 = mybir.dt.float32
BF16 = mybir.dt.bfloat16
AF = mybir.ActivationFunctionType
ALU = mybir.AluOpType

D = 1024
IN_W = 5824
EPS = 1e-6
SAME_ENGINE_SYNC = True
ENGS = ("sync", "scalar", "vector", "gpsimd", "tensor")

C_A, C_B, C_MF, C_MB, C_C1, C_C2, C_ID, C_ONE, C_SM = 0, 128, 256, 384, 512, 640, 768, 896, 1024
NCF = 1032


class Op:
    __slots__ = ("eng", "fn", "chan", "inc", "waits", "signal", "val", "idx")

    def __init__(self, eng, fn, chan, inc):
        self.eng = eng; self.fn = fn; self.chan = chan; self.inc = inc
        self.waits = {}; self.signal = False; self.val = None; self.idx = None


class Sched:
    def __init__(self, nc, tag=""):
        self.nc = nc
        self.tag = tag
        self.ops = {e: [] for e in ENGS}
        self.chan_ops = {}
        self.last_w = {}
        self.readers = {}
        self.n = 0

    def op(self, eng, fn, reads=(), writes=(), chan=None):
        is_dma = chan is not None
        if chan is None:
            chan = "E_" + eng
        o = Op(eng, fn, chan, 16 if is_dma else 1)
        o.idx = self.n; self.n += 1
        deps = []
        for b in reads:
            w = self.last_w.get(b)
            if w is not None:
                deps.append(w)
        for b in writes:
            w = self.last_w.get(b)
            if w is not None:
                deps.append(w)
            deps.extend(self.readers.get(b, ()))
        for d in deps:
            if d is o:
                continue
            if d.eng == eng and d.chan == chan:
                if eng == "tensor" or not SAME_ENGINE_SYNC:
                    continue
            cur = o.waits.get(d.chan)
            if cur is None or cur.idx < d.idx:
                o.waits[d.chan] = d
        for b in writes:
            self.last_w[b] = o
            self.readers[b] = []
        for b in reads:
            self.readers.setdefault(b, []).append(o)
        self.ops[eng].append(o)
        self.chan_ops.setdefault(chan, []).append(o)
        return o

    def emit(self, es):
        nc = self.nc
        for e in ENGS:
            for o in self.ops[e]:
                for d in o.waits.values():
                    d.signal = True
        for c, lst in self.chan_ops.items():
            lst[-1].signal = True
            if lst[0].inc == 16:
                for o in lst:
                    o.signal = True
        sems = {}
        finals = {}
        for c, lst in self.chan_ops.items():
            sems[c] = nc.alloc_semaphore(name="s_" + self.tag + "_" + c)
            v = 0
            for o in lst:
                if o.signal:
                    v += o.inc
                    o.val = v
            finals[c] = v
        block = es.enter_context(nc.Block())

        def run(engname):
            def body(eng):
                waited = {}
                for o in self.ops[engname]:
                    for c, d in o.waits.items():
                        if waited.get(c, 0) >= d.val:
                            continue
                        eng.wait_ge(sems[c], d.val)
                        waited[c] = d.val
                    ins = o.fn(eng)
                    if o.signal:
                        ins.then_inc(sems[o.chan], o.inc)
                for c, v in finals.items():
                    if v > 0 and waited.get(c, 0) < v:
                        eng.wait_ge(sems[c], v)
            return body
        block.sync(run("sync"))
        block.scalar(run("scalar"))
        block.vector(run("vector"))
        block.gpsimd(run("gpsimd"))
        block.tensor(run("tensor"))


class Ring:
    def __init__(self, name, tiles):
        self.name = name; self.tiles = tiles; self.i = -1

    def next(self):
        self.i = (self.i + 1) % len(self.tiles)
        return self.tiles[self.i], "%s%d" % (self.name, self.i)


class Phase:
    def __init__(self, nc, name):
        self.nc = nc; self.name = name
        self.es = contextlib.ExitStack()
        self.es.enter_context(nc.cleanup_on_exit())
        self.es.callback(nc.all_engine_barrier)
        self.S = Sched(nc, name)
        self.wtog = 0

    def sb(self, name, shape, dt):
        return self.es.enter_context(self.nc.sbuf_tensor(self.name + name, shape, dt))

    def ring(self, name, n, shape, dt):
        return Ring(name, [self.sb("%s_%d" % (name, i), shape, dt) for i in range(n)])

    def pring(self, name, n, shape, dt):
        return Ring(name, [self.es.enter_context(self.nc.psum_tensor("%s%s_%d" % (self.name, name, i), shape, dt))
                           for i in range(n)])

    def dma(self, out, in_, reads, writes, chan, q="sync", slow=False):
        if slow:
            f = lambda e: e.dma_start(out=out, in_=in_, allow_slow_non_contiguous=True)
        else:
            f = lambda e: e.dma_start(out=out, in_=in_)
        return self.S.op(q, f, reads, writes, chan=chan)

    def load(self, out, in_, key, slow=False):
        return self.dma(out, in_, (), [key], chan="L" + key, q="sync", slow=slow)

    def store(self, out, in_, key, dkey=None):
        return self.dma(out, in_, [key], [dkey] if dkey else (), chan="T" + key, q="gpsimd")

    def act(self, out, in_, func, reads, writes, scale=None, bias=None, accum=None):
        kw = {}
        if scale is not None:
            kw["scale"] = scale
        if bias is not None:
            kw["bias"] = bias
        if accum is not None:
            kw["accum_out"] = accum
        return self.S.op("scalar", lambda e: e.activation(out=out, in_=in_, func=func, **kw), reads, writes)

    def ts(self, out, in0, s1, s2, op0, op1, reads, writes, eng="vector"):
        return self.S.op(eng, lambda e: e.tensor_scalar(out=out, in0=in0, scalar1=s1, scalar2=s2, op0=op0, op1=op1),
                         reads, writes)

    def tt(self, out, in0, in1, op, reads, writes, eng="vector"):
        return self.S.op(eng, lambda e: e.tensor_tensor(out=out, in0=in0, in1=in1, op=op), reads, writes)

    def stt(self, out, in0, scalar, in1, op0, op1, reads, writes):
        return self.S.op("vector", lambda e: e.scalar_tensor_tensor(out=out, in0=in0, scalar=scalar, in1=in1,
                                                                    op0=op0, op1=op1), reads, writes)

    def cp(self, out, in_, reads, writes, eng="vector"):
        return self.S.op(eng, lambda e: e.tensor_copy(out=out, in_=in_), reads, writes)

    def recip(self, out, in_, reads, writes):
        return self.S.op("vector", lambda e: e.reciprocal(out=out, in_=in_), reads, writes)

    def memset(self, ap, val, writes, eng="vector"):
        return self.S.op(eng, lambda e: e.memset(ap, val), (), writes)

    def mm(self, out, pairs, reads, writes):
        n = len(pairs)
        for i, (l, r) in enumerate(pairs):
            self.S.op("tensor", lambda e, l=l, r=r, i=i: e.matmul(out, l, r, start=(i == 0), stop=(i == n - 1)),
                      reads, writes)

    def tr(self, out, in_, ident, reads, writes):
        return self.S.op("tensor", lambda e: e.transpose(out, in_, ident), reads, writes)

    def finish(self):
        self.S.emit(self.es)
        self.es.close()

    def wcast(self, out, in_, scol, const, reads, writes):
        if scol is None:
            return self.ts(out, in_, float(const), None, ALU.mult, ALU.bypass, reads, writes)
        return self.ts(out, in_, scol, float(const), ALU.mult, ALU.mult, reads, writes)


def r3(ap, pat, **kw):
    return ap.rearrange(pat, **kw)


def ffn_groups(S):
    ng = -(-S // 510)
    base, rem = divmod(S, ng)
    out = []
    t = 0
    for i in range(ng):
        n = base + (1 if i < rem else 0)
        out.append((t, n))
        t += n
    return out


def build(SEQS):
    nc = bass.Bass("TRN2", target_bir_lowering=False)
    NS = len(SEQS)
    SMAX = max(SEQS)

    def din(name, shape, dt=F32):
        return nc.dram_tensor(name, list(shape), dt, kind="ExternalInput").ap()

    import os
    DBG = os.environ.get("KDBG", "") != ""

    def dscr(name, shape, dt=BF16):
        return nc.dram_tensor(name, list(shape), dt, kind="ExternalOutput" if DBG else "Internal").ap()

    X = [din("x%d" % i, [S, D]) for i, S in enumerate(SEQS)]
    Y = [nc.dram_tensor("y%d" % i, [S, D], F32, kind="ExternalOutput").ap() for i, S in enumerate(SEQS)]
    g_mix = din("g_mix", [D]); w_in = din("w_in", [D, IN_W])
    dec_f = din("ret_decay_fwd", [4]); dec_b = din("ret_decay_bwd", [4])
    ret_gn_g = din("ret_gn_g", [1024]); w_ret_o = din("w_ret_o", [1024, D])
    g_cq = din("g_cq", [384]); w_uq = din("w_uq", [384, 1536])
    g_ckv = din("g_ckv", [256]); w_ukv = din("w_ukv", [256, 2048])
    g_qn = din("g_qn", [192]); g_kn = din("g_kn", [192])
    w_mla_o = din("w_mla_o", [1024, D]); w_out = din("w_out", [D, D])
    g_ffn = din("g_ffn", [D]); w_up = din("w_up", [D, 5632])
    conv_w = din("conv_w", [3, 5632]); conv_b = din("conv_b", [5632])
    w_down = din("w_down", [2816, D])
    cf_d = din("cf", [128, NCF]); cb_d = din("cb", [128, 256], BF16)
    cosr_d = din("cosr", [128, SMAX]); sinr_d = din("sinr", [128, SMAX])
    cosm_d = din("cosm", [64, SMAX]); sinm_d = din("sinm", [64, SMAX])

    SC = []
    for i, S in enumerate(SEQS):
        p = "s%d_" % i
        SC.append(dict(
            hT=dscr(p + "hT", [D, S]), qrT=dscr(p + "qrT", [512, S]), krT=dscr(p + "krT", [512, S]),
            ktok=dscr(p + "ktok", [S, 512]), vtok=dscr(p + "vtok", [S, 1024]), rgsT=dscr(p + "rgsT", [1024, S]),
            grT=dscr(p + "grT", [1024, S]), gaT=dscr(p + "gaT", [1024, S]),
            qmnT=dscr(p + "qmnT", [8, 128, S]), qmrT=dscr(p + "qmrT", [8, 64, S]),
            kmnT=dscr(p + "kmnT", [8, 128, S]), kmrT=dscr(p + "kmrT", [64, S]),
            vmtok=dscr(p + "vmtok", [S, 1024]), rstdk=dscr(p + "rstdk", [S, 8], F32),
            sb=dscr(p + "sb", [S // 128, 128, 1024]), m1T=dscr(p + "m1T", [1024, S]),
            attnT=dscr(p + "attnT", [1024, S]), h2T=dscr(p + "h2T", [D, S]),
        ))

    def consts(P, need_cf=True):
        cb = P.sb("cb", [128, 256], BF16)
        P.load(cb[:], cb_d[:, :], "cb")
        cf = None
        if need_cf:
            cf = P.sb("cf", [128, NCF], F32)
            P.load(cf[:], cf_d[:, :], "cf")
        return cf, cb

    def make_cols(P, cf, rows, psk, name="cols"):
        R = sum(max(s.shape[0] for _, s in segs) for segs in rows)
        vst = P.sb(name + "_st", [R, 128], F32)
        P.memset(vst[:], 0.0, [name + "_st"])
        r = 0
        k = 0
        for segs in rows:
            nr = max(s.shape[0] for _, s in segs)
            for c0, src in segs:
                P.dma(vst[r:r + src.shape[0], c0:c0 + src.shape[1]], src, [], [name + "_st"],
                      chan="L%s%d" % (name, k), q="sync")
                k += 1
            r += nr
        ps, pkey = psk
        P.mm(ps[:, 0:R], [(vst[0:R, :], cf[0:R, C_ID:C_ID + R])], [name + "_st", "cf"], [pkey])
        cols = P.sb(name, [128, R], F32)
        P.cp(cols[:], ps[:, 0:R], [pkey], [name])
        return cols

    def v2(ap, p=128):
        return ap.rearrange("(k p) -> k p", p=p)

    def v1(ap):
        return ap.rearrange("(o n) -> o n", o=1)

    def phase1a():
        P = Phase(nc, "a")
        cf, cb = consts(P)
        ident = cb[:, 0:128]
        ps_r = P.pring("ps", 6, [128, 512], F32)
        cols = make_cols(P, cf, [[(0, v2(g_mix))]], ps_r.next())
        W = P.sb("W", [128, 8, 4096], BF16)
        stg = P.ring("stg", 2, [128, 3072], F32)
        for k in range(8):
            st, sk = stg.next()
            P.load(st[:], w_in[k * 128:(k + 1) * 128, 0:3072], sk)
            g = cols[:, k:k + 1]
            wk = "W"
            s4 = st[:, 0:512].rearrange("p (h d) -> p h d", d=128)
            P.wcast(W[:, k, 0:512], st[:, 0:512], g, 1.0, [sk, "cols"], [wk])
            o4 = W[:, k, 512:1024].rearrange("p (h d) -> p h d", d=128)
            P.wcast(o4[:, :, 0:64], s4[:, :, 64:128], g, -1.0, [sk, "cols"], [wk])
            P.wcast(o4[:, :, 64:128], s4[:, :, 0:64], g, 1.0, [sk, "cols"], [wk])
            sc = 128.0 ** -0.5
            s4 = st[:, 512:1024].rearrange("p (h d) -> p h d", d=128)
            P.wcast(W[:, k, 1024:1536], st[:, 512:1024], g, sc, [sk, "cols"], [wk])
            o4 = W[:, k, 1536:2048].rearrange("p (h d) -> p h d", d=128)
            P.wcast(o4[:, :, 0:64], s4[:, :, 64:128], g, -sc, [sk, "cols"], [wk])
            P.wcast(o4[:, :, 64:128], s4[:, :, 0:64], g, sc, [sk, "cols"], [wk])
            P.wcast(W[:, k, 2048:4096], st[:, 1024:3072], g, 1.0, [sk, "cols"], [wk])

        xt_r = P.ring("xt", 4, [128, D], F32)
        junk = P.sb("junk", [128, D], BF16)
        st_r = P.ring("stat", 2, [128, 4], F32)
        h_r = P.ring("h", 2, [128, D], BF16)
        pT_r = P.pring("pT", 1, [128, 8, 128], BF16)
        hT_r = P.ring("hT", 2, [128, 8, 512], BF16)
        cs_r = P.ring("cs", 2, [128, 2, 512], F32)
        ktp_r = P.pring("ktp", 1, [128, 4, 128], BF16)
        t1_r = P.ring("t1", 2, [128, 512], F32)
        t2_r = P.ring("t2", 2, [128, 512], F32)
        qo_r = P.ring("qo", 2, [128, 4, 512], BF16)
        ko_r = P.ring("ko", 2, [128, 4, 512], BF16)
        kt_r = P.ring("kt", 2, [128, 4, 512], BF16)
        rg_r = P.ring("rg", 2, [128, 8, 512], BF16)
        vo_r = P.ring("vo", 2, [128, 4, 1024], BF16)

        glist = [(si, g) for si, S in enumerate(SEQS) for g in range(S // 512)]

        def norm_group(si, g):
            sc_ = SC[si]
            t0 = g * 512
            xs = []
            for j in range(4):
                xt, xk = xt_r.next()
                P.load(xt[:], X[si][t0 + j * 128:t0 + (j + 1) * 128, :], xk)
                xs.append((xt, xk))
            hT, hk = hT_r.next()
            for j in range(4):
                xt, xk = xs[j]
                st, stk = st_r.next()
                P.act(junk[:], xt[:], AF.Square, [xk], ["junk", stk + "a"], accum=st[:, 0:1])
                P.act(st[:, 1:2], st[:, 0:1], AF.Sqrt, [stk + "a"], [stk + "b"], scale=1.0 / D, bias=EPS)
                P.recip(st[:, 2:3], st[:, 1:2], [stk + "b"], [stk + "c"])
                h, hhk = h_r.next()
                P.act(h[:], xt[:], AF.Copy, [xk, stk + "c"], [hhk], scale=st[:, 2:3])
                pT, pk = pT_r.next()
                for k in range(8):
                    P.tr(pT[:, k, :], h[:, k * 128:(k + 1) * 128], ident, [hhk, "cb"], [pk])
                P.cp(hT[:, :, j * 128:(j + 1) * 128], pT[:], [pk], [hk])
            P.store(sc_["hT"].rearrange("(k p) s -> p k s", p=128)[:, :, t0:t0 + 512], hT[:], hk)
            return hT, hk

        nxt = norm_group(*glist[0])
        for gi, (si, g) in enumerate(glist):
            if True:
                sc_ = SC[si]
                t0 = g * 512
                hT, hk = nxt
                cs, ck = cs_r.next()
                P.load(cs[:, 0, :], cosr_d[:, t0:t0 + 512], ck + "c")
                P.load(cs[:, 1, :], sinr_d[:, t0:t0 + 512], ck + "s")
                def proj(col0):
                    ps, pk = ps_r.next()
                    P.mm(ps[:], [(W[:, k, col0:col0 + 128], hT[:, k, :]) for k in range(8)], [hk, "W"], [pk])
                    return ps, pk

                qo, qk = qo_r.next()
                ko, kk = ko_r.next()
                kt, ktk = kt_r.next()
                pend_kt = []
                for (base, dst, dk) in ((0, qo, qk), (1024, ko, kk)):
                    for hh in range(4):
                        pa, pak = proj(base + hh * 128)
                        pb, pbk = proj(base + 512 + hh * 128)
                        t1, t1k = t1_r.next()
                        t2, t2k = t2_r.next()
                        P.tt(t1[:], pa[:], cs[:, 0, :], ALU.mult, [pak, ck + "c"], [t1k])
                        P.tt(t2[:], pb[:], cs[:, 1, :], ALU.mult, [pbk, ck + "s"], [t2k])
                        P.tt(dst[:, hh, :], t1[:], t2[:], ALU.add, [t1k, t2k], [dk + "h%d" % hh], eng="gpsimd")
                        if base == 1024:
                            def ktail(hh=hh, dk=dk):
                                ktp, ktpk = ktp_r.next()
                                for j in range(4):
                                    P.tr(ktp[:, j, :], ko[:, hh, j * 128:(j + 1) * 128], ident, [dk + "h%d" % hh, "cb"], [ktpk])
                                P.cp(kt[:, :, hh * 128:(hh + 1) * 128], ktp[:], [ktpk], [ktk + "h%d" % hh])
                            if pend_kt:
                                pend_kt.pop(0)()
                            pend_kt.append(ktail)
                while pend_kt:
                    pend_kt.pop(0)()
                hkeys = lambda kk_: [kk_ + "h%d" % i for i in range(4)]
                P.dma(sc_["qrT"].rearrange("(h p) s -> p h s", p=128)[:, :, t0:t0 + 512], qo[:], hkeys(qk), [],
                      chan="T" + qk, q="gpsimd")
                P.dma(sc_["krT"].rearrange("(h p) s -> p h s", p=128)[:, :, t0:t0 + 512], ko[:], hkeys(kk), [],
                      chan="T" + kk, q="gpsimd")
                P.dma(sc_["ktok"][t0:t0 + 512, :].rearrange("(j p) c -> p j c", p=128), kt[:], hkeys(ktk), [],
                      chan="T" + ktk, q="gpsimd")
                if gi + 1 < len(glist):
                    nxt = norm_group(*glist[gi + 1])
                rg, rgk = rg_r.next()
                for c in range(8):
                    ps, pk = proj(3072 + c * 128)
                    P.act(rg[:, c, :], ps[:], AF.Silu, [pk], [rgk + "c%d" % c])
                P.dma(sc_["rgsT"].rearrange("(c p) s -> p c s", p=128)[:, :, t0:t0 + 512], rg[:],
                      [rgk + "c%d" % c for c in range(8)], [], chan="T" + rgk, q="gpsimd")
                vo, vk = vo_r.next()
                for j in range(4):
                    for n in range(2):
                        ps, pk = ps_r.next()
                        P.mm(ps[:], [(hT[:, k, j * 128:(j + 1) * 128], W[:, k, 2048 + n * 512:2048 + (n + 1) * 512])
                                     for k in range(8)], [hk, "W"], [pk])
                        P.act(vo[:, j, n * 512:(n + 1) * 512], ps[:], AF.Copy, [pk], [vk + "p%d" % (j * 2 + n)])
                P.dma(sc_["vtok"][t0:t0 + 512, :].rearrange("(j p) c -> p j c", p=128), vo[:],
                      [vk + "p%d" % i for i in range(8)], [], chan="T" + vk, q="gpsimd")
        P.finish()

    def phase1b():
        P = Phase(nc, "b")
        cf, cb = consts(P)
        ones = cb[:, 128:256]
        rows = [[(0, v2(g_mix))], [(0, v2(g_cq))], [(0, v2(g_ckv))],
                [(0, v1(g_qn[0:128]))], [(0, v1(g_kn[0:128]))],
                [(0, v1(g_qn[128:192]))], [(0, v1(g_qn[160:192])), (32, v1(g_qn[128:160]))],
                [(0, v1(g_kn[128:192]))], [(0, v1(g_kn[160:192])), (32, v1(g_kn[128:160]))]]
        ps_r = P.pring("ps", 6, [128, 512], F32)
        cols = make_cols(P, cf, rows, ps_r.next())
        GCQ, GCKV, GQN, GKN, GQR, GQRS, GKR, GKRS = 8, 11, 13, 14, 15, 16, 17, 18
        gx = P.sb("gx", [128, 4], F32)
        qs = 192.0 ** -0.5
        P.stt(gx[:, 0:1], cols[:, GQN:GQN + 1], qs, cols[:, GKN:GKN + 1], ALU.mult, ALU.mult, ["cols"], ["gx"])
        P.ts(gx[:, 1:3], cols[:, GQR:GQR + 2], qs, None, ALU.mult, ALU.bypass, ["cols"], ["gx"])
        W = P.sb("W", [128, 8, 2816], BF16)
        stg = P.ring("stg", 2, [128, 3072], F32)
        for k in range(8):
            st, sk = stg.next()
            P.load(st[:, 0:2752], w_in[k * 128:(k + 1) * 128, 3072:5824], sk)
            g = cols[:, k:k + 1]
            P.wcast(W[:, k, 0:704], st[:, 0:704], g, 1.0, [sk, "cols"], ["W"])
            P.wcast(W[:, k, 704:736], st[:, 672:704], g, -1.0, [sk, "cols"], ["W"])
            P.wcast(W[:, k, 736:768], st[:, 640:672], g, 1.0, [sk, "cols"], ["W"])
            P.wcast(W[:, k, 768:2816], st[:, 704:2752], g, 1.0, [sk, "cols"], ["W"])
        Wuq = P.sb("Wuq", [128, 3, 2112], BF16)
        P.memset(Wuq[:, :, 2048:2112], 0.0, ["Wuqz"])
        for k in range(3):
            st, sk = stg.next()
            P.load(st[:, 0:1536], w_uq[k * 128:(k + 1) * 128, :], sk)
            s3 = st[:, 0:1536].rearrange("p (h d) -> p h d", d=192)
            P.wcast(Wuq[:, k, 0:1024].rearrange("p (h d) -> p h d", d=128), s3[:, :, 0:128], None, 1.0, [sk], ["Wuq"])
            P.wcast(Wuq[:, k, 1024:1536].rearrange("p (h d) -> p h d", d=64), s3[:, :, 128:192], None, 1.0, [sk], ["Wuq"])
            o3 = Wuq[:, k, 1536:2048].rearrange("p (h d) -> p h d", d=64)
            P.wcast(o3[:, :, 0:32], s3[:, :, 160:192], None, -1.0, [sk], ["Wuq"])
            P.wcast(o3[:, :, 32:64], s3[:, :, 128:160], None, 1.0, [sk], ["Wuq"])
        Wuk = P.sb("Wuk", [128, 2, 1024], BF16)
        Wuv = P.sb("Wuv", [128, 2, 1024], BF16)
        for k in range(2):
            st, sk = stg.next()
            P.load(st[:, 0:2048], w_ukv[k * 128:(k + 1) * 128, :], sk)
            s3 = st[:, 0:2048].rearrange("p (h d) -> p h d", d=256)
            P.wcast(Wuk[:, k, :].rearrange("p (h d) -> p h d", d=128), s3[:, :, 0:128], None, 1.0, [sk], ["Wuk"])
            P.wcast(Wuv[:, k, :].rearrange("p (h d) -> p h d", d=128), s3[:, :, 128:256], None, 1.0, [sk], ["Wuv"])

        hT_r = P.ring("hT", 2, [128, 8, 512], BF16)
        cs_r = P.ring("cs", 2, [64, 2, 512], F32)
        pss_r = P.pring("pss", 1, [128, 512], F32)
        pst_r = P.pring("pst", 1, [128, 4, 8], F32)
        gr_r = P.ring("gr", 2, [128, 8, 512], BF16)
        ga_r = gr_r
        sq_r = P.ring("sq", 3, [128, 512], BF16)
        sqr_r = P.ring("sqr", 2, [128, 512], BF16)
        sqkr_r = P.ring("sqkr", 2, [128, 512], BF16)
        for r_ in (sqr_r, sqkr_r):
            for i_, t_ in enumerate(r_.tiles):
                P.memset(t_[64:128, :], 0.0, ["%sz%d" % (r_.name, i_)])
        ZK = ["sqrz0", "sqrz1", "sqkrz0", "sqkrz1"]
        sd_r = P.ring("sd", 2, [128, 512], F32)
        rs_r = P.ring("rs", 2, [128, 512], F32)
        cqn_r = P.ring("cqn", 2, [128, 3, 512], BF16)
        ckvn_r = P.ring("ckvn", 2, [128, 2, 512], BF16)
        t1_r = P.ring("t1", 2, [64, 512], F32)
        t2_r = P.ring("t2", 2, [64, 512], F32)
        t3_r = P.ring("t3", 1, [64, 512], F32)
        kro_r = P.ring("kro", 2, [64, 512], BF16)
        ka1 = P.sb("ka1", [64, 512], F32)
        ka2 = P.sb("ka2", [64, 512], F32)
        qno_r = P.ring("qno", 1, [128, 8, 512], BF16)
        qro_r = P.ring("qro", 1, [64, 8, 512], BF16)
        kno_r = P.ring("kno", 1, [128, 8, 512], BF16)
        sdk_r = P.ring("sdk", 2, [128, 32], F32)
        rk_r = P.ring("rk", 2, [128, 4, 8], F32)
        vmo_r = P.ring("vmo", 1, [128, 4, 1024], BF16)

        for si, S in enumerate(SEQS):
            sc_ = SC[si]
            for g in range(S // 512):
                t0 = g * 512
                hT, hk = hT_r.next()
                P.load(hT[:], sc_["hT"].rearrange("(k p) s -> p k s", p=128)[:, :, t0:t0 + 512], hk)
                cs, ck = cs_r.next()
                P.load(cs[:, 0, :], cosm_d[:, t0:t0 + 512], ck + "c")
                P.load(cs[:, 1, :], sinm_d[:, t0:t0 + 512], ck + "s")

                def proj(col0, m=128):
                    ps, pk = ps_r.next()
                    P.mm(ps[0:m, :], [(W[:, k, col0:col0 + m], hT[:, k, :]) for k in range(8)], [hk, "W"], [pk])
                    return ps, pk
                for (base, ring, dst) in ((768, gr_r, "grT"), (1792, ga_r, "gaT")):
                    gt, gk = ring.next()
                    for c in range(8):
                        ps, pk = proj(base + c * 128)
                        P.act(gt[:, c, :], ps[:], AF.Sigmoid, [pk], [gk + "c%d" % c])
                    P.dma(sc_[dst].rearrange("(c p) s -> p c s", p=128)[:, :, t0:t0 + 512], gt[:],
                          [gk + "c%d" % c for c in range(8)], [], chan="T" + gk, q="gpsimd")

                def latent(col0, nch, gcol0, ring, inv_n):
                    pcs = []
                    pss, pssk = pss_r.next()
                    sqs = []
                    for c in range(nch):
                        ps, pk = proj(col0 + c * 128)
                        sq, sqk = sq_r.next()
                        P.act(sq[:], ps[:], AF.Square, [pk], [sqk])
                        pcs.append((ps, pk)); sqs.append((sq, sqk))
                    P.mm(pss[:], [(ones, sq[:]) for sq, _ in sqs], [k_ for _, k_ in sqs] + ["cb"], [pssk])
                    sd, sdk_ = sd_r.next()
                    P.act(sd[:], pss[:], AF.Sqrt, [pssk], [sdk_], scale=inv_n, bias=EPS)
                    rs, rsk = rs_r.next()
                    P.recip(rs[:], sd[:], [sdk_], [rsk])
                    o, ok = ring.next()
                    for c in range(nch):
                        ps, pk = pcs[c]
                        P.stt(o[:, c, :], ps[:], cols[:, gcol0 + c:gcol0 + c + 1], rs[:], ALU.mult, ALU.mult,
                              [pk, rsk, "cols"], [ok + "c%d" % c])
                    return o, [ok + "c%d" % c for c in range(nch)]
                cqn, cqk = latent(0, 3, GCQ, cqn_r, 1.0 / 384)
                ckvn, ckvk = latent(384, 2, GCKV, ckvn_r, 1.0 / 256)

                pkr, pkrk = proj(640, 128)
                pkrr, pkrrk = proj(704, 128)
                sqkr, sqkrk = sqkr_r.next()
                P.act(sqkr[0:64, :], pkr[0:64, :], AF.Square, [pkrk], [sqkrk])
                u1, u1k = t1_r.next(); u2, u2k = t2_r.next()
                P.tt(u1[:], pkr[0:64, :], cs[:, 0, :], ALU.mult, [pkrk, ck + "c"], [u1k])
                P.tt(u2[:], pkrr[0:64, :], cs[:, 1, :], ALU.mult, [pkrrk, ck + "s"], [u2k])
                P.tt(ka1[:], u1[:], u2[:], ALU.add, [u1k, u2k], ["ka1"], eng="gpsimd")
                u3, u3k = t1_r.next(); u4, u4k = t2_r.next()
                P.tt(u3[:], pkrr[0:64, :], cs[:, 0, :], ALU.mult, [pkrrk, ck + "c"], [u3k])
                P.tt(u4[:], pkr[0:64, :], cs[:, 1, :], ALU.mult, [pkrk, ck + "s"], [u4k])
                P.tt(ka2[:], u3[:], u4[:], ALU.subtract, [u3k, u4k], ["ka2"], eng="gpsimd")
                t1, t1k = t1_r.next(); t2, t2k = t2_r.next()
                P.stt(t1[:], ka1[:], cols[0:64, GKR:GKR + 1], cs[:, 0, :], ALU.mult, ALU.mult,
                      ["ka1", ck + "c", "cols"], [t1k])
                P.stt(t2[:], ka2[:], cols[0:64, GKRS:GKRS + 1], cs[:, 1, :], ALU.mult, ALU.mult,
                      ["ka2", ck + "s", "cols"], [t2k])
                kro, krok = kro_r.next()
                P.tt(kro[:], t1[:], t2[:], ALU.add, [t1k, t2k], [krok], eng="gpsimd")
                P.store(sc_["kmrT"][:, t0:t0 + 512], kro[:], krok)

                qno, qnk = qno_r.next()
                qro, qrk = qro_r.next()
                for hh in range(8):
                    psr, psrk = ps_r.next()
                    P.mm(psr[:, :], [(Wuq[:, k, 1024 + hh * 64:1024 + hh * 64 + 128], cqn[:, k, :]) for k in range(3)],
                         cqk + ["Wuq"], [psrk])
                    psrr, psrrk = ps_r.next()
                    P.mm(psrr[:, :], [(Wuq[:, k, 1536 + hh * 64:1536 + hh * 64 + 128], cqn[:, k, :]) for k in range(3)],
                         cqk + ["Wuq", "Wuqz"], [psrrk])
                    psn, psnk = ps_r.next()
                    P.mm(psn[:], [(Wuq[:, k, hh * 128:(hh + 1) * 128], cqn[:, k, :]) for k in range(3)], cqk + ["Wuq"], [psnk])
                    sqr, sqrk = sqr_r.next()
                    P.act(sqr[0:64, :], psr[0:64, :], AF.Square, [psrk], [sqrk])
                    sq, sqk = sq_r.next()
                    P.act(sq[:], psn[:], AF.Square, [psnk], [sqk])
                    pss, pssk = pss_r.next()
                    P.mm(pss[:], [(ones, sq[:]), (ones, sqr[:])], [sqk, sqrk, "cb"] + ZK, [pssk])
                    sd, sdk_ = sd_r.next()
                    P.act(sd[:], pss[:], AF.Sqrt, [pssk], [sdk_], scale=1.0 / 192, bias=EPS)
                    rs, rsk = rs_r.next()
                    P.recip(rs[:], sd[:], [sdk_], [rsk])
                    P.stt(qno[:, hh, :], psn[:], gx[:, 0:1], rs[:], ALU.mult, ALU.mult, [psnk, rsk, "gx"], [qnk + "h%d" % hh])
                    t1, t1k = t1_r.next(); t2, t2k = t2_r.next(); t3, t3k = t3_r.next()
                    P.stt(t1[:], psr[0:64, :], gx[0:64, 1:2], cs[:, 0, :], ALU.mult, ALU.mult, [psrk, ck + "c", "gx"], [t1k])
                    P.stt(t2[:], psrr[0:64, :], gx[0:64, 2:3], cs[:, 1, :], ALU.mult, ALU.mult, [psrrk, ck + "s", "gx"], [t2k])
                    P.tt(t3[:], t1[:], t2[:], ALU.add, [t1k, t2k], [t3k], eng="gpsimd")
                    P.tt(qro[:, hh, :], t3[:], rs[0:64, :], ALU.mult, [t3k, rsk], [qrk + "h%d" % hh], eng="gpsimd")
                P.dma(sc_["qmnT"].rearrange("h p s -> p h s")[:, :, t0:t0 + 512], qno[:],
                      [qnk + "h%d" % i for i in range(8)], [], chan="T" + qnk, q="gpsimd")
                P.dma(sc_["qmrT"].rearrange("h p s -> p h s")[:, :, t0:t0 + 512], qro[:],
                      [qrk + "h%d" % i for i in range(8)], [], chan="T" + qrk, q="gpsimd")

                kno, knk = kno_r.next()
                pst, pstk = pst_r.next()
                pend_ks = []
                for hh in range(8):
                    ps, pk = ps_r.next()
                    P.mm(ps[:], [(Wuk[:, k, hh * 128:(hh + 1) * 128], ckvn[:, k, :]) for k in range(2)], ckvk + ["Wuk"], [pk])
                    P.act(kno[:, hh, :], ps[:], AF.Copy, [pk], [knk + "h%d" % hh])
                    sq, sqk = sq_r.next()
                    P.act(sq[:], ps[:], AF.Square, [pk], [sqk])

                    def kstat(hh=hh, sq=sq, sqk=sqk):
                        for j in range(4):
                            P.mm(pst[:, j, hh:hh + 1], [(sq[:, j * 128:(j + 1) * 128], cb[:, 128:129]),
                                                        (sqkr[:, j * 128:(j + 1) * 128], cb[:, 128:129])],
                                 [sqk, sqkrk, "cb"] + ZK, [pstk])
                    pend_ks.append(kstat)
                    if len(pend_ks) > 2:
                        pend_ks.pop(0)()
                while pend_ks:
                    pend_ks.pop(0)()
                P.dma(sc_["kmnT"].rearrange("h p s -> p h s")[:, :, t0:t0 + 512], kno[:],
                      [knk + "h%d" % i for i in range(8)], [], chan="T" + knk, q="gpsimd")
                sdk, sdkk = sdk_r.next()
                P.act(sdk[:], pst[:].rearrange("p j h -> p (j h)"), AF.Sqrt, [pstk], [sdkk], scale=1.0 / 192, bias=EPS)
                rk, rkk = rk_r.next()
                P.recip(rk[:].rearrange("p j h -> p (j h)"), sdk[:], [sdkk], [rkk])
                P.store(sc_["rstdk"][t0:t0 + 512, :].rearrange("(j p) h -> p j h", p=128), rk[:], rkk)

                vmo, vmk = vmo_r.next()
                for j in range(4):
                    for n in range(2):
                        ps, pk = ps_r.next()
                        P.mm(ps[:], [(ckvn[:, k, j * 128:(j + 1) * 128], Wuv[:, k, n * 512:(n + 1) * 512]) for k in range(2)],
                             ckvk + ["Wuv"], [pk])
                        P.act(vmo[:, j, n * 512:(n + 1) * 512], ps[:], AF.Copy, [pk], [vmk + "p%d" % (j * 2 + n)])
                P.dma(sc_["vmtok"][t0:t0 + 512, :].rearrange("(j p) c -> p j c", p=128), vmo[:],
                      [vmk + "p%d" % i for i in range(8)], [], chan="T" + vmk, q="gpsimd")
        P.finish()

    def phase2():
        P = Phase(nc, "r")
        cf, cb = consts(P)
        ident = cb[:, 0:128]
        psP_r = P.pring("psP", 1, [128, 512], F32)
        cols = make_cols(P, cf, [[(0, v2(ret_gn_g))]], psP_r.next())
        dst = P.sb("dst", [1, 8], F32)
        P.dma(dst[0:1, 0:4], v1(dec_f), [], ["dst"], chan="Ldst0")
        P.dma(dst[0:1, 4:8], v1(dec_b), [], ["dst"], chan="Ldst1")
        pdc, pdck = psP_r.next()
        P.mm(pdc[:, 0:8], [(cf[0:1, C_ONE:C_ONE + 128], dst[0:1, 0:8])], ["dst", "cf"], [pdck])
        lg = P.sb("lg", [128, 8], F32)
        P.act(lg[:], pdc[:, 0:8], AF.Exp, [pdck], ["lg"], scale=-1.0)
        P.ts(lg[:], lg[:], 1.0, None, ALU.add, ALU.bypass, ["lg"], ["lg"])
        P.act(lg[:], lg[:], AF.Ln, ["lg"], ["lg"])
        P.ts(lg[:], lg[:], -1.0, None, ALU.mult, ALU.bypass, ["lg"], ["lg"])
        DT = P.sb("DT", [128, 4, 128], F32)
        decf = P.sb("decf", [128, 4, 128], F32)
        decb = P.sb("decb", [128, 4, 128], F32)
        kcol = P.sb("kcol", [128, 16], F32)
        e1 = P.sb("e1", [128, 128], F32)
        e2 = P.sb("e2", [128, 128], F32)
        for hh in range(4):
            lf = lg[:, hh:hh + 1]; lb = lg[:, 4 + hh:5 + hh]
            P.act(e1[:], cf[:, C_A:C_A + 128], AF.Exp, ["cf", "lg"], ["e1"], scale=lf)
            P.tt(e1[:], e1[:], cf[:, C_MF:C_MF + 128], ALU.mult, ["e1", "cf"], ["e1"])
            P.act(e2[:], cf[:, C_B:C_B + 128], AF.Exp, ["cf", "lg"], ["e2"], scale=lb)
            P.tt(e2[:], e2[:], cf[:, C_MB:C_MB + 128], ALU.mult, ["e2", "cf"], ["e2"])
            P.tt(DT[:, hh, :], e1[:], e2[:], ALU.add, ["e1", "e2"], ["DT"])
            P.act(decf[:, hh, :], cf[:, C_C1:C_C1 + 128], AF.Exp, ["cf", "lg"], ["decf"], scale=lf)
            P.act(decb[:, hh, :], cf[:, C_C2:C_C2 + 128], AF.Exp, ["cf", "lg"], ["decb"], scale=lb)
            P.act(kcol[:, hh:hh + 1], cf[:, C_SM:C_SM + 1], AF.Exp, ["cf", "lg"], ["kcol"], scale=lf)
            P.act(kcol[:, 4 + hh:5 + hh], cf[:, C_SM + 1:C_SM + 2], AF.Exp, ["cf", "lg"], ["kcol"], scale=lb)
            P.act(kcol[:, 8 + hh:9 + hh], cf[:, C_SM + 2:C_SM + 3], AF.Exp, ["cf", "lg"], ["kcol"], scale=lf)
            P.act(kcol[:, 12 + hh:13 + hh], cf[:, C_SM + 2:C_SM + 3], AF.Exp, ["cf", "lg"], ["kcol"], scale=lb)
        Wro = P.sb("Wro", [128, 8, 1024], BF16)
        stg = P.ring("stg", 2, [128, 1024], F32)
        for k in range(8):
            st, sk = stg.next()
            P.load(st[:], w_ret_o[k * 128:(k + 1) * 128, :], sk)
            P.wcast(Wro[:, k, :], st[:], cols[:, k:k + 1], 1.0, [sk, "cols"], ["Wro"])

        kt_r = P.ring("kt", 2, [128, 4, 512], BF16)
        v_r = P.ring("v", 2, [128, 4, 1024], BF16)
        kd_r = P.ring("kd", 2, [128, 4, 128], BF16)
        psU_r = P.pring("psU", 1, [128, 512], F32)
        St = P.sb("St", [128, 4, 256], F32)
        sbb_r = P.ring("sbb", 3, [128, 1024], BF16)

        kfb = {}
        for col0 in (0, 4):
            t_ = P.sb("kfb%d" % col0, [128, 4, 128], F32)
            for hh in range(4):
                P.act(t_[:, hh, :], cf[:, C_ONE:C_ONE + 128], AF.Copy, ["cf", "kcol"], ["kfb%d" % col0],
                      scale=kcol[:, col0 + hh:col0 + hh + 1])
            kfb[col0] = t_

        def kdec(kt, ktk, j, col0):
            kd, kdk = kd_r.next()
            P.tt(kd[:], kt[:, j, :].rearrange("p (h d) -> p h d", d=128), kfb[col0][:], ALU.mult,
                 [ktk, "kfb%d" % col0], [kdk], eng="gpsimd")
            return kd, [kdk]

        def state_update(kd, kdk, v, vk, j, gcol0):
            for hp in range(2):
                psU, psUk = psU_r.next()
                for h2 in range(2):
                    hh = hp * 2 + h2
                    P.mm(psU[:, h2 * 256:(h2 + 1) * 256], [(kd[:, hh, :], v[:, j, hh * 256:(hh + 1) * 256])], kdk + [vk], [psUk])
                for h2 in range(2):
                    hh = hp * 2 + h2
                    P.stt(St[:, hh, :], St[:, hh, :], kcol[:, gcol0 + hh:gcol0 + hh + 1], psU[:, h2 * 256:(h2 + 1) * 256],
                          ALU.mult, ALU.add, ["St%d" % hh, psUk, "kcol"], ["St%d" % hh])

        for si, S in enumerate(SEQS):
            sc_ = SC[si]
            NG = S // 512
            P.memset(St[:], 0.0, ["St%d" % i_ for i_ in range(4)])
            for g in range(NG - 1, -1, -1):
                t0 = g * 512
                kt, ktk = kt_r.next()
                P.load(kt[:], sc_["ktok"][t0:t0 + 512, :].rearrange("(j p) c -> p j c", p=128), ktk)
                v, vk = v_r.next()
                P.load(v[:], sc_["vtok"][t0:t0 + 512, :].rearrange("(j p) c -> p j c", p=128), vk)
                for j in range(3, -1, -1):
                    n = g * 4 + j
                    sbb, sbk = sbb_r.next()
                    P.cp(sbb[:], St[:].rearrange("p h d -> p (h d)"), ["St%d" % i_ for i_ in range(4)], [sbk])
                    P.store(sc_["sb"][n, :, :], sbb[:], sbk, dkey="sbd%d_%d" % (si, n))
                    if n > 0:
                        kd, kdk = kdec(kt, ktk, j, 4)
                        state_update(kd, kdk, v, vk, j, 12)
            P.memset(St[:], 0.0, ["St%d" % i_ for i_ in range(4)])
            fw = getattr(P, "_fw", None)
            if fw is None:
                fw = dict(
                    qT=P.ring("qT", 2, [128, 4, 512], BF16), kT=P.ring("kT", 2, [128, 4, 512], BF16),
                    sbl=P.ring("sbl", 2, [128, 4, 1024], BF16), rgs=P.ring("rgs", 2, [128, 8, 512], BF16),
                    gr=P.ring("gr", 2, [128, 8, 512], BF16),
                    psS=P.pring("psS", 1, [128, 4, 128], F32), psO=P.pring("psO", 2, [128, 1024], F32),
                    tp=P.pring("tp", 1, [128, 8, 128], BF16), psP=psP_r,
                    pT=P.ring("pT", 2, [128, 4, 128], BF16), qf=P.ring("qf", 2, [128, 4, 128], BF16),
                    qb=P.ring("qb", 2, [128, 4, 128], BF16), Sfb=P.ring("Sfb", 3, [128, 4, 256], BF16),
                    st=P.ring("st", 2, [128, 32], F32), retn=P.ring("retn", 2, [128, 1024], BF16),
                    retg=P.ring("retg", 2, [128, 8, 512], BF16), m1=P.ring("m1", 2, [128, 8, 512], BF16),
                    junk=P.sb("junk", [128, 256], BF16),
                )
                P._fw = fw
            Sfb, Sfk = fw["Sfb"].next()
            P.cp(Sfb[:], St[:], ["St%d" % i_ for i_ in range(4)], [Sfk])
            tails = []
            for g in range(NG):
                t0 = g * 512
                qT, qTk = fw["qT"].next()
                P.load(qT[:], sc_["qrT"].rearrange("(h p) s -> p h s", p=128)[:, :, t0:t0 + 512], qTk)
                kT, kTk = fw["kT"].next()
                P.load(kT[:], sc_["krT"].rearrange("(h p) s -> p h s", p=128)[:, :, t0:t0 + 512], kTk)
                kt, ktk = kt_r.next()
                P.load(kt[:], sc_["ktok"][t0:t0 + 512, :].rearrange("(j p) c -> p j c", p=128), ktk)
                v, vk = v_r.next()
                P.load(v[:], sc_["vtok"][t0:t0 + 512, :].rearrange("(j p) c -> p j c", p=128), vk)
                sbl, sblk = fw["sbl"].next()
                P.dma(sbl[:], sc_["sb"][g * 4:(g + 1) * 4, :, :].rearrange("j p c -> p j c"),
                      ["sbd%d_%d" % (si, g * 4 + j) for j in range(4)], [sblk], chan="L" + sblk)
                rgs, rgsk = fw["rgs"].next()
                P.load(rgs[:], sc_["rgsT"].rearrange("(c p) s -> p c s", p=128)[:, :, t0:t0 + 512], rgsk)
                gr, grk = fw["gr"].next()
                P.load(gr[:], sc_["grT"].rearrange("(c p) s -> p c s", p=128)[:, :, t0:t0 + 512], grk)
                retg, retgk = fw["retg"].next()
                for j in range(4):
                    sl = slice(j * 128, (j + 1) * 128)
                    psS, psSk = fw["psS"].next()
                    for hh in range(4):
                        P.mm(psS[:, hh, :], [(kT[:, hh, sl], qT[:, hh, sl])], [kTk, qTk], [psSk])
                    pT, pTk = fw["pT"].next()
                    P.tt(pT[:], psS[:], DT[:], ALU.mult, [psSk, "DT"], [pTk])
                    qf, qfk = fw["qf"].next()
                    P.tt(qf[:], qT[:, :, sl], decf[:], ALU.mult, [qTk, "decf"], [qfk], eng="gpsimd")
                    qb, qbk = fw["qb"].next()
                    P.tt(qb[:], qT[:, :, sl], decb[:], ALU.mult, [qTk, "decb"], [qbk], eng="gpsimd")
                    Sfb_old, Sfk_old = Sfb, Sfk
                    kd, kdk = kdec(kt, ktk, j, 0)
                    state_update(kd, kdk, v, vk, j, 8)
                    Sfb, Sfk = fw["Sfb"].next()
                    P.cp(Sfb[:], St[:], ["St%d" % i_ for i_ in range(4)], [Sfk])
                    psO, psOk = fw["psO"].next()
                    for hh in range(4):
                        vs = v[:, j, hh * 256:(hh + 1) * 256]
                        P.mm(psO[:, hh * 256:(hh + 1) * 256],
                             [(pT[:, hh, :], vs), (qf[:, hh, :], Sfb_old[:, hh, :]),
                              (qb[:, hh, :], sbl[:, j, hh * 256:(hh + 1) * 256])],
                             [pTk, vk, qfk, Sfk_old, qbk, sblk], [psOk])
                    while tails:
                        tails.pop(0)()
                    st, stk = fw["st"].next()
                    for hh in range(4):
                        o = psO[:, hh * 256:(hh + 1) * 256]
                        P.act(fw["junk"][:], o, AF.Copy, [psOk], ["rjunk", stk + "a"], accum=st[:, hh:hh + 1])
                    for hh in range(4):
                        o = psO[:, hh * 256:(hh + 1) * 256]
                        P.act(fw["junk"][:], o, AF.Square, [psOk], ["rjunk", stk + "a2"], accum=st[:, 4 + hh:5 + hh])
                    P.ts(st[:, 8:12], st[:, 0:4], 1.0 / 256, None, ALU.mult, ALU.bypass, [stk + "a"], [stk + "b"])
                    P.tt(st[:, 12:16], st[:, 8:12], st[:, 8:12], ALU.mult, [stk + "b"], [stk + "c"])
                    P.stt(st[:, 16:20], st[:, 4:8], 1.0 / 256, st[:, 12:16], ALU.mult, ALU.subtract, [stk + "a2", stk + "c"], [stk + "d"])
                    P.act(st[:, 20:24], st[:, 16:20], AF.Sqrt, [stk + "d"], [stk + "e"], bias=EPS)
                    P.recip(st[:, 24:28], st[:, 20:24], [stk + "e"], [stk + "f"])
                    P.stt(st[:, 28:32], st[:, 8:12], -1.0, st[:, 24:28], ALU.mult, ALU.mult, [stk + "b", stk + "f"], [stk + "g"])
                    retn, retnk = fw["retn"].next()
                    for hh in range(4):
                        P.act(retn[:, hh * 256:(hh + 1) * 256], psO[:, hh * 256:(hh + 1) * 256], AF.Identity,
                              [psOk, stk + "f", stk + "g"], [retnk], scale=st[:, 24 + hh:25 + hh], bias=st[:, 28 + hh:29 + hh])

                    def tail(j=j, sl=sl, retn=retn, retnk=retnk, retg=retg, retgk=retgk, rgs=rgs, rgsk=rgsk,
                             gr=gr, grk=grk, t0=t0):
                        tp, tpk = fw["tp"].next()
                        for c in range(8):
                            P.tr(tp[:, c, :], retn[:, c * 128:(c + 1) * 128], ident, [retnk, "cb"], [tpk])
                        P.tt(retg[:, :, sl], tp[:], rgs[:, :, sl], ALU.mult, [tpk, rgsk], [retgk + "j%d" % j])
                        if j == 3:
                            m1, m1k = fw["m1"].next()
                            for c in range(8):
                                psP, psPk = fw["psP"].next()
                                P.mm(psP[:], [(Wro[:, k, c * 128:(c + 1) * 128], retg[:, k, :]) for k in range(8)],
                                     [retgk + "j%d" % jj for jj in range(4)] + ["Wro"], [psPk])
                                P.tt(m1[:, c, :], psP[:], gr[:, c, :], ALU.mult, [psPk, grk], [m1k + "c%d" % c])
                            P.dma(sc_["m1T"].rearrange("(c p) s -> p c s", p=128)[:, :, t0:t0 + 512], m1[:],
                                  [m1k + "c%d" % c for c in range(8)], [], chan="T" + m1k, q="gpsimd")
                    tails.append(tail)
            while tails:
                tails.pop(0)()
        P.finish()

    def phase3():
        P = Phase(nc, "m")
        cf, cb = consts(P)
        ones = cb[:, 128:256]
        onesf = cf[:, C_ONE:C_ONE + 128]
        Kn_r = P.ring("Kn", 2, [128, SMAX], BF16)
        V_r = P.ring("V", 2, [128, SMAX // 128, 128], BF16)
        Kr = P.sb("Kr", [128, SMAX], BF16)
        P.memset(Kr[64:128, :], 0.0, ["Krz"])
        rk = P.sb("rk", [128, SMAX // 128, 8], F32)
        Qn_r = P.ring("Qn", 3, [128, 512], BF16)
        Qr_r = P.ring("Qr", 3, [128, 512], BF16)
        for i_, t_ in enumerate(Qr_r.tiles):
            P.memset(t_[64:128, :], 0.0, ["Qrz%d" % i_])
        st_r = P.pring("st", 4, [128, 512], F32)
        o_r = P.pring("o", 2, [128, 512], F32)
        l_r = P.pring("l", 2, [128, 512], F32)
        p_r = P.ring("p", 6, [128, 512], BF16)
        acc_r = [P.ring("acc0", 2, [128, 512], F32), P.ring("acc1", 2, [128, 512], F32)]
        accs_r = P.ring("accs", 2, [128, 512], F32)
        rl_r = P.ring("rl", 2, [128, 512], F32)
        at_r = P.ring("at", 3, [128, 512], BF16)
        LAG = 2
        units = [(si, hh, qg, kt) for si, S in enumerate(SEQS) for hh in range(8)
                 for qg in range(S // 512) for kt in range(S // 128)]
        cur = {}
        pend = []

        def stage_a(u):
            si, hh, qg, kt = u
            S = SEQS[si]; sc_ = SC[si]; NK = S // 128
            if hh == 0 and qg == 0 and kt == 0:
                P.load(Kr[0:64, 0:S], sc_["kmrT"][:, :], "Kr")
                P.load(rk[:, 0:NK, :], sc_["rstdk"].rearrange("(t p) h -> p t h", p=128), "rk")
            if qg == 0 and kt == 0:
                Kn, Knk = Kn_r.next()
                P.load(Kn[:, 0:S], sc_["kmnT"][hh, :, :], Knk)
                V, Vk = V_r.next()
                P.load(V[:, 0:NK, :], sc_["vmtok"][:, hh * 128:(hh + 1) * 128].rearrange("(t p) c -> p t c", p=128), Vk)
                cur["K"] = (Kn, Knk, V, Vk)
            if kt == 0:
                q0 = qg * 512
                Qn, Qnk = Qn_r.next()
                P.load(Qn[:], sc_["qmnT"][hh, :, q0:q0 + 512], Qnk)
                Qr, Qrk = Qr_r.next()
                P.load(Qr[0:64, :], sc_["qmrT"][hh, :, q0:q0 + 512], Qrk)
                cur["Q"] = (Qn, Qnk, Qr, Qrk)
            Kn, Knk, V, Vk = cur["K"]
            Qn, Qnk, Qr, Qrk = cur["Q"]
            ks = slice(kt * 128, (kt + 1) * 128)
            st, stk = st_r.next()
            P.mm(st[:], [(Kn[:, ks], Qn[:]), (Kr[:, ks], Qr[:])], [Knk, "Kr", "Krz", Qnk, Qrk] + ["Qrz%d" % i_ for i_ in range(3)], [stk])
            p, pk = p_r.next()
            P.act(p[:], st[:], AF.Exp, [stk, "rk"], [pk], scale=rk[:, kt, hh:hh + 1])
            pend.append((u, p, pk, V, Vk))

        def stage_b():
            u, p, pk, V, Vk = pend.pop(0)
            si, hh, qg, kt = u
            S = SEQS[si]; sc_ = SC[si]; NK = S // 128
            if kt == 0:
                cur["o"] = o_r.next()
                cur["l"] = l_r.next()
                cur["acc"] = [acc_r[0].next(), acc_r[1].next()]
                cur["na"] = 0
            o, ok = cur["o"]
            l, lk = cur["l"]
            P.S.op("tensor", lambda e, o=o, V=V, kt=kt, p=p, NK=NK: e.matmul(
                o[:], V[:, kt, :], p[:], start=(kt == 0), stop=(kt == NK - 1)), [Vk, pk], [ok])
            if kt % 4 == 3:
                P.S.op("tensor", lambda e, l=l, p=p, kt=kt: e.matmul(
                    l[:], ones, p[:], start=(kt == 3), stop=False), [pk, "cb"], [lk])
            else:
                na = cur["na"]; cur["na"] = na + 1
                a, ak = cur["acc"][na % 2]
                if na < 2:
                    P.cp(a[:], p[:], [pk], [ak])
                else:
                    P.tt(a[:], a[:], p[:], ALU.add, [ak, pk], [ak])
            if kt == NK - 1:
                q0 = qg * 512
                (a0, a0k), (a1, a1k) = cur["acc"]
                asum, asumk = accs_r.next()
                P.tt(asum[:], a0[:], a1[:], ALU.add, [a0k, a1k], [asumk])
                P.S.op("tensor", lambda e, l=l, asum=asum: e.matmul(
                    l[:], onesf, asum[:], start=False, stop=True), [asumk, "cf"], [lk])
                rl, rlk = rl_r.next()
                P.recip(rl[:], l[:], [lk], [rlk])
                at, atk = at_r.next()
                P.tt(at[:], o[:], rl[:], ALU.mult, [ok, rlk], [atk])
                P.store(sc_["attnT"][hh * 128:(hh + 1) * 128, q0:q0 + 512], at[:], atk)

        for i in range(len(units) + LAG):
            if i < len(units):
                stage_a(units[i])
            if i >= LAG:
                stage_b()
        P.finish()

    def phase4():
        P = Phase(nc, "o")
        cf, cb = consts(P, need_cf=False)
        ident = cb[:, 0:128]
        Wmo = P.sb("Wmo", [128, 8, 1024], BF16)
        Wo = P.sb("Wo", [128, 8, 1024], BF16)
        stg = P.ring("stg", 2, [128, 1024], F32)
        for (Wt, src, wk) in ((Wmo, w_mla_o, "Wmo"), (Wo, w_out, "Wo")):
            for k in range(8):
                st, sk = stg.next()
                P.load(st[:], src[k * 128:(k + 1) * 128, :], sk)
                P.wcast(Wt[:, k, :], st[:], None, 1.0, [sk], [wk])
        at_r = P.ring("at", 2, [128, 8, 512], BF16)
        ga_r = P.ring("ga", 2, [128, 8, 512], BF16)
        m1_r = P.ring("m1", 2, [128, 8, 512], BF16)
        xt_r = P.ring("xt", 4, [128, D], F32)
        ps_r = P.pring("ps", 6, [128, 512], F32)
        pT_r = P.pring("pT", 2, [128, 8, 128], BF16)
        tm_r = P.ring("tm", 2, [128, 512], F32)
        mg_r = P.ring("mg", 2, [128, 8, 512], BF16)
        x1_r = P.ring("x1", 2, [128, D], F32)
        junk = P.sb("junk", [128, D], BF16)
        st_r = P.ring("stat", 2, [128, 4], F32)
        h_r = P.ring("h", 2, [128, D], BF16)
        hT_r = P.ring("hT", 2, [128, 8, 512], BF16)
        for si, S in enumerate(SEQS):
            sc_ = SC[si]
            h2v = sc_["h2T"].rearrange("(k p) s -> p k s", p=128)
            for g in range(S // 512):
                t0 = g * 512
                at, atk = at_r.next()
                P.load(at[:], sc_["attnT"].rearrange("(c p) s -> p c s", p=128)[:, :, t0:t0 + 512], atk)
                ga, gak = ga_r.next()
                P.load(ga[:], sc_["gaT"].rearrange("(c p) s -> p c s", p=128)[:, :, t0:t0 + 512], gak)
                m1, m1k = m1_r.next()
                P.load(m1[:], sc_["m1T"].rearrange("(c p) s -> p c s", p=128)[:, :, t0:t0 + 512], m1k)
                xs = []
                for j in range(4):
                    xt, xk = xt_r.next()
                    P.load(xt[:], X[si][t0 + j * 128:t0 + (j + 1) * 128, :], xk)
                    xs.append((xt, xk))
                mg, mgk = mg_r.next()
                for c in range(8):
                    ps, pk = ps_r.next()
                    P.mm(ps[:], [(Wmo[:, k, c * 128:(c + 1) * 128], at[:, k, :]) for k in range(8)], [atk, "Wmo"], [pk])
                    tm, tmk = tm_r.next()
                    P.tt(tm[:], ps[:], ga[:, c, :], ALU.mult, [pk, gak], [tmk])
                    P.tt(mg[:, c, :], tm[:], m1[:, c, :], ALU.add, [tmk, m1k], [mgk + "c%d" % c], eng="gpsimd")
                mgks = [mgk + "c%d" % c for c in range(8)]
                hT, hk = hT_r.next()
                pend_tr = []
                for j in range(4):
                    xt, xk = xs[j]
                    x1, x1k = x1_r.next()
                    for n in range(2):
                        ps, pk = ps_r.next()
                        P.mm(ps[:], [(mg[:, k, j * 128:(j + 1) * 128], Wo[:, k, n * 512:(n + 1) * 512]) for k in range(8)],
                             mgks + ["Wo"], [pk])
                        P.tt(x1[:, n * 512:(n + 1) * 512], ps[:], xt[:, n * 512:(n + 1) * 512], ALU.add, [pk, xk], [x1k + "n%d" % n])
                    x1ks = [x1k + "n0", x1k + "n1"]
                    P.dma(Y[si][t0 + j * 128:t0 + (j + 1) * 128, :], x1[:], x1ks, [], chan="T" + x1k, q="gpsimd")
                    st, stk = st_r.next()
                    P.act(junk[:], x1[:], AF.Square, x1ks, ["junk", stk + "a"], accum=st[:, 0:1])
                    P.act(st[:, 1:2], st[:, 0:1], AF.Sqrt, [stk + "a"], [stk + "b"], scale=1.0 / D, bias=EPS)
                    P.recip(st[:, 2:3], st[:, 1:2], [stk + "b"], [stk + "c"])
                    h, hhk = h_r.next()
                    P.act(h[:], x1[:], AF.Copy, x1ks + [stk + "c"], [hhk], scale=st[:, 2:3])

                    def tr_tail(j=j, h=h, hhk=hhk, hT=hT, hk=hk):
                        pT, pk = pT_r.next()
                        for k in range(8):
                            P.tr(pT[:, k, :], h[:, k * 128:(k + 1) * 128], ident, [hhk, "cb"], [pk])
                        P.cp(hT[:, :, j * 128:(j + 1) * 128], pT[:], [pk], [hk])
                    if pend_tr:
                        pend_tr.pop(0)()
                    pend_tr.append(tr_tail)
                while pend_tr:
                    pend_tr.pop(0)()
                P.store(h2v[:, :, t0:t0 + 512], hT[:], hk)
        P.finish()

    def phase5():
        P = Phase(nc, "f")
        cf, cb = consts(P)
        pu_r = P.pring("pu", 4, [128, 512], F32)
        cols = make_cols(P, cf, [[(0, v2(g_ffn))]], pu_r.next())
        cwp = P.es.enter_context(nc.psum_tensor("f_cwp", [128, 44, 4], F32))
        stg = P.ring("stg", 2, [128, 1408], F32)
        for n in range(4):
            st, sk = stg.next()
            P.dma(st[0:3, :], conv_w[:, n * 1408:(n + 1) * 1408], [], [sk], chan="Lcw0" + sk)
            P.dma(st[3:4, :], v1(conv_b)[:, n * 1408:(n + 1) * 1408], [], [sk], chan="Lcw1" + sk)
            for c in range(11):
                P.mm(cwp[:, n * 11 + c, :], [(st[0:4, c * 128:(c + 1) * 128], cf[0:4, C_ID:C_ID + 4])], [sk, "cf"], ["cwp"])
        cw = P.sb("cw", [128, 44, 4], F32)
        P.cp(cw[:], cwp[:], ["cwp"], ["cw"])
        Wup = P.sb("Wup", [128, 8, 5632], BF16)
        Wd = P.sb("Wd", [128, 22, 1024], BF16)
        for k in range(8):
            for n in range(4):
                st, sk = stg.next()
                P.load(st[:], w_up[k * 128:(k + 1) * 128, n * 1408:(n + 1) * 1408], sk)
                P.wcast(Wup[:, k, n * 1408:(n + 1) * 1408], st[:], cols[:, k:k + 1], 1.0, [sk, "cols"], ["Wup"])
        for k in range(22):
            st, sk = stg.next()
            P.load(st[:, 0:1024], w_down[k * 128:(k + 1) * 128, :], sk)
            P.wcast(Wd[:, k, :], st[:, 0:1024], None, 1.0, [sk], ["Wd"])
        hT_r = P.ring("hT", 1, [128, 8, 512], BF16)
        pd_r = P.pring("pd", 3, [128, 512], F32)
        ta_r = P.ring("ta", 2, [128, 512], F32)
        tb_r = P.ring("tb", 2, [128, 512], F32)
        sa_r = P.ring("sa", 2, [128, 512], F32)
        act_r = P.ring("act", 1, [128, 22, 512], BF16)
        x1_r = P.ring("x1", 2, [128, D], F32)
        yo_r = P.ring("yo", 2, [128, D], F32)
        for si, S in enumerate(SEQS):
            sc_ = SC[si]
            h2v = sc_["h2T"].rearrange("(k p) s -> p k s", p=128)
            for (t0, n) in ffn_groups(S):
                hT, hk = hT_r.next()
                lo = max(t0 - 1, 0); hi = min(t0 + n + 1, S)
                P.load(hT[:, :, lo - (t0 - 1):hi - (t0 - 1)], h2v[:, :, lo:hi], hk)
                if t0 == 0:
                    P.memset(hT[:, :, 0:1], 0.0, [hk])
                if t0 + n == S:
                    P.memset(hT[:, :, n + 1:n + 2], 0.0, [hk])
                act, actk = act_r.next()
                for c in range(22):
                    def up(ch):
                        pu, puk = pu_r.next()
                        P.mm(pu[:, 0:n + 2], [(Wup[:, k, ch * 128:(ch + 1) * 128], hT[:, k, 0:n + 2]) for k in range(8)],
                             [hk, "Wup"], [puk])
                        return pu, puk

                    def conv(pu, puk, ch, ring):
                        t, tk = ring.next()
                        P.act(t[:, 0:n], pu[:, 1:n + 1], AF.Identity, [puk, "cw"], [tk], scale=cw[:, ch, 1:2], bias=cw[:, ch, 3:4])
                        P.stt(t[:, 0:n], pu[:, 0:n], cw[:, ch, 0:1], t[:, 0:n], ALU.mult, ALU.add, [puk, "cw", tk], [tk])
                        P.stt(t[:, 0:n], pu[:, 2:n + 2], cw[:, ch, 2:3], t[:, 0:n], ALU.mult, ALU.add, [puk, "cw", tk], [tk])
                        return t, tk
                    pa, pak = up(c)
                    pb, pbk = up(22 + c)
                    ta, tak = conv(pa, pak, c, ta_r)
                    tb, tbk = conv(pb, pbk, 22 + c, tb_r)
                    sa, sak = sa_r.next()
                    P.act(sa[:, 0:n], ta[:, 0:n], AF.Silu, [tak], [sak])
                    P.tt(act[:, c, 0:n], sa[:, 0:n], tb[:, 0:n], ALU.mult, [sak, tbk], [actk + "c%d" % c], eng="gpsimd")
                actks = [actk + "c%d" % c for c in range(22)]
                m0 = 0
                while m0 < n:
                    m = min(128, n - m0)
                    x1, x1k = x1_r.next()
                    P.load(x1[0:m, :], Y[si][t0 + m0:t0 + m0 + m, :], x1k)
                    yo, yok = yo_r.next()
                    for nn in range(2):
                        pd, pdk = pd_r.next()
                        P.mm(pd[0:m, :], [(act[:, k, m0:m0 + m], Wd[:, k, nn * 512:(nn + 1) * 512]) for k in range(22)],
                             actks + ["Wd"], [pdk])
                        P.tt(yo[0:m, nn * 512:(nn + 1) * 512], pd[0:m, :], x1[0:m, nn * 512:(nn + 1) * 512], ALU.add,
                             [pdk, x1k], [yok + "n%d" % nn])
                    P.dma(Y[si][t0 + m0:t0 + m0 + m, :], yo[0:m, :], [yok + "n0", yok + "n1"], [], chan="T" + yok, q="gpsimd")
                    m0 += m
        P.finish()

    import os
    nph = int(os.environ.get("KPH", "6"))
    for ph in (phase1a, phase1b, phase2, phase3, phase4, phase5)[:nph]:
        ph()
    return nc


def host_consts(SMAX):
    i = np.arange(128, dtype=np.float32)
    cf = np.zeros((128, NCF), np.float32)
    diff = i[None, :] - i[:, None]
    cf[:, C_A:C_A + 128] = np.maximum(diff, 0)
    cf[:, C_B:C_B + 128] = np.maximum(-diff, 0)
    cf[:, C_MF:C_MF + 128] = (diff >= 0)
    cf[:, C_MB:C_MB + 128] = (diff < 0)
    cf[:, C_C1:C_C1 + 128] = (i + 1.0)[None, :]
    cf[:, C_C2:C_C2 + 128] = (128.0 - i)[None, :]
    cf[:, C_ID:C_ID + 128] = np.eye(128, dtype=np.float32)
    cf[:, C_ONE:C_ONE + 128] = 1.0
    cf[:, C_SM] = 127.0 - i
    cf[:, C_SM + 1] = i
    cf[:, C_SM + 2] = 128.0
    cb = np.zeros((128, 256), np.float32)
    cb[:, 0:128] = np.eye(128)
    cb[:, 128:256] = 1.0
    cb = cb.astype(ml_dtypes.bfloat16)
    pos = np.arange(SMAX, dtype=np.float32)

    def tab(d):
        inv = (np.float32(10000.0) ** (-np.arange(0, d, 2, dtype=np.float32) / np.float32(d))).astype(np.float32)
        ang = (pos[:, None] * inv[None, :]).astype(np.float32)
        c = np.cos(ang).astype(np.float32).T
        s = np.sin(ang).astype(np.float32).T
        return (np.ascontiguousarray(np.concatenate([c, c], 0)), np.ascontiguousarray(np.concatenate([s, s], 0)))
    cosr, sinr = tab(128)
    cosm, sinm = tab(64)
    return dict(cf=cf, cb=cb, cosr=cosr, sinr=sinr, cosm=cosm, sinm=sinm)


_CACHE = {}


def run(x_list, weights, n_cores=8):
    SEQS = tuple(int(x.shape[1]) for x in x_list)
    if SEQS not in _CACHE:
        _CACHE[SEQS] = build(SEQS)
    nc = _CACHE[SEQS]
    hc = host_consts(max(SEQS))
    w = {}
    for k, v in weights.items():
        a = np.asarray(v, dtype=np.float32)
        w[k] = np.ascontiguousarray(a.reshape(a.shape[1:]))
    in_maps = []
    for c in range(n_cores):
        m = dict(w)
        m.update(hc)
        for i, x in enumerate(x_list):
            m["x%d" % i] = np.ascontiguousarray(np.asarray(x[c], dtype=np.float32))
        in_maps.append(m)
    res = run_bass_kernel_spmd(nc, in_maps, core_ids=list(range(n_cores)))
    global LAST_RES
    LAST_RES = res
    outs = []
    for i in range(len(x_list)):
        outs.append(np.stack([np.asarray(res.results[c]["y%d" % i], dtype=np.float32) for c in range(n_cores)], 0))
    return tuple(outs)


def kernel(x_prompt, x_sample, **weights):
    return run([np.asarray(x_prompt), np.asarray(x_sample)], weights)
```

```python
import contextlib
import math
import numpy as np
import ml_dtypes
import concourse.bass as bass
import concourse.mybir as mybir
from concourse.bass_utils import run_bass_kernel_spmd

F32 = mybir.dt.float32
BF16 = mybir.dt.bfloat16
AF = mybir.ActivationFunctionType
ALU = mybir.AluOpType

D = 1024
IN_W = 5824
EPS = 1e-6
SAME_ENGINE_SYNC = True
ENGS = ("sync", "scalar", "vector", "gpsimd", "tensor")

C_A, C_B, C_MF, C_MB, C_C1, C_C2, C_ID, C_ONE, C_SM = 0, 128, 256, 384, 512, 640, 768, 896, 1024
NCF = 1032


class Op:
    __slots__ = ("eng", "fn", "chan", "inc", "waits", "signal", "val", "idx")

    def __init__(self, eng, fn, chan, inc):
        self.eng = eng; self.fn = fn; self.chan = chan; self.inc = inc
        self.waits = {}; self.signal = False; self.val = None; self.idx = None


class Sched:
    def __init__(self, nc, tag=""):
        self.nc = nc
        self.tag = tag
        self.ops = {e: [] for e in ENGS}
        self.chan_ops = {}
        self.last_w = {}
        self.readers = {}
        self.n = 0

    def op(self, eng, fn, reads=(), writes=(), chan=None):
        is_dma = chan is not None
        if chan is None:
            chan = "E_" + eng
        o = Op(eng, fn, chan, 16 if is_dma else 1)
        o.idx = self.n; self.n += 1
        deps = []
        for b in reads:
            w = self.last_w.get(b)
            if w is not None:
                deps.append(w)
        for b in writes:
            w = self.last_w.get(b)
            if w is not None:
                deps.append(w)
            deps.extend(self.readers.get(b, ()))
        for d in deps:
            if d is o:
                continue
            if d.eng == eng and d.chan == chan:
                if eng == "tensor" or not SAME_ENGINE_SYNC:
                    continue
            cur = o.waits.get(d.chan)
            if cur is None or cur.idx < d.idx:
                o.waits[d.chan] = d
        for b in writes:
            self.last_w[b] = o
            self.readers[b] = []
        for b in reads:
            self.readers.setdefault(b, []).append(o)
        self.ops[eng].append(o)
        self.chan_ops.setdefault(chan, []).append(o)
        return o

    def emit(self, es):
        nc = self.nc
        for e in ENGS:
            for o in self.ops[e]:
                for d in o.waits.values():
                    d.signal = True
        for c, lst in self.chan_ops.items():
            lst[-1].signal = True
            if lst[0].inc == 16:
                for o in lst:
                    o.signal = True
        sems = {}
        finals = {}
        for c, lst in self.chan_ops.items():
            sems[c] = nc.alloc_semaphore(name="s_" + self.tag + "_" + c)
            v = 0
            for o in lst:
                if o.signal:
                    v += o.inc
                    o.val = v
            finals[c] = v
        block = es.enter_context(nc.Block())

        def run(engname):
            def body(eng):
                waited = {}
                for o in self.ops[engname]:
                    for c, d in o.waits.items():
                        if waited.get(c, 0) >= d.val:
                            continue
                        eng.wait_ge(sems[c], d.val)
                        waited[c] = d.val
                    ins = o.fn(eng)
                    if o.signal:
                        ins.then_inc(sems[o.chan], o.inc)
                for c, v in finals.items():
                    if v > 0 and waited.get(c, 0) < v:
                        eng.wait_ge(sems[c], v)
            return body
        block.sync(run("sync"))
        block.scalar(run("scalar"))
        block.vector(run("vector"))
        block.gpsimd(run("gpsimd"))
        block.tensor(run("tensor"))


class Ring:
    def __init__(self, name, tiles):
        self.name = name; self.tiles = tiles; self.i = -1

    def next(self):
        self.i = (self.i + 1) % len(self.tiles)
        return self.tiles[self.i], "%s%d" % (self.name, self.i)


class Phase:
    def __init__(self, nc, name):
        self.nc = nc; self.name = name
        self.es = contextlib.ExitStack()
        self.es.enter_context(nc.cleanup_on_exit())
        self.es.callback(nc.all_engine_barrier)
        self.S = Sched(nc, name)
        self.wtog = 0

    def sb(self, name, shape, dt):
        return self.es.enter_context(self.nc.sbuf_tensor(self.name + name, shape, dt))

    def ring(self, name, n, shape, dt):
        return Ring(name, [self.sb("%s_%d" % (name, i), shape, dt) for i in range(n)])

    def pring(self, name, n, shape, dt):
        return Ring(name, [self.es.enter_context(self.nc.psum_tensor("%s%s_%d" % (self.name, name, i), shape, dt))
                           for i in range(n)])

    def dma(self, out, in_, reads, writes, chan, q="sync", slow=False):
        if slow:
            f = lambda e: e.dma_start(out=out, in_=in_, allow_slow_non_contiguous=True)
        else:
            f = lambda e: e.dma_start(out=out, in_=in_)
        return self.S.op(q, f, reads, writes, chan=chan)

    def load(self, out, in_, key, slow=False):
        return self.dma(out, in_, (), [key], chan="L" + key, q="sync", slow=slow)

    def store(self, out, in_, key, dkey=None):
        return self.dma(out, in_, [key], [dkey] if dkey else (), chan="T" + key, q="gpsimd")

    def act(self, out, in_, func, reads, writes, scale=None, bias=None, accum=None):
        kw = {}
        if scale is not None:
            kw["scale"] = scale
        if bias is not None:
            kw["bias"] = bias
        if accum is not None:
            kw["accum_out"] = accum
        return self.S.op("scalar", lambda e: e.activation(out=out, in_=in_, func=func, **kw), reads, writes)

    def ts(self, out, in0, s1, s2, op0, op1, reads, writes, eng="vector"):
        return self.S.op(eng, lambda e: e.tensor_scalar(out=out, in0=in0, scalar1=s1, scalar2=s2, op0=op0, op1=op1),
                         reads, writes)

    def tt(self, out, in0, in1, op, reads, writes, eng="vector"):
        return self.S.op(eng, lambda e: e.tensor_tensor(out=out, in0=in0, in1=in1, op=op), reads, writes)

    def stt(self, out, in0, scalar, in1, op0, op1, reads, writes):
        return self.S.op("vector", lambda e: e.scalar_tensor_tensor(out=out, in0=in0, scalar=scalar, in1=in1,
                                                                    op0=op0, op1=op1), reads, writes)

    def cp(self, out, in_, reads, writes, eng="vector"):
        return self.S.op(eng, lambda e: e.tensor_copy(out=out, in_=in_), reads, writes)

    def recip(self, out, in_, reads, writes):
        return self.S.op("vector", lambda e: e.reciprocal(out=out, in_=in_), reads, writes)

    def memset(self, ap, val, writes, eng="vector"):
        return self.S.op(eng, lambda e: e.memset(ap, val), (), writes)

    def mm(self, out, pairs, reads, writes):
        n = len(pairs)
        for i, (l, r) in enumerate(pairs):
            self.S.op("tensor", lambda e, l=l, r=r, i=i: e.matmul(out, l, r, start=(i == 0), stop=(i == n - 1)),
                      reads, writes)

    def tr(self, out, in_, ident, reads, writes):
        return self.S.op("tensor", lambda e: e.transpose(out, in_, ident), reads, writes)

    def finish(self):
        self.S.emit(self.es)
        self.es.close()

    def wcast(self, out, in_, scol, const, reads, writes):
        if scol is None:
            return self.ts(out, in_, float(const), None, ALU.mult, ALU.bypass, reads, writes)
        return self.ts(out, in_, scol, float(const), ALU.mult, ALU.mult, reads, writes)


def r3(ap, pat, **kw):
    return ap.rearrange(pat, **kw)


def ffn_groups(S):
    ng = -(-S // 510)
    base, rem = divmod(S, ng)
    out = []
    t = 0
    for i in range(ng):
        n = base + (1 if i < rem else 0)
        out.append((t, n))
        t += n
    return out


def build(SEQS):
    nc = bass.Bass("TRN2", target_bir_lowering=False)
    NS = len(SEQS)
    SMAX = max(SEQS)

    def din(name, shape, dt=F32):
        return nc.dram_tensor(name, list(shape), dt, kind="ExternalInput").ap()

    import os
    DBG = os.environ.get("KDBG", "") != ""

    def dscr(name, shape, dt=BF16):
        return nc.dram_tensor(name, list(shape), dt, kind="ExternalOutput" if DBG else "Internal").ap()

    X = [din("x%d" % i, [S, D]) for i, S in enumerate(SEQS)]
    Y = [nc.dram_tensor("y%d" % i, [S, D], F32, kind="ExternalOutput").ap() for i, S in enumerate(SEQS)]
    g_mix = din("g_mix", [D]); w_in = din("w_in", [D, IN_W])
    dec_f = din("ret_decay_fwd", [4]); dec_b = din("ret_decay_bwd", [4])
    ret_gn_g = din("ret_gn_g", [1024]); w_ret_o = din("w_ret_o", [1024, D])
    g_cq = din("g_cq", [384]); w_uq = din("w_uq", [384, 1536])
    g_ckv = din("g_ckv", [256]); w_ukv = din("w_ukv", [256, 2048])
    g_qn = din("g_qn", [192]); g_kn = din("g_kn", [192])
    w_mla_o = din("w_mla_o", [1024, D]); w_out = din("w_out", [D, D])
    g_ffn = din("g_ffn", [D]); w_up = din("w_up", [D, 5632])
    conv_w = din("conv_w", [3, 5632]); conv_b = din("conv_b", [5632])
    w_down = din("w_down", [2816, D])
    cf_d = din("cf", [128, NCF]); cb_d = din("cb", [128, 256], BF16)
    cosr_d = din("cosr", [128, SMAX]); sinr_d = din("sinr", [128, SMAX])
    cosm_d = din("cosm", [64, SMAX]); sinm_d = din("sinm", [64, SMAX])

    SC = []
    for i, S in enumerate(SEQS):
        p = "s%d_" % i
        SC.append(dict(
            hT=dscr(p + "hT", [D, S]), qrT=dscr(p + "qrT", [512, S]), krT=dscr(p + "krT", [512, S]),
            ktok=dscr(p + "ktok", [S, 512]), vtok=dscr(p + "vtok", [S, 1024]), rgsT=dscr(p + "rgsT", [1024, S]),
            grT=dscr(p + "grT", [1024, S]), gaT=dscr(p + "gaT", [1024, S]),
            qmnT=dscr(p + "qmnT", [8, 128, S]), qmrT=dscr(p + "qmrT", [8, 64, S]),
            kmnT=dscr(p + "kmnT", [8, 128, S]), kmrT=dscr(p + "kmrT", [64, S]),
            vmtok=dscr(p + "vmtok", [S, 1024]), rstdk=dscr(p + "rstdk", [S, 8], F32),
            sb=dscr(p + "sb", [S // 128, 128, 1024]), m1T=dscr(p + "m1T", [1024, S]),
            attnT=dscr(p + "attnT", [1024, S]), h2T=dscr(p + "h2T", [D, S]),
        ))

    def consts(P, need_cf=True):
        cb = P.sb("cb", [128, 256], BF16)
        P.load(cb[:], cb_d[:, :], "cb")
        cf = None
        if need_cf:
            cf = P.sb("cf", [128, NCF], F32)
            P.load(cf[:], cf_d[:, :], "cf")
        return cf, cb

    def make_cols(P, cf, rows, psk, name="cols"):
        R = sum(max(s.shape[0] for _, s in segs) for segs in rows)
        vst = P.sb(name + "_st", [R, 128], F32)
        P.memset(vst[:], 0.0, [name + "_st"])
        r = 0
        k = 0
        for segs in rows:
            nr = max(s.shape[0] for _, s in segs)
            for c0, src in segs:
                P.dma(vst[r:r + src.shape[0], c0:c0 + src.shape[1]], src, [], [name + "_st"],
                      chan="L%s%d" % (name, k), q="sync")
                k += 1
            r += nr
        ps, pkey = psk
        P.mm(ps[:, 0:R], [(vst[0:R, :], cf[0:R, C_ID:C_ID + R])], [name + "_st", "cf"], [pkey])
        cols = P.sb(name, [128, R], F32)
        P.cp(cols[:], ps[:, 0:R], [pkey], [name])
        return cols

    def v2(ap, p=128):
        return ap.rearrange("(k p) -> k p", p=p)

    def v1(ap):
        return ap.rearrange("(o n) -> o n", o=1)

    def phase1a():
        P = Phase(nc, "a")
        cf, cb = consts(P)
        ident = cb[:, 0:128]
        ps_r = P.pring("ps", 6, [128, 512], F32)
        cols = make_cols(P, cf, [[(0, v2(g_mix))]], ps_r.next())
        W = P.sb("W", [128, 8, 4096], BF16)
        stg = P.ring("stg", 2, [128, 3072], F32)
        for k in range(8):
            st, sk = stg.next()
            P.load(st[:], w_in[k * 128:(k + 1) * 128, 0:3072], sk)
            g = cols[:, k:k + 1]
            wk = "W"
            s4 = st[:, 0:512].rearrange("p (h d) -> p h d", d=128)
            P.wcast(W[:, k, 0:512], st[:, 0:512], g, 1.0, [sk, "cols"], [wk])
            o4 = W[:, k, 512:1024].rearrange("p (h d) -> p h d", d=128)
            P.wcast(o4[:, :, 0:64], s4[:, :, 64:128], g, -1.0, [sk, "cols"], [wk])
            P.wcast(o4[:, :, 64:128], s4[:, :, 0:64], g, 1.0, [sk, "cols"], [wk])
            sc = 128.0 ** -0.5
            s4 = st[:, 512:1024].rearrange("p (h d) -> p h d", d=128)
            P.wcast(W[:, k, 1024:1536], st[:, 512:1024], g, sc, [sk, "cols"], [wk])
            o4 = W[:, k, 1536:2048].rearrange("p (h d) -> p h d", d=128)
            P.wcast(o4[:, :, 0:64], s4[:, :, 64:128], g, -sc, [sk, "cols"], [wk])
            P.wcast(o4[:, :, 64:128], s4[:, :, 0:64], g, sc, [sk, "cols"], [wk])
            P.wcast(W[:, k, 2048:4096], st[:, 1024:3072], g, 1.0, [sk, "cols"], [wk])

        xt_r = P.ring("xt", 4, [128, D], F32)
        junk = P.sb("junk", [128, D], BF16)
        st_r = P.ring("stat", 2, [128, 4], F32)
        h_r = P.ring("h", 2, [128, D], BF16)
        pT_r = P.pring("pT", 1, [128, 8, 128], BF16)
        hT_r = P.ring("hT", 2, [128, 8, 512], BF16)
        cs_r = P.ring("cs", 2, [128, 2, 512], F32)
        ktp_r = P.pring("ktp", 1, [128, 4, 128], BF16)
        t1_r = P.ring("t1", 2, [128, 512], F32)
        t2_r = P.ring("t2", 2, [128, 512], F32)
        qo_r = P.ring("qo", 2, [128, 4, 512], BF16)
        ko_r = P.ring("ko", 2, [128, 4, 512], BF16)
        kt_r = P.ring("kt", 2, [128, 4, 512], BF16)
        rg_r = P.ring("rg", 2, [128, 8, 512], BF16)
        vo_r = P.ring("vo", 2, [128, 4, 1024], BF16)

        glist = [(si, g) for si, S in enumerate(SEQS) for g in range(S // 512)]

        def norm_group(si, g):
            sc_ = SC[si]
            t0 = g * 512
            xs = []
            for j in range(4):
                xt, xk = xt_r.next()
                P.load(xt[:], X[si][t0 + j * 128:t0 + (j + 1) * 128, :], xk)
                xs.append((xt, xk))
            hT, hk = hT_r.next()
            for j in range(4):
                xt, xk = xs[j]
                st, stk = st_r.next()
                P.act(junk[:], xt[:], AF.Square, [xk], ["junk", stk + "a"], accum=st[:, 0:1])
                P.act(st[:, 1:2], st[:, 0:1], AF.Sqrt, [stk + "a"], [stk + "b"], scale=1.0 / D, bias=EPS)
                P.recip(st[:, 2:3], st[:, 1:2], [stk + "b"], [stk + "c"])
                h, hhk = h_r.next()
                P.act(h[:], xt[:], AF.Copy, [xk, stk + "c"], [hhk], scale=st[:, 2:3])
                pT, pk = pT_r.next()
                for k in range(8):
                    P.tr(pT[:, k, :], h[:, k * 128:(k + 1) * 128], ident, [hhk, "cb"], [pk])
                P.cp(hT[:, :, j * 128:(j + 1) * 128], pT[:], [pk], [hk])
            P.store(sc_["hT"].rearrange("(k p) s -> p k s", p=128)[:, :, t0:t0 + 512], hT[:], hk)
            return hT, hk

        nxt = norm_group(*glist[0])
        for gi, (si, g) in enumerate(glist):
            if True:
                sc_ = SC[si]
                t0 = g * 512
                hT, hk = nxt
                cs, ck = cs_r.next()
                P.load(cs[:, 0, :], cosr_d[:, t0:t0 + 512], ck + "c")
                P.load(cs[:, 1, :], sinr_d[:, t0:t0 + 512], ck + "s")
                def proj(col0):
                    ps, pk = ps_r.next()
                    P.mm(ps[:], [(W[:, k, col0:col0 + 128], hT[:, k, :]) for k in range(8)], [hk, "W"], [pk])
                    return ps, pk

                qo, qk = qo_r.next()
                ko, kk = ko_r.next()
                kt, ktk = kt_r.next()
                pend_kt = []
                for (base, dst, dk) in ((0, qo, qk), (1024, ko, kk)):
                    for hh in range(4):
                        pa, pak = proj(base + hh * 128)
                        pb, pbk = proj(base + 512 + hh * 128)
                        t1, t1k = t1_r.next()
                        t2, t2k = t2_r.next()
                        P.tt(t1[:], pa[:], cs[:, 0, :], ALU.mult, [pak, ck + "c"], [t1k])
                        P.tt(t2[:], pb[:], cs[:, 1, :], ALU.mult, [pbk, ck + "s"], [t2k])
                        P.tt(dst[:, hh, :], t1[:], t2[:], ALU.add, [t1k, t2k], [dk + "h%d" % hh], eng="gpsimd")
                        if base == 1024:
                            def ktail(hh=hh, dk=dk):
                                ktp, ktpk = ktp_r.next()
                                for j in range(4):
                                    P.tr(ktp[:, j, :], ko[:, hh, j * 128:(j + 1) * 128], ident, [dk + "h%d" % hh, "cb"], [ktpk])
                                P.cp(kt[:, :, hh * 128:(hh + 1) * 128], ktp[:], [ktpk], [ktk + "h%d" % hh])
                            if pend_kt:
                                pend_kt.pop(0)()
                            pend_kt.append(ktail)
                while pend_kt:
                    pend_kt.pop(0)()
                hkeys = lambda kk_: [kk_ + "h%d" % i for i in range(4)]
                P.dma(sc_["qrT"].rearrange("(h p) s -> p h s", p=128)[:, :, t0:t0 + 512], qo[:], hkeys(qk), [],
                      chan="T" + qk, q="gpsimd")
                P.dma(sc_["krT"].rearrange("(h p) s -> p h s", p=128)[:, :, t0:t0 + 512], ko[:], hkeys(kk), [],
                      chan="T" + kk, q="gpsimd")
                P.dma(sc_["ktok"][t0:t0 + 512, :].rearrange("(j p) c -> p j c", p=128), kt[:], hkeys(ktk), [],
                      chan="T" + ktk, q="gpsimd")
                if gi + 1 < len(glist):
                    nxt = norm_group(*glist[gi + 1])
                rg, rgk = rg_r.next()
                for c in range(8):
                    ps, pk = proj(3072 + c * 128)
                    P.act(rg[:, c, :], ps[:], AF.Silu, [pk], [rgk + "c%d" % c])
                P.dma(sc_["rgsT"].rearrange("(c p) s -> p c s", p=128)[:, :, t0:t0 + 512], rg[:],
                      [rgk + "c%d" % c for c in range(8)], [], chan="T" + rgk, q="gpsimd")
                vo, vk = vo_r.next()
                for j in range(4):
                    for n in range(2):
                        ps, pk = ps_r.next()
                        P.mm(ps[:], [(hT[:, k, j * 128:(j + 1) * 128], W[:, k, 2048 + n * 512:2048 + (n + 1) * 512])
                                     for k in range(8)], [hk, "W"], [pk])
                        P.act(vo[:, j, n * 512:(n + 1) * 512], ps[:], AF.Copy, [pk], [vk + "p%d" % (j * 2 + n)])
                P.dma(sc_["vtok"][t0:t0 + 512, :].rearrange("(j p) c -> p j c", p=128), vo[:],
                      [vk + "p%d" % i for i in range(8)], [], chan="T" + vk, q="gpsimd")
        P.finish()

    def phase1b():
        P = Phase(nc, "b")
        cf, cb = consts(P)
        ones = cb[:, 128:256]
        rows = [[(0, v2(g_mix))], [(0, v2(g_cq))], [(0, v2(g_ckv))],
                [(0, v1(g_qn[0:128]))], [(0, v1(g_kn[0:128]))],
                [(0, v1(g_qn[128:192]))], [(0, v1(g_qn[160:192])), (32, v1(g_qn[128:160]))],
                [(0, v1(g_kn[128:192]))], [(0, v1(g_kn[160:192])), (32, v1(g_kn[128:160]))]]
        ps_r = P.pring("ps", 6, [128, 512], F32)
        cols = make_cols(P, cf, rows, ps_r.next())
        GCQ, GCKV, GQN, GKN, GQR, GQRS, GKR, GKRS = 8, 11, 13, 14, 15, 16, 17, 18
        gx = P.sb("gx", [128, 4], F32)
        qs = 192.0 ** -0.5
        P.stt(gx[:, 0:1], cols[:, GQN:GQN + 1], qs, cols[:, GKN:GKN + 1], ALU.mult, ALU.mult, ["cols"], ["gx"])
        P.ts(gx[:, 1:3], cols[:, GQR:GQR + 2], qs, None, ALU.mult, ALU.bypass, ["cols"], ["gx"])
        W = P.sb("W", [128, 8, 2816], BF16)
        stg = P.ring("stg", 2, [128, 3072], F32)
        for k in range(8):
            st, sk = stg.next()
            P.load(st[:, 0:2752], w_in[k * 128:(k + 1) * 128, 3072:5824], sk)
            g = cols[:, k:k + 1]
            P.wcast(W[:, k, 0:704], st[:, 0:704], g, 1.0, [sk, "cols"], ["W"])
            P.wcast(W[:, k, 704:736], st[:, 672:704], g, -1.0, [sk, "cols"], ["W"])
            P.wcast(W[:, k, 736:768], st[:, 640:672], g, 1.0, [sk, "cols"], ["W"])
            P.wcast(W[:, k, 768:2816], st[:, 704:2752], g, 1.0, [sk, "cols"], ["W"])
        Wuq = P.sb("Wuq", [128, 3, 2112], BF16)
        P.memset(Wuq[:, :, 2048:2112], 0.0, ["Wuqz"])
        for k in range(3):
            st, sk = stg.next()
            P.load(st[:, 0:1536], w_uq[k * 128:(k + 1) * 128, :], sk)
            s3 = st[:, 0:1536].rearrange("p (h d) -> p h d", d=192)
            P.wcast(Wuq[:, k, 0:1024].rearrange("p (h d) -> p h d", d=128), s3[:, :, 0:128], None, 1.0, [sk], ["Wuq"])
            P.wcast(Wuq[:, k, 1024:1536].rearrange("p (h d) -> p h d", d=64), s3[:, :, 128:192], None, 1.0, [sk], ["Wuq"])
            o3 = Wuq[:, k, 1536:2048].rearrange("p (h d) -> p h d", d=64)
            P.wcast(o3[:, :, 0:32], s3[:, :, 160:192], None, -1.0, [sk], ["Wuq"])
            P.wcast(o3[:, :, 32:64], s3[:, :, 128:160], None, 1.0, [sk], ["Wuq"])
        Wuk = P.sb("Wuk", [128, 2, 1024], BF16)
        Wuv = P.sb("Wuv", [128, 2, 1024], BF16)
        for k in range(2):
            st, sk = stg.next()
            P.load(st[:, 0:2048], w_ukv[k * 128:(k + 1) * 128, :], sk)
            s3 = st[:, 0:2048].rearrange("p (h d) -> p h d", d=256)
            P.wcast(Wuk[:, k, :].rearrange("p (h d) -> p h d", d=128), s3[:, :, 0:128], None, 1.0, [sk], ["Wuk"])
            P.wcast(Wuv[:, k, :].rearrange("p (h d) -> p h d", d=128), s3[:, :, 128:256], None, 1.0, [sk], ["Wuv"])

        hT_r = P.ring("hT", 2, [128, 8, 512], BF16)
        cs_r = P.ring("cs", 2, [64, 2, 512], F32)
        pss_r = P.pring("pss", 1, [128, 512], F32)
        pst_r = P.pring("pst", 1, [128, 4, 8], F32)
        gr_r = P.ring("gr", 2, [128, 8, 512], BF16)
        ga_r = gr_r
        sq_r = P.ring("sq", 3, [128, 512], BF16)
        sqr_r = P.ring("sqr", 2, [128, 512], BF16)
        sqkr_r = P.ring("sqkr", 2, [128, 512], BF16)
        for r_ in (sqr_r, sqkr_r):
            for i_, t_ in enumerate(r_.tiles):
                P.memset(t_[64:128, :], 0.0, ["%sz%d" % (r_.name, i_)])
        ZK = ["sqrz0", "sqrz1", "sqkrz0", "sqkrz1"]
        sd_r = P.ring("sd", 2, [128, 512], F32)
        rs_r = P.ring("rs", 2, [128, 512], F32)
        cqn_r = P.ring("cqn", 2, [128, 3, 512], BF16)
        ckvn_r = P.ring("ckvn", 2, [128, 2, 512], BF16)
        t1_r = P.ring("t1", 2, [64, 512], F32)
        t2_r = P.ring("t2", 2, [64, 512], F32)
        t3_r = P.ring("t3", 1, [64, 512], F32)
        kro_r = P.ring("kro", 2, [64, 512], BF16)
        ka1 = P.sb("ka1", [64, 512], F32)
        ka2 = P.sb("ka2", [64, 512], F32)
        qno_r = P.ring("qno", 1, [128, 8, 512], BF16)
        qro_r = P.ring("qro", 1, [64, 8, 512], BF16)
        kno_r = P.ring("kno", 1, [128, 8, 512], BF16)
        sdk_r = P.ring("sdk", 2, [128, 32], F32)
        rk_r = P.ring("rk", 2, [128, 4, 8], F32)
        vmo_r = P.ring("vmo", 1, [128, 4, 1024], BF16)

        for si, S in enumerate(SEQS):
            sc_ = SC[si]
            for g in range(S // 512):
                t0 = g * 512
                hT, hk = hT_r.next()
                P.load(hT[:], sc_["hT"].rearrange("(k p) s -> p k s", p=128)[:, :, t0:t0 + 512], hk)
                cs, ck = cs_r.next()
                P.load(cs[:, 0, :], cosm_d[:, t0:t0 + 512], ck + "c")
                P.load(cs[:, 1, :], sinm_d[:, t0:t0 + 512], ck + "s")

                def proj(col0, m=128):
                    ps, pk = ps_r.next()
                    P.mm(ps[0:m, :], [(W[:, k, col0:col0 + m], hT[:, k, :]) for k in range(8)], [hk, "W"], [pk])
                    return ps, pk
                for (base, ring, dst) in ((768, gr_r, "grT"), (1792, ga_r, "gaT")):
                    gt, gk = ring.next()
                    for c in range(8):
                        ps, pk = proj(base + c * 128)
                        P.act(gt[:, c, :], ps[:], AF.Sigmoid, [pk], [gk + "c%d" % c])
                    P.dma(sc_[dst].rearrange("(c p) s -> p c s", p=128)[:, :, t0:t0 + 512], gt[:],
                          [gk + "c%d" % c for c in range(8)], [], chan="T" + gk, q="gpsimd")

                def latent(col0, nch, gcol0, ring, inv_n):
                    pcs = []
                    pss, pssk = pss_r.next()
                    sqs = []
                    for c in range(nch):
                        ps, pk = proj(col0 + c * 128)
                        sq, sqk = sq_r.next()
                        P.act(sq[:], ps[:], AF.Square, [pk], [sqk])
                        pcs.append((ps, pk)); sqs.append((sq, sqk))
                    P.mm(pss[:], [(ones, sq[:]) for sq, _ in sqs], [k_ for _, k_ in sqs] + ["cb"], [pssk])
                    sd, sdk_ = sd_r.next()
                    P.act(sd[:], pss[:], AF.Sqrt, [pssk], [sdk_], scale=inv_n, bias=EPS)
                    rs, rsk = rs_r.next()
                    P.recip(rs[:], sd[:], [sdk_], [rsk])
                    o, ok = ring.next()
                    for c in range(nch):
                        ps, pk = pcs[c]
                        P.stt(o[:, c, :], ps[:], cols[:, gcol0 + c:gcol0 + c + 1], rs[:], ALU.mult, ALU.mult,
                              [pk, rsk, "cols"], [ok + "c%d" % c])
                    return o, [ok + "c%d" % c for c in range(nch)]
                cqn, cqk = latent(0, 3, GCQ, cqn_r, 1.0 / 384)
                ckvn, ckvk = latent(384, 2, GCKV, ckvn_r, 1.0 / 256)

                pkr, pkrk = proj(640, 128)
                pkrr, pkrrk = proj(704, 128)
                sqkr, sqkrk = sqkr_r.next()
                P.act(sqkr[0:64, :], pkr[0:64, :], AF.Square, [pkrk], [sqkrk])
                u1, u1k = t1_r.next(); u2, u2k = t2_r.next()
                P.tt(u1[:], pkr[0:64, :], cs[:, 0, :], ALU.mult, [pkrk, ck + "c"], [u1k])
                P.tt(u2[:], pkrr[0:64, :], cs[:, 1, :], ALU.mult, [pkrrk, ck + "s"], [u2k])
                P.tt(ka1[:], u1[:], u2[:], ALU.add, [u1k, u2k], ["ka1"], eng="gpsimd")
                u3, u3k = t1_r.next(); u4, u4k = t2_r.next()
                P.tt(u3[:], pkrr[0:64, :], cs[:, 0, :], ALU.mult, [pkrrk, ck + "c"], [u3k])
                P.tt(u4[:], pkr[0:64, :], cs[:, 1, :], ALU.mult, [pkrk, ck + "s"], [u4k])
                P.tt(ka2[:], u3[:], u4[:], ALU.subtract, [u3k, u4k], ["ka2"], eng="gpsimd")
                t1, t1k = t1_r.next(); t2, t2k = t2_r.next()
                P.stt(t1[:], ka1[:], cols[0:64, GKR:GKR + 1], cs[:, 0, :], ALU.mult, ALU.mult,
                      ["ka1", ck + "c", "cols"], [t1k])
                P.stt(t2[:], ka2[:], cols[0:64, GKRS:GKRS + 1], cs[:, 1, :], ALU.mult, ALU.mult,
                      ["ka2", ck + "s", "cols"], [t2k])
                kro, krok = kro_r.next()
                P.tt(kro[:], t1[:], t2[:], ALU.add, [t1k, t2k], [krok], eng="gpsimd")
                P.store(sc_["kmrT"][:, t0:t0 + 512], kro[:], krok)

                qno, qnk = qno_r.next()
                qro, qrk = qro_r.next()
                for hh in range(8):
                    psn, psnk = ps_r.next()
                    P.mm(psn[:], [(Wuq[:, k, hh * 128:(hh + 1) * 128], cqn[:, k, :]) for k in range(3)], cqk + ["Wuq"], [psnk])
                    psr, psrk = ps_r.next()
                    P.mm(psr[:, :], [(Wuq[:, k, 1024 + hh * 64:1024 + hh * 64 + 128], cqn[:, k, :]) for k in range(3)],
                         cqk + ["Wuq"], [psrk])
                    psrr, psrrk = ps_r.next()
                    P.mm(psrr[:, :], [(Wuq[:, k, 1536 + hh * 64:1536 + hh * 64 + 128], cqn[:, k, :]) for k in range(3)],
                         cqk + ["Wuq", "Wuqz"], [psrrk])
                    sq, sqk = sq_r.next()
                    P.act(sq[:], psn[:], AF.Square, [psnk], [sqk])
                    sqr, sqrk = sqr_r.next()
                    P.act(sqr[0:64, :], psr[0:64, :], AF.Square, [psrk], [sqrk])
                    pss, pssk = pss_r.next()
                    P.mm(pss[:], [(ones, sq[:]), (ones, sqr[:])], [sqk, sqrk, "cb"] + ZK, [pssk])
                    sd, sdk_ = sd_r.next()
                    P.act(sd[:], pss[:], AF.Sqrt, [pssk], [sdk_], scale=1.0 / 192, bias=EPS)
                    rs, rsk = rs_r.next()
                    P.recip(rs[:], sd[:], [sdk_], [rsk])
                    P.stt(qno[:, hh, :], psn[:], gx[:, 0:1], rs[:], ALU.mult, ALU.mult, [psnk, rsk, "gx"], [qnk + "h%d" % hh])
                    t1, t1k = t1_r.next(); t2, t2k = t2_r.next(); t3, t3k = t3_r.next()
                    P.stt(t1[:], psr[0:64, :], gx[0:64, 1:2], cs[:, 0, :], ALU.mult, ALU.mult, [psrk, ck + "c", "gx"], [t1k])
                    P.stt(t2[:], psrr[0:64, :], gx[0:64, 2:3], cs[:, 1, :], ALU.mult, ALU.mult, [psrrk, ck + "s", "gx"], [t2k])
                    P.tt(t3[:], t1[:], t2[:], ALU.add, [t1k, t2k], [t3k], eng="gpsimd")
                    P.tt(qro[:, hh, :], t3[:], rs[0:64, :], ALU.mult, [t3k, rsk], [qrk + "h%d" % hh], eng="gpsimd")
                P.dma(sc_["qmnT"].rearrange("h p s -> p h s")[:, :, t0:t0 + 512], qno[:],
                      [qnk + "h%d" % i for i in range(8)], [], chan="T" + qnk, q="gpsimd")
                P.dma(sc_["qmrT"].rearrange("h p s -> p h s")[:, :, t0:t0 + 512], qro[:],
                      [qrk + "h%d" % i for i in range(8)], [], chan="T" + qrk, q="gpsimd")

                kno, knk = kno_r.next()
                pst, pstk = pst_r.next()
                pend_ks = []
                for hh in range(8):
                    ps, pk = ps_r.next()
                    P.mm(ps[:], [(Wuk[:, k, hh * 128:(hh + 1) * 128], ckvn[:, k, :]) for k in range(2)], ckvk + ["Wuk"], [pk])
                    P.act(kno[:, hh, :], ps[:], AF.Copy, [pk], [knk + "h%d" % hh])
                    sq, sqk = sq_r.next()
                    P.act(sq[:], ps[:], AF.Square, [pk], [sqk])

                    def kstat(hh=hh, sq=sq, sqk=sqk):
                        for j in range(4):
                            P.mm(pst[:, j, hh:hh + 1], [(sq[:, j * 128:(j + 1) * 128], cb[:, 128:129]),
                                                        (sqkr[:, j * 128:(j + 1) * 128], cb[:, 128:129])],
                                 [sqk, sqkrk, "cb"] + ZK, [pstk])
                    pend_ks.append(kstat)
                    if len(pend_ks) > 2:
                        pend_ks.pop(0)()
                while pend_ks:
                    pend_ks.pop(0)()
                P.dma(sc_["kmnT"].rearrange("h p s -> p h s")[:, :, t0:t0 + 512], kno[:],
                      [knk + "h%d" % i for i in range(8)], [], chan="T" + knk, q="gpsimd")
                sdk, sdkk = sdk_r.next()
                P.act(sdk[:], pst[:].rearrange("p j h -> p (j h)"), AF.Sqrt, [pstk], [sdkk], scale=1.0 / 192, bias=EPS)
                rk, rkk = rk_r.next()
                P.recip(rk[:].rearrange("p j h -> p (j h)"), sdk[:], [sdkk], [rkk])
                P.store(sc_["rstdk"][t0:t0 + 512, :].rearrange("(j p) h -> p j h", p=128), rk[:], rkk)

                vmo, vmk = vmo_r.next()
                for j in range(4):
                    for n in range(2):
                        ps, pk = ps_r.next()
                        P.mm(ps[:], [(ckvn[:, k, j * 128:(j + 1) * 128], Wuv[:, k, n * 512:(n + 1) * 512]) for k in range(2)],
                             ckvk + ["Wuv"], [pk])
                        P.act(vmo[:, j, n * 512:(n + 1) * 512], ps[:], AF.Copy, [pk], [vmk + "p%d" % (j * 2 + n)])
                P.dma(sc_["vmtok"][t0:t0 + 512, :].rearrange("(j p) c -> p j c", p=128), vmo[:],
                      [vmk + "p%d" % i for i in range(8)], [], chan="T" + vmk, q="gpsimd")
        P.finish()

    def phase2():
        P = Phase(nc, "r")
        cf, cb = consts(P)
        ident = cb[:, 0:128]
        psP_r = P.pring("psP", 1, [128, 512], F32)
        cols = make_cols(P, cf, [[(0, v2(ret_gn_g))]], psP_r.next())
        dst = P.sb("dst", [1, 8], F32)
        P.dma(dst[0:1, 0:4], v1(dec_f), [], ["dst"], chan="Ldst0")
        P.dma(dst[0:1, 4:8], v1(dec_b), [], ["dst"], chan="Ldst1")
        pdc, pdck = psP_r.next()
        P.mm(pdc[:, 0:8], [(cf[0:1, C_ONE:C_ONE + 128], dst[0:1, 0:8])], ["dst", "cf"], [pdck])
        lg = P.sb("lg", [128, 8], F32)
        P.act(lg[:], pdc[:, 0:8], AF.Exp, [pdck], ["lg"], scale=-1.0)
        P.ts(lg[:], lg[:], 1.0, None, ALU.add, ALU.bypass, ["lg"], ["lg"])
        P.act(lg[:], lg[:], AF.Ln, ["lg"], ["lg"])
        P.ts(lg[:], lg[:], -1.0, None, ALU.mult, ALU.bypass, ["lg"], ["lg"])
        DT = P.sb("DT", [128, 4, 128], F32)
        decf = P.sb("decf", [128, 4, 128], F32)
        decb = P.sb("decb", [128, 4, 128], F32)
        kcol = P.sb("kcol", [128, 16], F32)
        e1 = P.sb("e1", [128, 128], F32)
        e2 = P.sb("e2", [128, 128], F32)
        for hh in range(4):
            lf = lg[:, hh:hh + 1]; lb = lg[:, 4 + hh:5 + hh]
            P.act(e1[:], cf[:, C_A:C_A + 128], AF.Exp, ["cf", "lg"], ["e1"], scale=lf)
            P.tt(e1[:], e1[:], cf[:, C_MF:C_MF + 128], ALU.mult, ["e1", "cf"], ["e1"])
            P.act(e2[:], cf[:, C_B:C_B + 128], AF.Exp, ["cf", "lg"], ["e2"], scale=lb)
            P.tt(e2[:], e2[:], cf[:, C_MB:C_MB + 128], ALU.mult, ["e2", "cf"], ["e2"])
            P.tt(DT[:, hh, :], e1[:], e2[:], ALU.add, ["e1", "e2"], ["DT"])
            P.act(decf[:, hh, :], cf[:, C_C1:C_C1 + 128], AF.Exp, ["cf", "lg"], ["decf"], scale=lf)
            P.act(decb[:, hh, :], cf[:, C_C2:C_C2 + 128], AF.Exp, ["cf", "lg"], ["decb"], scale=lb)
            P.act(kcol[:, hh:hh + 1], cf[:, C_SM:C_SM + 1], AF.Exp, ["cf", "lg"], ["kcol"], scale=lf)
            P.act(kcol[:, 4 + hh:5 + hh], cf[:, C_SM + 1:C_SM + 2], AF.Exp, ["cf", "lg"], ["kcol"], scale=lb)
            P.act(kcol[:, 8 + hh:9 + hh], cf[:, C_SM + 2:C_SM + 3], AF.Exp, ["cf", "lg"], ["kcol"], scale=lf)
            P.act(kcol[:, 12 + hh:13 + hh], cf[:, C_SM + 2:C_SM + 3], AF.Exp, ["cf", "lg"], ["kcol"], scale=lb)
        Wro = P.sb("Wro", [128, 8, 1024], BF16)
        stg = P.ring("stg", 2, [128, 1024], F32)
        for k in range(8):
            st, sk = stg.next()
            P.load(st[:], w_ret_o[k * 128:(k + 1) * 128, :], sk)
            P.wcast(Wro[:, k, :], st[:], cols[:, k:k + 1], 1.0, [sk, "cols"], ["Wro"])

        kt_r = P.ring("kt", 2, [128, 4, 512], BF16)
        v_r = P.ring("v", 2, [128, 4, 1024], BF16)
        kd_r = P.ring("kd", 2, [128, 4, 128], BF16)
        psU_r = P.pring("psU", 1, [128, 512], F32)
        St = P.sb("St", [128, 4, 256], F32)
        sbb_r = P.ring("sbb", 3, [128, 1024], BF16)

        kfb = {}
        for col0 in (0, 4):
            t_ = P.sb("kfb%d" % col0, [128, 4, 128], F32)
            for hh in range(4):
                P.act(t_[:, hh, :], cf[:, C_ONE:C_ONE + 128], AF.Copy, ["cf", "kcol"], ["kfb%d" % col0],
                      scale=kcol[:, col0 + hh:col0 + hh + 1])
            kfb[col0] = t_

        def kdec(kt, ktk, j, col0):
            kd, kdk = kd_r.next()
            P.tt(kd[:], kt[:, j, :].rearrange("p (h d) -> p h d", d=128), kfb[col0][:], ALU.mult,
                 [ktk, "kfb%d" % col0], [kdk], eng="gpsimd")
            return kd, [kdk]

        def state_update(kd, kdk, v, vk, j, gcol0):
            for hp in range(2):
                psU, psUk = psU_r.next()
                for h2 in range(2):
                    hh = hp * 2 + h2
                    P.mm(psU[:, h2 * 256:(h2 + 1) * 256], [(kd[:, hh, :], v[:, j, hh * 256:(hh + 1) * 256])], kdk + [vk], [psUk])
                for h2 in range(2):
                    hh = hp * 2 + h2
                    P.stt(St[:, hh, :], St[:, hh, :], kcol[:, gcol0 + hh:gcol0 + hh + 1], psU[:, h2 * 256:(h2 + 1) * 256],
                          ALU.mult, ALU.add, ["St%d" % hh, psUk, "kcol"], ["St%d" % hh])

        for si, S in enumerate(SEQS):
            sc_ = SC[si]
            NG = S // 512
            P.memset(St[:], 0.0, ["St%d" % i_ for i_ in range(4)])
            for g in range(NG - 1, -1, -1):
                t0 = g * 512
                kt, ktk = kt_r.next()
                P.load(kt[:], sc_["ktok"][t0:t0 + 512, :].rearrange("(j p) c -> p j c", p=128), ktk)
                v, vk = v_r.next()
                P.load(v[:], sc_["vtok"][t0:t0 + 512, :].rearrange("(j p) c -> p j c", p=128), vk)
                for j in range(3, -1, -1):
                    n = g * 4 + j
                    sbb, sbk = sbb_r.next()
                    P.cp(sbb[:], St[:].rearrange("p h d -> p (h d)"), ["St%d" % i_ for i_ in range(4)], [sbk])
                    P.store(sc_["sb"][n, :, :], sbb[:], sbk, dkey="sbd%d_%d" % (si, n))
                    if n > 0:
                        kd, kdk = kdec(kt, ktk, j, 4)
                        state_update(kd, kdk, v, vk, j, 12)
            P.memset(St[:], 0.0, ["St%d" % i_ for i_ in range(4)])
            fw = getattr(P, "_fw", None)
            if fw is None:
                fw = dict(
                    qT=P.ring("qT", 2, [128, 4, 512], BF16), kT=P.ring("kT", 2, [128, 4, 512], BF16),
                    sbl=P.ring("sbl", 2, [128, 4, 1024], BF16), rgs=P.ring("rgs", 2, [128, 8, 512], BF16),
                    gr=P.ring("gr", 2, [128, 8, 512], BF16),
                    psS=P.pring("psS", 1, [128, 4, 128], F32), psO=P.pring("psO", 2, [128, 1024], F32),
                    tp=P.pring("tp", 1, [128, 8, 128], BF16), psP=psP_r,
                    pT=P.ring("pT", 2, [128, 4, 128], BF16), qf=P.ring("qf", 2, [128, 4, 128], BF16),
                    qb=P.ring("qb", 2, [128, 4, 128], BF16), Sfb=P.ring("Sfb", 3, [128, 4, 256], BF16),
                    st=P.ring("st", 2, [128, 32], F32), retn=P.ring("retn", 2, [128, 1024], BF16),
                    retg=P.ring("retg", 2, [128, 8, 512], BF16), m1=P.ring("m1", 2, [128, 8, 512], BF16),
                    junk=P.sb("junk", [128, 256], BF16),
                )
                P._fw = fw
            Sfb, Sfk = fw["Sfb"].next()
            P.cp(Sfb[:], St[:], ["St%d" % i_ for i_ in range(4)], [Sfk])
            tails = []
            for g in range(NG):
                t0 = g * 512
                qT, qTk = fw["qT"].next()
                P.load(qT[:], sc_["qrT"].rearrange("(h p) s -> p h s", p=128)[:, :, t0:t0 + 512], qTk)
                kT, kTk = fw["kT"].next()
                P.load(kT[:], sc_["krT"].rearrange("(h p) s -> p h s", p=128)[:, :, t0:t0 + 512], kTk)
                kt, ktk = kt_r.next()
                P.load(kt[:], sc_["ktok"][t0:t0 + 512, :].rearrange("(j p) c -> p j c", p=128), ktk)
                v, vk = v_r.next()
                P.load(v[:], sc_["vtok"][t0:t0 + 512, :].rearrange("(j p) c -> p j c", p=128), vk)
                sbl, sblk = fw["sbl"].next()
                P.dma(sbl[:], sc_["sb"][g * 4:(g + 1) * 4, :, :].rearrange("j p c -> p j c"),
                      ["sbd%d_%d" % (si, g * 4 + j) for j in range(4)], [sblk], chan="L" + sblk)
                rgs, rgsk = fw["rgs"].next()
                P.load(rgs[:], sc_["rgsT"].rearrange("(c p) s -> p c s", p=128)[:, :, t0:t0 + 512], rgsk)
                gr, grk = fw["gr"].next()
                P.load(gr[:], sc_["grT"].rearrange("(c p) s -> p c s", p=128)[:, :, t0:t0 + 512], grk)
                retg, retgk = fw["retg"].next()
                for j in range(4):
                    sl = slice(j * 128, (j + 1) * 128)
                    psS, psSk = fw["psS"].next()
                    for hh in range(4):
                        P.mm(psS[:, hh, :], [(kT[:, hh, sl], qT[:, hh, sl])], [kTk, qTk], [psSk])
                    pT, pTk = fw["pT"].next()
                    P.tt(pT[:], psS[:], DT[:], ALU.mult, [psSk, "DT"], [pTk])
                    qf, qfk = fw["qf"].next()
                    P.tt(qf[:], qT[:, :, sl], decf[:], ALU.mult, [qTk, "decf"], [qfk], eng="gpsimd")
                    qb, qbk = fw["qb"].next()
                    P.tt(qb[:], qT[:, :, sl], decb[:], ALU.mult, [qTk, "decb"], [qbk], eng="gpsimd")
                    Sfb_old, Sfk_old = Sfb, Sfk
                    kd, kdk = kdec(kt, ktk, j, 0)
                    state_update(kd, kdk, v, vk, j, 8)
                    Sfb, Sfk = fw["Sfb"].next()
                    P.cp(Sfb[:], St[:], ["St%d" % i_ for i_ in range(4)], [Sfk])
                    psO, psOk = fw["psO"].next()
                    for hh in range(4):
                        vs = v[:, j, hh * 256:(hh + 1) * 256]
                        P.mm(psO[:, hh * 256:(hh + 1) * 256],
                             [(pT[:, hh, :], vs), (qf[:, hh, :], Sfb_old[:, hh, :]),
                              (qb[:, hh, :], sbl[:, j, hh * 256:(hh + 1) * 256])],
                             [pTk, vk, qfk, Sfk_old, qbk, sblk], [psOk])
                    while tails:
                        tails.pop(0)()
                    st, stk = fw["st"].next()
                    for hh in range(4):
                        o = psO[:, hh * 256:(hh + 1) * 256]
                        P.act(fw["junk"][:], o, AF.Copy, [psOk], ["rjunk", stk + "a"], accum=st[:, hh:hh + 1])
                    for hh in range(4):
                        o = psO[:, hh * 256:(hh + 1) * 256]
                        P.act(fw["junk"][:], o, AF.Square, [psOk], ["rjunk", stk + "a2"], accum=st[:, 4 + hh:5 + hh])
                    P.ts(st[:, 8:12], st[:, 0:4], 1.0 / 256, None, ALU.mult, ALU.bypass, [stk + "a"], [stk + "b"])
                    P.tt(st[:, 12:16], st[:, 8:12], st[:, 8:12], ALU.mult, [stk + "b"], [stk + "c"])
                    P.stt(st[:, 16:20], st[:, 4:8], 1.0 / 256, st[:, 12:16], ALU.mult, ALU.subtract, [stk + "a2", stk + "c"], [stk + "d"])
                    P.act(st[:, 20:24], st[:, 16:20], AF.Sqrt, [stk + "d"], [stk + "e"], bias=EPS)
                    P.recip(st[:, 24:28], st[:, 20:24], [stk + "e"], [stk + "f"])
                    P.stt(st[:, 28:32], st[:, 8:12], -1.0, st[:, 24:28], ALU.mult, ALU.mult, [stk + "b", stk + "f"], [stk + "g"])
                    retn, retnk = fw["retn"].next()
                    for hh in range(4):
                        P.act(retn[:, hh * 256:(hh + 1) * 256], psO[:, hh * 256:(hh + 1) * 256], AF.Identity,
                              [psOk, stk + "f", stk + "g"], [retnk], scale=st[:, 24 + hh:25 + hh], bias=st[:, 28 + hh:29 + hh])

                    def tail(j=j, sl=sl, retn=retn, retnk=retnk, retg=retg, retgk=retgk, rgs=rgs, rgsk=rgsk,
                             gr=gr, grk=grk, t0=t0):
                        tp, tpk = fw["tp"].next()
                        for c in range(8):
                            P.tr(tp[:, c, :], retn[:, c * 128:(c + 1) * 128], ident, [retnk, "cb"], [tpk])
                        P.tt(retg[:, :, sl], tp[:], rgs[:, :, sl], ALU.mult, [tpk, rgsk], [retgk + "j%d" % j])
                        if j == 3:
                            m1, m1k = fw["m1"].next()
                            for c in range(8):
                                psP, psPk = fw["psP"].next()
                                P.mm(psP[:], [(Wro[:, k, c * 128:(c + 1) * 128], retg[:, k, :]) for k in range(8)],
                                     [retgk + "j%d" % jj for jj in range(4)] + ["Wro"], [psPk])
                                P.tt(m1[:, c, :], psP[:], gr[:, c, :], ALU.mult, [psPk, grk], [m1k + "c%d" % c])
                            P.dma(sc_["m1T"].rearrange("(c p) s -> p c s", p=128)[:, :, t0:t0 + 512], m1[:],
                                  [m1k + "c%d" % c for c in range(8)], [], chan="T" + m1k, q="gpsimd")
                    tails.append(tail)
            while tails:
                tails.pop(0)()
        P.finish()

    def phase3():
        P = Phase(nc, "m")
        cf, cb = consts(P)
        ones = cb[:, 128:256]
        onesf = cf[:, C_ONE:C_ONE + 128]
        Kn_r = P.ring("Kn", 2, [128, SMAX], BF16)
        V_r = P.ring("V", 2, [128, SMAX // 128, 128], BF16)
        Kr = P.sb("Kr", [128, SMAX], BF16)
        P.memset(Kr[64:128, :], 0.0, ["Krz"])
        rk = P.sb("rk", [128, SMAX // 128, 8], F32)
        Qn_r = P.ring("Qn", 3, [128, 512], BF16)
        Qr_r = P.ring("Qr", 3, [128, 512], BF16)
        for i_, t_ in enumerate(Qr_r.tiles):
            P.memset(t_[64:128, :], 0.0, ["Qrz%d" % i_])
        st_r = P.pring("st", 4, [128, 512], F32)
        o_r = P.pring("o", 2, [128, 512], F32)
        l_r = P.pring("l", 2, [128, 512], F32)
        p_r = P.ring("p", 6, [128, 512], BF16)
        acc_r = [P.ring("acc0", 2, [128, 512], F32), P.ring("acc1", 2, [128, 512], F32)]
        accs_r = P.ring("accs", 2, [128, 512], F32)
        rl_r = P.ring("rl", 2, [128, 512], F32)
        at_r = P.ring("at", 3, [128, 512], BF16)
        LAG = 2
        units = [(si, hh, qg, kt) for si, S in enumerate(SEQS) for hh in range(8)
                 for qg in range(S // 512) for kt in range(S // 128)]
        cur = {}
        pend = []

        def stage_a(u):
            si, hh, qg, kt = u
            S = SEQS[si]; sc_ = SC[si]; NK = S // 128
            if hh == 0 and qg == 0 and kt == 0:
                P.load(Kr[0:64, 0:S], sc_["kmrT"][:, :], "Kr")
                P.load(rk[:, 0:NK, :], sc_["rstdk"].rearrange("(t p) h -> p t h", p=128), "rk")
            if qg == 0 and kt == 0:
                Kn, Knk = Kn_r.next()
                P.load(Kn[:, 0:S], sc_["kmnT"][hh, :, :], Knk)
                V, Vk = V_r.next()
                P.load(V[:, 0:NK, :], sc_["vmtok"][:, hh * 128:(hh + 1) * 128].rearrange("(t p) c -> p t c", p=128), Vk)
                cur["K"] = (Kn, Knk, V, Vk)
            if kt == 0:
                q0 = qg * 512
                Qn, Qnk = Qn_r.next()
                P.load(Qn[:], sc_["qmnT"][hh, :, q0:q0 + 512], Qnk)
                Qr, Qrk = Qr_r.next()
                P.load(Qr[0:64, :], sc_["qmrT"][hh, :, q0:q0 + 512], Qrk)
                cur["Q"] = (Qn, Qnk, Qr, Qrk)
            Kn, Knk, V, Vk = cur["K"]
            Qn, Qnk, Qr, Qrk = cur["Q"]
            ks = slice(kt * 128, (kt + 1) * 128)
            st, stk = st_r.next()
            P.mm(st[:], [(Kn[:, ks], Qn[:]), (Kr[:, ks], Qr[:])], [Knk, "Kr", "Krz", Qnk, Qrk] + ["Qrz%d" % i_ for i_ in range(3)], [stk])
            p, pk = p_r.next()
            P.act(p[:], st[:], AF.Exp, [stk, "rk"], [pk], scale=rk[:, kt, hh:hh + 1])
            pend.append((u, p, pk, V, Vk))

        def stage_b():
            u, p, pk, V, Vk = pend.pop(0)
            si, hh, qg, kt = u
            S = SEQS[si]; sc_ = SC[si]; NK = S // 128
            if kt == 0:
                cur["o"] = o_r.next()
                cur["l"] = l_r.next()
                cur["acc"] = [acc_r[0].next(), acc_r[1].next()]
                cur["na"] = 0
            o, ok = cur["o"]
            l, lk = cur["l"]
            P.S.op("tensor", lambda e, o=o, V=V, kt=kt, p=p, NK=NK: e.matmul(
                o[:], V[:, kt, :], p[:], start=(kt == 0), stop=(kt == NK - 1)), [Vk, pk], [ok])
            if kt % 4 == 3:
                P.S.op("tensor", lambda e, l=l, p=p, kt=kt: e.matmul(
                    l[:], ones, p[:], start=(kt == 3), stop=False), [pk, "cb"], [lk])
            else:
                na = cur["na"]; cur["na"] = na + 1
                a, ak = cur["acc"][na % 2]
                if na < 2:
                    P.cp(a[:], p[:], [pk], [ak])
                else:
                    P.tt(a[:], a[:], p[:], ALU.add, [ak, pk], [ak])
            if kt == NK - 1:
                q0 = qg * 512
                (a0, a0k), (a1, a1k) = cur["acc"]
                asum, asumk = accs_r.next()
                P.tt(asum[:], a0[:], a1[:], ALU.add, [a0k, a1k], [asumk])
                P.S.op("tensor", lambda e, l=l, asum=asum: e.matmul(
                    l[:], onesf, asum[:], start=False, stop=True), [asumk, "cf"], [lk])
                rl, rlk = rl_r.next()
                P.recip(rl[:], l[:], [lk], [rlk])
                at, atk = at_r.next()
                P.tt(at[:], o[:], rl[:], ALU.mult, [ok, rlk], [atk])
                P.store(sc_["attnT"][hh * 128:(hh + 1) * 128, q0:q0 + 512], at[:], atk)

        for i in range(len(units) + LAG):
            if i < len(units):
                stage_a(units[i])
            if i >= LAG:
                stage_b()
        P.finish()

    def phase4():
        P = Phase(nc, "o")
        cf, cb = consts(P, need_cf=False)
        ident = cb[:, 0:128]
        Wmo = P.sb("Wmo", [128, 8, 1024], BF16)
        Wo = P.sb("Wo", [128, 8, 1024], BF16)
        stg = P.ring("stg", 2, [128, 1024], F32)
        for (Wt, src, wk) in ((Wmo, w_mla_o, "Wmo"), (Wo, w_out, "Wo")):
            for k in range(8):
                st, sk = stg.next()
                P.load(st[:], src[k * 128:(k + 1) * 128, :], sk)
                P.wcast(Wt[:, k, :], st[:], None, 1.0, [sk], [wk])
        at_r = P.ring("at", 2, [128, 8, 512], BF16)
        ga_r = P.ring("ga", 2, [128, 8, 512], BF16)
        m1_r = P.ring("m1", 2, [128, 8, 512], BF16)
        xt_r = P.ring("xt", 4, [128, D], F32)
        ps_r = P.pring("ps", 6, [128, 512], F32)
        pT_r = P.pring("pT", 2, [128, 8, 128], BF16)
        tm_r = P.ring("tm", 2, [128, 512], F32)
        mg_r = P.ring("mg", 2, [128, 8, 512], BF16)
        x1_r = P.ring("x1", 2, [128, D], F32)
        junk = P.sb("junk", [128, D], BF16)
        st_r = P.ring("stat", 2, [128, 4], F32)
        h_r = P.ring("h", 2, [128, D], BF16)
        hT_r = P.ring("hT", 2, [128, 8, 512], BF16)
        for si, S in enumerate(SEQS):
            sc_ = SC[si]
            h2v = sc_["h2T"].rearrange("(k p) s -> p k s", p=128)
            for g in range(S // 512):
                t0 = g * 512
                at, atk = at_r.next()
                P.load(at[:], sc_["attnT"].rearrange("(c p) s -> p c s", p=128)[:, :, t0:t0 + 512], atk)
                ga, gak = ga_r.next()
                P.load(ga[:], sc_["gaT"].rearrange("(c p) s -> p c s", p=128)[:, :, t0:t0 + 512], gak)
                m1, m1k = m1_r.next()
                P.load(m1[:], sc_["m1T"].rearrange("(c p) s -> p c s", p=128)[:, :, t0:t0 + 512], m1k)
                xs = []
                for j in range(4):
                    xt, xk = xt_r.next()
                    P.load(xt[:], X[si][t0 + j * 128:t0 + (j + 1) * 128, :], xk)
                    xs.append((xt, xk))
                mg, mgk = mg_r.next()
                for c in range(8):
                    ps, pk = ps_r.next()
                    P.mm(ps[:], [(Wmo[:, k, c * 128:(c + 1) * 128], at[:, k, :]) for k in range(8)], [atk, "Wmo"], [pk])
                    tm, tmk = tm_r.next()
                    P.tt(tm[:], ps[:], ga[:, c, :], ALU.mult, [pk, gak], [tmk])
                    P.tt(mg[:, c, :], tm[:], m1[:, c, :], ALU.add, [tmk, m1k], [mgk + "c%d" % c], eng="gpsimd")
                mgks = [mgk + "c%d" % c for c in range(8)]
                hT, hk = hT_r.next()
                pend_tr = []
                for j in range(4):
                    xt, xk = xs[j]
                    x1, x1k = x1_r.next()
                    for n in range(2):
                        ps, pk = ps_r.next()
                        P.mm(ps[:], [(mg[:, k, j * 128:(j + 1) * 128], Wo[:, k, n * 512:(n + 1) * 512]) for k in range(8)],
                             mgks + ["Wo"], [pk])
                        P.tt(x1[:, n * 512:(n + 1) * 512], ps[:], xt[:, n * 512:(n + 1) * 512], ALU.add, [pk, xk], [x1k + "n%d" % n])
                    x1ks = [x1k + "n0", x1k + "n1"]
                    P.dma(Y[si][t0 + j * 128:t0 + (j + 1) * 128, :], x1[:], x1ks, [], chan="T" + x1k, q="gpsimd")
                    st, stk = st_r.next()
                    P.act(junk[:], x1[:], AF.Square, x1ks, ["junk", stk + "a"], accum=st[:, 0:1])
                    P.act(st[:, 1:2], st[:, 0:1], AF.Sqrt, [stk + "a"], [stk + "b"], scale=1.0 / D, bias=EPS)
                    P.recip(st[:, 2:3], st[:, 1:2], [stk + "b"], [stk + "c"])
                    h, hhk = h_r.next()
                    P.act(h[:], x1[:], AF.Copy, x1ks + [stk + "c"], [hhk], scale=st[:, 2:3])

                    def tr_tail(j=j, h=h, hhk=hhk, hT=hT, hk=hk):
                        pT, pk = pT_r.next()
                        for k in range(8):
                            P.tr(pT[:, k, :], h[:, k * 128:(k + 1) * 128], ident, [hhk, "cb"], [pk])
                        P.cp(hT[:, :, j * 128:(j + 1) * 128], pT[:], [pk], [hk])
                    if pend_tr:
                        pend_tr.pop(0)()
                    pend_tr.append(tr_tail)
                while pend_tr:
                    pend_tr.pop(0)()
                P.store(h2v[:, :, t0:t0 + 512], hT[:], hk)
        P.finish()

    def phase5():
        P = Phase(nc, "f")
        cf, cb = consts(P)
        pu_r = P.pring("pu", 4, [128, 512], F32)
        cols = make_cols(P, cf, [[(0, v2(g_ffn))]], pu_r.next())
        cwp = P.es.enter_context(nc.psum_tensor("f_cwp", [128, 44, 4], F32))
        stg = P.ring("stg", 2, [128, 1408], F32)
        for n in range(4):
            st, sk = stg.next()
            P.dma(st[0:3, :], conv_w[:, n * 1408:(n + 1) * 1408], [], [sk], chan="Lcw0" + sk)
            P.dma(st[3:4, :], v1(conv_b)[:, n * 1408:(n + 1) * 1408], [], [sk], chan="Lcw1" + sk)
            for c in range(11):
                P.mm(cwp[:, n * 11 + c, :], [(st[0:4, c * 128:(c + 1) * 128], cf[0:4, C_ID:C_ID + 4])], [sk, "cf"], ["cwp"])
        cw = P.sb("cw", [128, 44, 4], F32)
        P.cp(cw[:], cwp[:], ["cwp"], ["cw"])
        Wup = P.sb("Wup", [128, 8, 5632], BF16)
        Wd = P.sb("Wd", [128, 22, 1024], BF16)
        for k in range(8):
            for n in range(4):
                st, sk = stg.next()
                P.load(st[:], w_up[k * 128:(k + 1) * 128, n * 1408:(n + 1) * 1408], sk)
                P.wcast(Wup[:, k, n * 1408:(n + 1) * 1408], st[:], cols[:, k:k + 1], 1.0, [sk, "cols"], ["Wup"])
        for k in range(22):
            st, sk = stg.next()
            P.load(st[:, 0:1024], w_down[k * 128:(k + 1) * 128, :], sk)
            P.wcast(Wd[:, k, :], st[:, 0:1024], None, 1.0, [sk], ["Wd"])
        hT_r = P.ring("hT", 1, [128, 8, 512], BF16)
        pd_r = P.pring("pd", 3, [128, 512], F32)
        ta_r = P.ring("ta", 2, [128, 512], F32)
        tb_r = P.ring("tb", 2, [128, 512], F32)
        sa_r = P.ring("sa", 2, [128, 512], F32)
        act_r = P.ring("act", 1, [128, 22, 512], BF16)
        P.memset(act_r.tiles[0][:], 0.0, ["act0c%d" % c for c in range(22)])
        x1_r = P.ring("x1", 2, [128, D], F32)
        yo_r = P.ring("yo", 2, [128, D], F32)
        for si, S in enumerate(SEQS):
            sc_ = SC[si]
            h2v = sc_["h2T"].rearrange("(k p) s -> p k s", p=128)
            for (t0, n) in ffn_groups(S):
                hT, hk = hT_r.next()
                lo = max(t0 - 1, 0); hi = min(t0 + n + 1, S)
                P.load(hT[:, :, lo - (t0 - 1):hi - (t0 - 1)], h2v[:, :, lo:hi], hk)
                if t0 == 0:
                    P.memset(hT[:, :, 0:1], 0.0, [hk])
                if t0 + n == S:
                    P.memset(hT[:, :, n + 1:n + 2], 0.0, [hk])
                act, actk = act_r.next()
                for c in range(22):
                    def up(ch):
                        pu, puk = pu_r.next()
                        P.mm(pu[:, 0:n + 2], [(Wup[:, k, ch * 128:(ch + 1) * 128], hT[:, k, 0:n + 2]) for k in range(8)],
                             [hk, "Wup"], [puk])
                        return pu, puk

                    def conv(pu, puk, ch, ring):
                        t, tk = ring.next()
                        P.act(t[:, 0:n], pu[:, 1:n + 1], AF.Identity, [puk, "cw"], [tk], scale=cw[:, ch, 1:2], bias=cw[:, ch, 3:4])
                        P.stt(t[:, 0:n], pu[:, 0:n], cw[:, ch, 0:1], t[:, 0:n], ALU.mult, ALU.add, [puk, "cw", tk], [tk])
                        P.stt(t[:, 0:n], pu[:, 2:n + 2], cw[:, ch, 2:3], t[:, 0:n], ALU.mult, ALU.add, [puk, "cw", tk], [tk])
                        return t, tk
                    pa, pak = up(c)
                    pb, pbk = up(22 + c)
                    ta, tak = conv(pa, pak, c, ta_r)
                    tb, tbk = conv(pb, pbk, 22 + c, tb_r)
                    sa, sak = sa_r.next()
                    P.act(sa[:, 0:n], ta[:, 0:n], AF.Silu, [tak], [sak])
                    P.tt(act[:, c, 0:n], sa[:, 0:n], tb[:, 0:n], ALU.mult, [sak, tbk], [actk + "c%d" % c], eng="gpsimd")
                actks = [actk + "c%d" % c for c in range(22)]
                m0 = 0
                while m0 < n:
                    m = min(128, n - m0)
                    x1, x1k = x1_r.next()
                    P.load(x1[0:m, :], Y[si][t0 + m0:t0 + m0 + m, :], x1k)
                    yo, yok = yo_r.next()
                    for nn in range(2):
                        pd, pdk = pd_r.next()
                        P.mm(pd[:, :], [(act[:, k, m0:m0 + 128], Wd[:, k, nn * 512:(nn + 1) * 512]) for k in range(22)],
                             actks + ["Wd"], [pdk])
                        P.tt(yo[0:m, nn * 512:(nn + 1) * 512], pd[0:m, :], x1[0:m, nn * 512:(nn + 1) * 512], ALU.add,
                             [pdk, x1k], [yok + "n%d" % nn])
                    P.dma(Y[si][t0 + m0:t0 + m0 + m, :], yo[0:m, :], [yok + "n0", yok + "n1"], [], chan="T" + yok, q="gpsimd")
                    m0 += m
        P.finish()

    import os
    nph = int(os.environ.get("KPH", "6"))
    for ph in (phase1a, phase1b, phase2, phase3, phase4, phase5)[:nph]:
        ph()
    return nc


def host_consts(SMAX):
    i = np.arange(128, dtype=np.float32)
    cf = np.zeros((128, NCF), np.float32)
    diff = i[None, :] - i[:, None]
    cf[:, C_A:C_A + 128] = np.maximum(diff, 0)
    cf[:, C_B:C_B + 128] = np.maximum(-diff, 0)
    cf[:, C_MF:C_MF + 128] = (diff >= 0)
    cf[:, C_MB:C_MB + 128] = (diff < 0)
    cf[:, C_C1:C_C1 + 128] = (i + 1.0)[None, :]
    cf[:, C_C2:C_C2 + 128] = (128.0 - i)[None, :]
    cf[:, C_ID:C_ID + 128] = np.eye(128, dtype=np.float32)
    cf[:, C_ONE:C_ONE + 128] = 1.0
    cf[:, C_SM] = 127.0 - i
    cf[:, C_SM + 1] = i
    cf[:, C_SM + 2] = 128.0
    cb = np.zeros((128, 256), np.float32)
    cb[:, 0:128] = np.eye(128)
    cb[:, 128:256] = 1.0
    cb = cb.astype(ml_dtypes.bfloat16)
    pos = np.arange(SMAX, dtype=np.float32)

    def tab(d):
        inv = (np.float32(10000.0) ** (-np.arange(0, d, 2, dtype=np.float32) / np.float32(d))).astype(np.float32)
        ang = (pos[:, None] * inv[None, :]).astype(np.float32)
        c = np.cos(ang).astype(np.float32).T
        s = np.sin(ang).astype(np.float32).T
        return (np.ascontiguousarray(np.concatenate([c, c], 0)), np.ascontiguousarray(np.concatenate([s, s], 0)))
    cosr, sinr = tab(128)
    cosm, sinm = tab(64)
    return dict(cf=cf, cb=cb, cosr=cosr, sinr=sinr, cosm=cosm, sinm=sinm)


_CACHE = {}


def run(x_list, weights, n_cores=8):
    SEQS = tuple(int(x.shape[1]) for x in x_list)
    if SEQS not in _CACHE:
        _CACHE[SEQS] = build(SEQS)
    nc = _CACHE[SEQS]
    hc = host_consts(max(SEQS))
    w = {}
    for k, v in weights.items():
        a = np.asarray(v, dtype=np.float32)
        w[k] = np.ascontiguousarray(a.reshape(a.shape[1:]))
    in_maps = []
    for c in range(n_cores):
        m = dict(w)
        m.update(hc)
        for i, x in enumerate(x_list):
            m["x%d" % i] = np.ascontiguousarray(np.asarray(x[c], dtype=np.float32))
        in_maps.append(m)
    res = run_bass_kernel_spmd(nc, in_maps, core_ids=list(range(n_cores)))
    global LAST_RES
    LAST_RES = res
    outs = []
    for i in range(len(x_list)):
        outs.append(np.stack([np.asarray(res.results[c]["y%d" % i], dtype=np.float32) for c in range(n_cores)], 0))
    return tuple(outs)


def kernel(x_prompt, x_sample, **weights):
    return run([np.asarray(x_prompt), np.asarray(x_sample)], weights)
```
